# Optimizing a Trainium2 kernel written in Bass

```python
import math
import jax, jax.numpy as jnp
from jax import lax
import numpy as np

D_MODEL = 4096
BATCH = 8
SEQ = 2048
DEPTH = 1
DEC_BATCH = 4
DEC_SEQ = 4096
PAST_LEN = 128

HEAD_DIM = 128
N_ATTN_HEADS = D_MODEL // 2 // HEAD_DIM
N_KV_HEADS = N_ATTN_HEADS // 4
ATTN_WIDTH = N_ATTN_HEADS * HEAD_DIM
KV_WIDTH = N_KV_HEADS * HEAD_DIM
WINDOW = 128
BLOCK = 128
GLA_DV = 256
N_GLA_HEADS = D_MODEL // 2 // GLA_DV
GLA_DK = GLA_DV // 2
GLA_K_WIDTH = N_GLA_HEADS * GLA_DK
GLA_V_WIDTH = N_GLA_HEADS * GLA_DV
GATE_RANK = 16
GATE_TEMP = 16.0
CHUNK = 32
MIX_WIDTH = ATTN_WIDTH + GLA_V_WIDTH
IN_WIDTH = ATTN_WIDTH + 2 * KV_WIDTH + 2 * GLA_K_WIDTH + 2 * GLA_V_WIDTH + 2 * GATE_RANK
D_FF = 4 * D_MODEL
N_BUCKETS = 32
MAX_DISTANCE = 128
EPS = 1e-6
NEG_INF = -1e30

kernel_name = "hymba_style_window_gqa_gla_encoder"


def _rms(x, g):
    xf = x.astype(jnp.float32)
    y = xf * lax.rsqrt(jnp.mean(xf * xf, axis=-1, keepdims=True) + EPS)
    return (y * g.astype(jnp.float32)).astype(x.dtype)


def _t5_bucket(rel):
    half = N_BUCKETS // 2
    max_exact = half // 2
    ret = jnp.where(rel > 0, half, 0)
    n = jnp.abs(rel)
    nf = jnp.maximum(n, 1).astype(jnp.float32)
    large = max_exact + (jnp.log(nf / max_exact) / math.log(MAX_DISTANCE / max_exact)
                         * (half - max_exact)).astype(jnp.int32)
    large = jnp.minimum(large, half - 1)
    return ret + jnp.where(n < max_exact, n, large)


def _split_proj(proj):
    sizes = [ATTN_WIDTH, KV_WIDTH, KV_WIDTH, GLA_K_WIDTH, GLA_K_WIDTH,
             GLA_V_WIDTH, GLA_V_WIDTH, GATE_RANK, GATE_RANK]
    idx = np.cumsum(sizes)[:-1].tolist()
    return jnp.split(proj, idx, axis=-1)


def _window_attention(q, k, v, sink, rel_bias):
    B, S, _ = q.shape
    nb = S // BLOCK
    G = N_ATTN_HEADS // N_KV_HEADS
    qb = q.reshape(B, nb, BLOCK, N_KV_HEADS, G, HEAD_DIM)

    def windows(t):
        t = t.reshape(B, S, N_KV_HEADS, HEAD_DIM)
        tp = jnp.pad(t, ((0, 0), (BLOCK, BLOCK), (0, 0), (0, 0)))
        tp = tp.reshape(B, nb + 2, BLOCK, N_KV_HEADS, HEAD_DIM)
        return jnp.concatenate([tp[:, :-2], tp[:, 1:-1], tp[:, 2:]], axis=2)

    kw, vw = windows(k), windows(v)
    s = jnp.einsum('bnqkgd,bnjkd->bnkgqj', qb, kw).astype(jnp.float32) * (HEAD_DIM ** -0.5)
    qi = jnp.arange(BLOCK)[:, None]
    kj = jnp.arange(3 * BLOCK)[None, :]
    rel = kj - BLOCK - qi
    bias = rel_bias[_t5_bucket(rel)].astype(jnp.float32)
    bias = bias.transpose(2, 0, 1).reshape(N_KV_HEADS, G, BLOCK, 3 * BLOCK)
    kpos = (jnp.arange(nb)[:, None] - 1) * BLOCK + jnp.arange(3 * BLOCK)[None, :]
    valid = (jnp.abs(rel) <= WINDOW)[None] & ((kpos >= 0) & (kpos < S))[:, None, :]
    s = jnp.where(valid[None, :, None, None], s + bias, NEG_INF)
    sink_l = jnp.broadcast_to(sink.astype(jnp.float32).reshape(N_KV_HEADS, G, 1, 1), s.shape[:-1] + (1,))
    p = jax.nn.softmax(jnp.concatenate([s, sink_l], axis=-1), axis=-1)[..., :-1]
    o = jnp.einsum('bnkgqj,bnjkd->bnqkgd', p.astype(v.dtype), vw)
    return o.reshape(B, S, ATTN_WIDTH)


def _gla_direction(q, k, v, logg):
    B, H, S, dk = q.shape
    dv = v.shape[-1]
    nc = S // CHUNK
    qc = q.reshape(B, H, nc, CHUNK, dk)
    kc = k.reshape(B, H, nc, CHUNK, dk)
    vc = v.reshape(B, H, nc, CHUNK, dv)
    gc = logg.reshape(B, H, nc, CHUNK, dk)
    b = jnp.cumsum(gc, axis=3)
    b_last = b[:, :, :, -1:, :]
    q_t = qc * jnp.exp(b)
    k_t = kc * jnp.exp(-b)
    mask = jnp.tril(jnp.ones((CHUNK, CHUNK), dtype=bool))
    A = jnp.where(mask, jnp.einsum('bhnck,bhnsk->bhncs', q_t, k_t), 0.0)
    o_intra = jnp.einsum('bhncs,bhnsv->bhncv', A, vc)
    k_dec = kc * jnp.exp(b_last - b)
    decay = jnp.exp(b_last[:, :, :, 0, :])

    def step(state, inp):
        q_n, k_n, v_n, d_n = inp
        o_n = jnp.einsum('bhck,bhkv->bhcv', q_n, state)
        state = state * d_n[..., None] + jnp.einsum('bhck,bhcv->bhkv', k_n, v_n)
        return state, o_n

    xs = (jnp.moveaxis(q_t, 2, 0), jnp.moveaxis(k_dec, 2, 0),
          jnp.moveaxis(vc, 2, 0), jnp.moveaxis(decay, 2, 0))
    _, o_inter = lax.scan(step, jnp.zeros((B, H, dk, dv), jnp.float32), xs)
    return (o_intra + jnp.moveaxis(o_inter, 0, 2)).reshape(B, H, S, dv)


def _gla_mixer(qg, kg, vg, rg, af, ab, wa_f, ba_f, wa_b, ba_b, norm_g):
    B, S, _ = qg.shape

    def heads(t, d):
        return t.reshape(B, S, N_GLA_HEADS, d).transpose(0, 2, 1, 3).astype(jnp.float32)

    q = heads(qg, GLA_DK) * (GLA_DK ** -0.5)
    k = heads(kg, GLA_DK)
    v = heads(vg, GLA_DV)
    lg_f = heads(jax.nn.log_sigmoid((af @ wa_f + ba_f).astype(jnp.float32)) / GATE_TEMP, GLA_DK)
    lg_b = heads(jax.nn.log_sigmoid((ab @ wa_b + ba_b).astype(jnp.float32)) / GATE_TEMP, GLA_DK)
    o_f = _gla_direction(q, k, v, lg_f)
    flip = lambda t: jnp.flip(t, axis=2)
    o_b = flip(_gla_direction(flip(q), flip(k), flip(v), flip(lg_b)))
    o = o_f + o_b
    o = o * lax.rsqrt(jnp.mean(o * o, axis=-1, keepdims=True) + EPS) * norm_g.astype(jnp.float32)
    o = o.transpose(0, 2, 1, 3).reshape(B, S, GLA_V_WIDTH).astype(rg.dtype)
    return o * jax.nn.silu(rg)


def _layer(x, c, w_ada, b_ada, norm1_g, w_in, gla_wa_fwd, gla_ba_fwd, gla_wa_bwd, gla_ba_bwd,
           gla_norm_g, attn_sink, rel_bias, w_out, norm2_g, w_mlp_in, w_mlp_out):
    mod = jax.nn.silu(c) @ w_ada + b_ada
    sh1, sc1, g1, sh2, sc2, g2 = jnp.split(mod[:, None, :], 6, axis=-1)
    h = _rms(x, norm1_g) * (1.0 + sc1) + sh1
    qa, ka, va, qg, kg, vg, rg, af, ab = _split_proj(h @ w_in)
    attn = _window_attention(qa, ka, va, attn_sink, rel_bias)
    gla = _gla_mixer(qg, kg, vg, rg, af, ab, gla_wa_fwd, gla_ba_fwd, gla_wa_bwd, gla_ba_bwd, gla_norm_g)
    x = x + g1 * (jnp.concatenate([attn, gla], axis=-1) @ w_out)
    h = _rms(x, norm2_g) * (1.0 + sc2) + sh2
    x = x + g2 * (jnp.square(jax.nn.relu(h @ w_mlp_in)) @ w_mlp_out)
    return x


def _trunk(x, c, w_ada, b_ada, norm1_g, w_in, gla_wa_fwd, gla_ba_fwd, gla_wa_bwd, gla_ba_bwd,
           gla_norm_g, attn_sink, rel_bias, w_out, norm2_g, w_mlp_in, w_mlp_out, final_g):
    for l in range(DEPTH):
        x = _layer(x, c, w_ada[l], b_ada[l], norm1_g[l], w_in[l], gla_wa_fwd[l], gla_ba_fwd[l],
                   gla_wa_bwd[l], gla_ba_bwd[l], gla_norm_g[l], attn_sink[l], rel_bias,
                   w_out[l], norm2_g[l], w_mlp_in[l], w_mlp_out[l])
    return _rms(x, final_g)


def setup_inputs(seed: int = 0) -> dict:
    key = jax.random.key(seed)
    ks = jax.random.split(key, 24)
    f32 = jnp.float32
    nrm = lambda k, shape, s: jax.random.normal(k, shape, f32) * s
    return {
        "x_prompt": nrm(ks[0], (BATCH, SEQ, D_MODEL), 1.0),
        "x_sample": nrm(ks[1], (DEC_BATCH, DEC_SEQ, D_MODEL), 1.0),
        "c_prompt": nrm(ks[2], (BATCH, D_MODEL), 1.0),
        "c_sample": nrm(ks[3], (DEC_BATCH, D_MODEL), 1.0),
        "w_ada": nrm(ks[4], (DEPTH, D_MODEL, 6 * D_MODEL), D_MODEL ** -0.5),
        "b_ada": nrm(ks[5], (DEPTH, 6 * D_MODEL), 0.01),
        "norm1_g": 1.0 + nrm(ks[6], (DEPTH, D_MODEL), 0.02),
        "w_in": nrm(ks[7], (DEPTH, D_MODEL, IN_WIDTH), D_MODEL ** -0.5),
        "gla_wa_fwd": nrm(ks[8], (DEPTH, GATE_RANK, GLA_K_WIDTH), GATE_RANK ** -0.5),
        "gla_ba_fwd": nrm(ks[9], (DEPTH, GLA_K_WIDTH), 0.1),
        "gla_wa_bwd": nrm(ks[10], (DEPTH, GATE_RANK, GLA_K_WIDTH), GATE_RANK ** -0.5),
        "gla_ba_bwd": nrm(ks[11], (DEPTH, GLA_K_WIDTH), 0.1),
        "gla_norm_g": 1.0 + nrm(ks[12], (DEPTH, GLA_DV), 0.02),
        "attn_sink": nrm(ks[13], (DEPTH, N_ATTN_HEADS), 1.0),
        "rel_bias": nrm(ks[14], (N_BUCKETS, N_ATTN_HEADS), 0.5),
        "w_out": nrm(ks[15], (DEPTH, MIX_WIDTH, D_MODEL), MIX_WIDTH ** -0.5),
        "norm2_g": 1.0 + nrm(ks[16], (DEPTH, D_MODEL), 0.02),
        "w_mlp_in": nrm(ks[17], (DEPTH, D_MODEL, D_FF), D_MODEL ** -0.5),
        "w_mlp_out": nrm(ks[18], (DEPTH, D_FF, D_MODEL), D_FF ** -0.5),
        "final_g": 1.0 + nrm(ks[19], (D_MODEL,), 0.02),
    }


def reference(x_prompt, x_sample, c_prompt, c_sample, w_ada, b_ada, norm1_g, w_in,
              gla_wa_fwd, gla_ba_fwd, gla_wa_bwd, gla_ba_bwd, gla_norm_g, attn_sink, rel_bias,
              w_out, norm2_g, w_mlp_in, w_mlp_out, final_g):
    y_prompt = _trunk(x_prompt, c_prompt, w_ada, b_ada, norm1_g, w_in, gla_wa_fwd, gla_ba_fwd,
                      gla_wa_bwd, gla_ba_bwd, gla_norm_g, attn_sink, rel_bias, w_out, norm2_g,
                      w_mlp_in, w_mlp_out, final_g)
    y_sample = _trunk(x_sample, c_sample, w_ada, b_ada, norm1_g, w_in, gla_wa_fwd, gla_ba_fwd,
                      gla_wa_bwd, gla_ba_bwd, gla_norm_g, attn_sink, rel_bias, w_out, norm2_g,
                      w_mlp_in, w_mlp_out, final_g)
    return (y_prompt, y_sample)
```

```python
import contextlib
import math

import numpy as np
import concourse.bass as bass
import concourse.mybir as mybir
from concourse.bass_utils import run_bass_kernel_spmd

F32 = mybir.dt.float32
BF16 = mybir.dt.bfloat16
AF = mybir.ActivationFunctionType
ALU = mybir.AluOpType
AX = mybir.AxisListType

ENGS = ("pe", "act", "dve", "pool", "sp")
EPS = 1e-6
NEG = -30000.0


class Buf:
    __slots__ = ("name", "last_w", "readers", "multi", "writers")

    def __init__(self, name, multi=False, carry=()):
        self.name = name
        self.last_w = None
        self.readers = list(carry)
        self.multi = multi
        self.writers = []


class Op:
    __slots__ = ("eng", "fn", "deps", "is_dma", "dsem", "dval", "needs_inc", "ms", "idx", "tiny")

    def __init__(self, eng, fn, is_dma=False):
        self.eng = eng
        self.fn = fn
        self.deps = []
        self.is_dma = is_dma
        self.dsem = None
        self.dval = 0
        self.needs_inc = False
        self.ms = None
        self.idx = None
        self.tiny = False


class Prog:
    def __init__(self, nc):
        self.nc = nc
        self.ops = []
        self.dma_sems = {}
        self.carry = []
        self.scope_bufs = []

    def buf(self, name, multi=False):
        b = Buf(name, multi, self.carry)
        self.scope_bufs.append(b)
        return b

    def release_scope(self):
        pend = list(self.carry)
        for b in self.scope_bufs:
            if b.last_w is not None:
                pend.append(b.last_w)
            pend.extend(b.readers)
            pend.extend(b.writers)
        last = {}
        dmas = {}
        for o in pend:
            if o.is_dma:
                k = o.dsem
                if k not in dmas or dmas[k].dval < o.dval:
                    dmas[k] = o
            else:
                if o.eng not in last or last[o.eng].idx < o.idx:
                    last[o.eng] = o
        self.carry = list(last.values()) + list(dmas.values())
        self.scope_bufs = []

    def _add(self, op, reads, writes):
        deps = []
        for b in reads:
            if b.multi:
                deps.extend(b.writers)
            elif b.last_w is not None:
                deps.append(b.last_w)
        for b in writes:
            if b.multi:
                deps.extend(b.readers)
                b.writers.append(op)
            else:
                if b.last_w is not None:
                    deps.append(b.last_w)
                deps.extend(b.readers)
                b.last_w = op
                b.readers = []
        for b in reads:
            if not b.multi or True:
                b.readers.append(op)
        op.deps = deps
        op.idx = len(self.ops)
        self.ops.append(op)
        return op

    def op(self, eng, fn, reads=(), writes=(), tiny=False):
        o = Op(eng, fn)
        o.tiny = tiny
        return self._add(o, reads, writes)

    def dma(self, queue, fn, semkey, reads=(), writes=()):
        op = Op(queue, fn, is_dma=True)
        ent = self.dma_sems.setdefault(semkey, [0])
        ent[0] += 16
        op.dsem = semkey
        op.dval = ent[0]
        return self._add(op, reads, writes)

    def emit(self):
        nc = self.nc
        for op in self.ops:
            for d in op.deps:
                if (not d.is_dma) and (d.eng != op.eng or d.tiny or op.is_dma):
                    d.needs_inc = True
        cnt = {e: 0 for e in ENGS}
        for op in self.ops:
            if (not op.is_dma) and op.needs_inc:
                cnt[op.eng] += 1
                op.ms = cnt[op.eng]
        with contextlib.ExitStack() as st:
            esem = {e: st.enter_context(nc.semaphore("ms_" + e)) for e in ENGS}
            dsem = {}
            for i, k in enumerate(self.dma_sems):
                dsem[k] = st.enter_context(nc.semaphore("d%d" % i))
            block = st.enter_context(nc.Block())
            per_eng = {e: [] for e in ENGS}
            for op in self.ops:
                per_eng[op.eng].append(op)
            final = [(k, v[0]) for k, v in self.dma_sems.items()]

            def run(eng_name, engine):
                waited_e = {e: 0 for e in ENGS}
                waited_d = {}
                for op in per_eng[eng_name]:
                    need_e = {}
                    need_d = {}
                    for d in op.deps:
                        if d.is_dma:
                            if need_d.get(d.dsem, 0) < d.dval:
                                need_d[d.dsem] = d.dval
                        elif d.eng != eng_name or d.tiny or op.is_dma:
                            if need_e.get(d.eng, 0) < d.ms:
                                need_e[d.eng] = d.ms
                    for e, v in need_e.items():
                        if waited_e[e] < v:
                            engine.wait_ge(esem[e], v)
                            waited_e[e] = v
                    for k, v in need_d.items():
                        if waited_d.get(k, 0) < v:
                            engine.wait_ge(dsem[k], v)
                            waited_d[k] = v
                    ins = op.fn(engine)
                    if op.is_dma:
                        ins.then_inc(dsem[op.dsem], 16)
                    elif op.needs_inc:
                        ins.then_inc(esem[eng_name], 1)
                if eng_name == "sp":
                    for k, v in final:
                        if waited_d.get(k, 0) < v:
                            engine.wait_ge(dsem[k], v)

            block.tensor(lambda e: run("pe", e))
            block.scalar(lambda e: run("act", e))
            block.vector(lambda e: run("dve", e))
            block.gpsimd(lambda e: run("pool", e))
            block.sync(lambda e: run("sp", e))
        return cnt


class Cfg:
    def __init__(self, D=4096, L=2048, TA=512, TC=512, debug=False, stop_after=None, ada_fp32r=False,
                 cast_frac_A=0.6, cast_pool=False, cast_q_A="sp"):
        self.cast_q_A = cast_q_A
        self.stop_after = stop_after
        self.ada_fp32r = ada_fp32r
        self.cast_frac_A = cast_frac_A
        self.cast_pool = cast_pool
        self.D = D
        self.L = L
        self.HD = 128
        self.NHA = D // 2 // 128
        self.NKV = max(self.NHA // 4, 1)
        self.GRP = self.NHA // self.NKV
        self.DV = 256
        self.DK = 128
        self.NHG = D // 2 // 256
        self.RANK = 16
        self.DFF = 4 * D
        self.KC = D // 128
        self.AW = self.NHA * 128
        self.KVW = self.NKV * 128
        self.GKW = self.NHG * 128
        self.GVW = self.NHG * 256
        self.INW = self.AW + 2 * self.KVW + 2 * self.GKW + 2 * self.GVW + 32
        self.NB = L // 128
        self.TOK = 2 * L
        self.NBT = 2 * self.NB
        self.TA = TA
        self.TC = TC
        self.FC = self.DFF // 128
        self.debug = debug
        o = 0
        self.o_q = o; o += self.AW
        self.o_k = o; o += self.KVW
        self.o_v = o; o += self.KVW
        self.o_qg = o; o += self.GKW
        self.o_kg = o; o += self.GKW
        self.o_vg = o; o += self.GVW
        self.o_rg = o; o += self.GVW
        self.o_af = o; o += 16
        self.o_ab = o; o += 16
        assert o == self.INW


def t5_bucket_np(rel):
    half = 16
    max_exact = 8
    ret = np.where(rel > 0, half, 0)
    n = np.abs(rel)
    nf = np.maximum(n, 1).astype(np.float32)
    large = max_exact + (np.log(nf / max_exact) / math.log(128 / max_exact) * (half - max_exact)).astype(np.int32)
    large = np.minimum(large, half - 1)
    return ret + np.where(n < max_exact, n, large)


def host_consts():
    i = np.arange(128)
    s = i[:, None]
    c = i[None, :]
    g = -1.0 / 16.0
    cA = np.zeros((128, 8, 128), np.float32)
    cA[:, 0, :] = np.eye(128)
    cA[:, 1, :] = np.eye(128)[::-1]
    cA[:, 2, :] = np.where(s <= c, g, 0.0)
    cA[:, 3, :] = np.where(s > c, g, 0.0)
    cA[:, 4, :] = np.where(s >= c, g, 0.0)
    cA[:, 5, :] = np.where(s < c, g, 0.0)
    cA[:, 6, :] = np.where(s <= c, 1.0, 0.0)
    cA[:, 7, :] = np.where(s >= c, 1.0, 0.0)
    r = np.arange(512)
    rel = r - 255
    bk = t5_bucket_np(rel)
    cOH = np.zeros((33, 512), np.float32)
    cOH[bk, r] = 1.0
    cOH[:32, 511] = 0.0
    cOH[32, :] = np.where(np.abs(rel) <= 128, 0.0, NEG)
    cOH[32, 511] = NEG
    return cA, cOH


class Arena:
    def __init__(self, big, nwords):
        self.big = big
        self.n = nwords
        self.top = 0

    def alloc(self, shape, dt):
        nel = 1
        for s in shape[1:]:
            nel *= s
        nbytes = nel * (2 if dt == BF16 else 4)
        words = (nbytes + 3) // 4
        a = self.top
        self.top += (words + 7) // 8 * 8
        assert self.top <= self.n, "SBUF arena overflow: %d > %d words" % (self.top, self.n)
        v = self.big[0:shape[0], a:a + words]
        if dt == BF16:
            v = v.bitcast(BF16)
        if len(shape) == 3:
            v = v.rearrange("p (a b) -> p a b", a=shape[1])
        elif len(shape) == 4:
            v = v.rearrange("p (a b c) -> p a b c", a=shape[1], b=shape[2])
        return v


def build_program(cfg):
    nc = bass.Bass("TRN2", target_bir_lowering=False)
    D, L, KC, TOK, NB, NBT = cfg.D, cfg.L, cfg.KC, cfg.TOK, cfg.NB, cfg.NBT
    NHA, NKV, GRP, NHG = cfg.NHA, cfg.NKV, cfg.GRP, cfg.NHG
    AW, KVW, GKW, GVW, INW, DFF, FC = cfg.AW, cfg.KVW, cfg.GKW, cfg.GVW, cfg.INW, cfg.DFF, cfg.FC

    def din(name, shape, dt=F32):
        return nc.dram_tensor(name, list(shape), dt, kind="ExternalInput").ap()

    def dscr(name, shape, dt):
        dbg = cfg.debug is True or (cfg.debug and name in cfg.debug)
        return nc.dram_tensor(name, list(shape), dt, kind="ExternalOutput" if dbg else "Internal").ap()

    x_d = din("x", [TOK, D])
    c2_d = din("c2", [2, D])
    flags_d = din("flags", [128, 2])
    wada_d = din("w_ada", [D, 6 * D])
    bada_d = din("b_ada", [1, 6 * D])
    n1g_d = din("n1g", [1, D])
    win_d = din("w_in", [D, INW])
    wabf_d = din("wab_f", [17, GKW])
    wabb_d = din("wab_b", [17, GKW])
    gng_d = din("gng", [1, 256])
    sink_d = din("sink", [1, NHA])
    relb_d = din("relb", [33, NHA])
    wout_d = din("w_out", [D, D])
    n2g_d = din("n2g", [1, D])
    w1_d = din("w1", [D, DFF])
    w2_d = din("w2", [DFF, D])
    fg_d = din("fg", [1, D])
    cA_d = din("cA", [128, 8, 128])
    cOH_d = din("cOH", [33, 512])
    y_d = nc.dram_tensor("y", [TOK, D], F32, kind="ExternalOutput").ap()

    modraw_d = dscr("modraw", [2, 6 * D], F32)
    gfull_d = dscr("gfull", [2, 2, D], F32)
    winb_d = dscr("winb", [D, INW], BF16)
    woutb_d = dscr("woutb", [D, D], BF16)
    w1t_d = dscr("w1t", [FC, 128, KC, 128], BF16)
    w2b_d = dscr("w2b", [DFF, D], BF16)
    qT_d = dscr("qT", [AW, TOK], BF16)
    kT_d = dscr("kT", [KVW, TOK], BF16)
    v_d = dscr("v", [TOK, KVW], BF16)
    qgT_d = dscr("qgT", [GKW, TOK], BF16)
    kgT_d = dscr("kgT", [GKW, TOK], BF16)
    kg_d = dscr("kg", [TOK, GKW], BF16)
    vg_d = dscr("vg", [TOK, GVW], BF16)
    rg_d = dscr("rg", [TOK, GVW], BF16)
    afT_d = dscr("afT", [16, TOK], F32)
    abT_d = dscr("abT", [16, TOK], F32)
    tvec_d = dscr("tvec", [NHA, 512], F32)
    mixT_d = dscr("mixT", [D, TOK], BF16)

    P = Prog(nc)
    es = contextlib.ExitStack()
    NWORDS = 53000
    big = es.enter_context(nc.sbuf_tensor("big", [128, NWORDS], F32))
    pbig = es.enter_context(nc.psum_tensor("pbig", [128, 4096], F32))
    ar = Arena(big, NWORDS)
    pstate = [0]

    def sb(shape, dt):
        return ar.alloc(list(shape), dt)

    def ps(shape, dt=F32):
        b = pstate[0]
        pstate[0] += 1
        assert b < 8, "PSUM banks exhausted"
        v = pbig[0:shape[0], b * 512:(b + 1) * 512]
        if dt == BF16:
            v = v.bitcast(BF16)
        return v[:, 0:shape[1]]

    def end_phase(mark):
        ar.top = mark
        pstate[0] = 0
        P.release_scope()

    B = {n: Buf(n, multi=True) for n in
         ["modraw", "gfull", "winb", "woutb", "w1t", "w2b", "qT", "kT", "v", "qgT", "kgT", "kg", "vg", "rg",
          "afT", "abT", "tvec", "mixT", "y"]}

    dbg_n = [0]

    def dbg_dump(name, ap, shape, dt, reads):
        if not (cfg.debug and (cfg.debug is True or name in cfg.debug)):
            return
        dten = nc.dram_tensor("dbg_" + name, list(shape), dt, kind="ExternalOutput").ap()
        dbg_n[0] += 1
        P.dma("sp", lambda e: e.dma_start(out=dten, in_=ap), "dbg%d" % dbg_n[0], reads=reads, writes=[])

    cA = sb([128, 8, 128], F32)
    identF = cA[:, 0, :]
    Jm = cA[:, 1, :]
    identB = sb([128, 128], BF16)
    flags = sb([128, 2], F32)
    modT = sb([128, 4, 2, KC], F32)
    bT = sb([128, 4, KC], F32)
    ngT = sb([128, 2, KC], F32)
    scl = sb([128, 2, 2, KC], F32)
    small = sb([128, 64], F32)
    negh = small[:, 0:1]
    b_cA = P.buf("cA"); b_identB = P.buf("identB"); b_flags = P.buf("flags")
    b_modT = P.buf("modT"); b_bT = P.buf("bT"); b_ngT = P.buf("ngT"); b_scl = P.buf("scl")
    b_negh = P.buf("negh")
    P.scope_bufs = []
    MARK0 = ar.top

    P.dma("sp", lambda e: e.dma_start(out=cA, in_=cA_d), "cA", writes=[b_cA])
    P.dma("sp", lambda e: e.dma_start(out=flags, in_=flags_d), "flags", writes=[b_flags])
    P.op("dve", lambda e: e.tensor_copy(out=identB, in_=identF), reads=[b_cA], writes=[b_identB])
    P.op("dve", lambda e: e.memset(negh, -0.5), writes=[b_negh])

    def rstd_ops(ss_ap, b_ss, out_ap, b_out, n, tmp_ap, b_tmp):
        P.op("dve", lambda e: e.tensor_scalar(out=tmp_ap, in0=ss_ap, scalar1=1.0 / n, scalar2=EPS,
                                              op0=ALU.mult, op1=ALU.add), reads=[b_ss], writes=[b_tmp], tiny=True)
        P.op("pool", lambda e: e.tensor_tensor(out=out_ap, in0=tmp_ap, in1=negh, op=ALU.pow),
             reads=[b_tmp, b_negh], writes=[b_out], tiny=True)

    if True:
        NW = 256
        KH = max(KC // 2, 1)
        cT = sb([128, KC, 2], F32)
        cTb = sb([128, KC, 2], BF16)
        wslot = [sb([128, KC, NW], F32) for i in range(3)]
        wslotb = [sb([128, KC, NW], BF16) for i in range(3)]
        mst = [sb([2, NW], F32) for i in range(2)]
        gr = sb([2, D], F32)
        gb = sb([2, D], F32)
        pm = [ps([2, NW]) for i in range(2)]
        b_cT = P.buf("cT"); b_cTb = P.buf("cTb")
        b_ws = [P.buf("wa%d" % i) for i in range(3)]
        b_wsb = [[P.buf("wab%d_%d" % (i, hh)) for hh in range(2)] for i in range(3)]
        b_mst = [P.buf("mst%d" % i) for i in range(2)]
        b_pm = [P.buf("pm%d" % i) for i in range(2)]
        b_gr = P.buf("gr"); b_gb = P.buf("gb")
        for h in range(2):
            P.dma("sp", lambda e, h=h: e.dma_start(out=cT[:, :, h], in_=c2_d[h:h + 1, :].rearrange("o (k p) -> p (o k)", p=128),
                                                   allow_slow_non_contiguous=True), "cT", writes=[b_cT])
        P.op("act", lambda e: e.activation(out=cTb, in_=cT, func=AF.Silu), reads=[b_cT], writes=[b_cTb])
        wv = wada_d.rearrange("(k p) n -> p k n", p=128)
        NT = 6 * D // NW
        for n in range(NT):
            s = n % 3
            m2 = n % 2
            for hh in range(KC // KH):
                P.dma("sp", lambda e, s=s, n=n, hh=hh: e.dma_start(
                    out=wslot[s][:, hh * KH:(hh + 1) * KH, :], in_=wv[:, hh * KH:(hh + 1) * KH, n * NW:(n + 1) * NW]),
                    "wa%d" % s, writes=[b_ws[s]])
            for hh in range(KC // KH):
                if hh % 2 == 0:
                    P.op("act", lambda e, s=s, hh=hh: e.activation(out=wslotb[s][:, hh * KH:(hh + 1) * KH, :],
                                                                   in_=wslot[s][:, hh * KH:(hh + 1) * KH, :], func=AF.Copy),
                         reads=[b_ws[s]], writes=[b_wsb[s][hh % 2]])
                else:
                    P.op("dve", lambda e, s=s, hh=hh: e.tensor_copy(out=wslotb[s][:, hh * KH:(hh + 1) * KH, :],
                                                                    in_=wslot[s][:, hh * KH:(hh + 1) * KH, :]),
                         reads=[b_ws[s]], writes=[b_wsb[s][hh % 2]])
            for k in range(KC):
                P.op("pe", lambda e, s=s, k=k, m2=m2: e.matmul(pm[m2], cTb[:, k, :], wslotb[s][:, k, :],
                                                        start=(k == 0), stop=(k == KC - 1)),
                     reads=[b_cTb, b_wsb[s][(k // KH) % 2]], writes=[b_pm[m2]])
            P.op("act", lambda e, m2=m2: e.activation(out=mst[m2], in_=pm[m2], func=AF.Copy),
                 reads=[b_pm[m2]], writes=[b_mst[m2]])
            P.dma("pool", lambda e, m2=m2, n=n: e.dma_start(out=modraw_d[:, n * NW:(n + 1) * NW], in_=mst[m2]),
                  "mst%d" % m2, reads=[b_mst[m2]], writes=[B["modraw"]])
        for i, sec in enumerate((0, 1, 3, 4)):
            for h in range(2):
                P.dma("sp", lambda e, i=i, sec=sec, h=h: e.dma_start(
                    out=modT[:, i, h, :], in_=modraw_d[h:h + 1, sec * D:(sec + 1) * D].rearrange("o (k p) -> p (o k)", p=128),
                    allow_slow_non_contiguous=True), "modT", reads=[B["modraw"]], writes=[b_modT])
            P.dma("sp", lambda e, i=i, sec=sec: e.dma_start(
                out=bT[:, i, :], in_=bada_d[:, sec * D:(sec + 1) * D].rearrange("o (k p) -> p (o k)", p=128),
                allow_slow_non_contiguous=True), "bT", writes=[b_bT])
        P.dma("sp", lambda e: e.dma_start(out=ngT[:, 0, :], in_=n1g_d.rearrange("o (k p) -> p (o k)", p=128),
                                          allow_slow_non_contiguous=True), "ngT", writes=[b_ngT])
        P.dma("sp", lambda e: e.dma_start(out=ngT[:, 1, :], in_=n2g_d.rearrange("o (k p) -> p (o k)", p=128),
                                          allow_slow_non_contiguous=True), "ngT", writes=[b_ngT])
        for i in range(4):
            for h in range(2):
                P.op("dve", lambda e, i=i, h=h: e.tensor_tensor(out=modT[:, i, h, :], in0=modT[:, i, h, :],
                                                                in1=bT[:, i, :], op=ALU.add),
                     reads=[b_modT, b_bT], writes=[b_modT], tiny=True)
        for nn, si in ((0, 1), (1, 3)):
            for h in range(2):
                P.op("dve", lambda e, nn=nn, si=si, h=h: e.scalar_tensor_tensor(
                    out=scl[:, nn, h, :], in0=modT[:, si, h, :], scalar=1.0, in1=ngT[:, nn, :],
                    op0=ALU.add, op1=ALU.mult), reads=[b_modT, b_ngT], writes=[b_scl], tiny=True)
        for gi, sec in enumerate((2, 5)):
            P.dma("sp", lambda e, sec=sec: e.dma_start(out=gr, in_=modraw_d[:, sec * D:(sec + 1) * D]),
                  "gr", reads=[B["modraw"]], writes=[b_gr])
            P.dma("sp", lambda e, sec=sec: e.dma_start(out=gb, in_=bada_d[:, sec * D:(sec + 1) * D].partition_broadcast(2)),
                  "gb", writes=[b_gb])
            P.op("dve", lambda e: e.tensor_tensor(out=gr, in0=gr, in1=gb, op=ALU.add),
                 reads=[b_gr, b_gb], writes=[b_gr])
            P.dma("sp", lambda e, gi=gi: e.dma_start(out=gfull_d[gi], in_=gr), "gr", reads=[b_gr],
                  writes=[B["gfull"]])
        end_phase(MARK0)

    sh = {0: 0, 1: 2}

    CW = 1024 if D >= 1024 else D
    NSL = 4
    w32 = [sb([128, CW], F32) for i in range(NSL)]
    w16 = [sb([128, CW], BF16) for i in range(NSL)]
    b_w32 = [P.buf("w32_%d" % i) for i in range(NSL)]
    b_w16 = [P.buf("w16_%d" % i) for i in range(NSL)]
    cast_bufs = b_w32 + b_w16
    P.scope_bufs = []
    MARK1 = ar.top
    cnt = [0]
    cast_engs = ("act", "dve", "pool") if cfg.cast_pool else ("act", "dve")

    cast_mode = {"q": "sp", "engs": ("act", "dve")}
    cast_pending = [None]

    def cast_flush():
        if cast_pending[0] is not None:
            cast_pending[0]()
            cast_pending[0] = None

    def cast_piece(src_ap, in_view_fn, out_view_fn, dst_fn, dst_buf, load_view_fn=None):
        i = cnt[0] % NSL
        engs = cast_mode["engs"]
        eng = engs[cnt[0] % len(engs)]
        cnt[0] += 1
        lv = w32[i] if load_view_fn is None else load_view_fn(w32[i])
        P.dma(cast_mode["q"], lambda e: e.dma_start(out=lv, in_=src_ap), "w32_%d" % i, writes=[b_w32[i]])
        ov = out_view_fn(w16[i])
        iv = in_view_fn(w32[i])

        def second():
            if eng == "act":
                P.op("act", lambda e: e.activation(out=ov, in_=iv, func=AF.Copy), reads=[b_w32[i]], writes=[b_w16[i]])
            else:
                P.op(eng, lambda e: e.tensor_copy(out=ov, in_=iv), reads=[b_w32[i]], writes=[b_w16[i]])
            P.dma("pool", lambda e: dst_fn(e, w16[i]), "w16_%d" % i, reads=[b_w16[i]], writes=[dst_buf])
        prev = cast_pending[0]
        cast_pending[0] = second
        if prev is not None:
            prev()

    def cast_natural(src_d, dst_d, rows, cols, bname):
        for r0 in range(0, rows, 128):
            for c0 in range(0, cols, CW):
                cw = min(CW, cols - c0)
                cast_piece(src_d[r0:r0 + 128, c0:c0 + cw], lambda t, cw=cw: t[:, 0:cw], lambda t, cw=cw: t[:, 0:cw],
                           lambda e, t, r0=r0, c0=c0, cw=cw: e.dma_start(out=dst_d[r0:r0 + 128, c0:c0 + cw],
                                                                          in_=t[:, 0:cw]), B[bname],
                           load_view_fn=lambda t, cw=cw: t[:, 0:cw])
                yield

    def cast_w1():
        KK = min(4, KC)
        CC = CW // KK
        NF = CC // 128
        for k0 in range(0, KC, KK):
            for c0 in range(0, DFF, CC):
                f0 = c0 // 128
                cast_piece(w1_d[k0 * 128:(k0 + KK) * 128, c0:c0 + CC].rearrange("(k p) c -> p k c", p=128),
                           lambda t: t[:, 0:KK * CC].rearrange("p (k f j) -> p f k j", k=KK, f=NF),
                           lambda t: t[:, 0:KK * CC].rearrange("p (f k j) -> p f k j", f=NF, k=KK),
                           lambda e, t, k0=k0, f0=f0: e.dma_start(
                               out=w1t_d[f0:f0 + NF, :, k0:k0 + KK, :].rearrange("f p k j -> p f k j"),
                               in_=t[:, 0:KK * CC].rearrange("p (f k j) -> p f k j", f=NF, k=KK)), B["w1t"],
                           load_view_fn=lambda t: t[:, 0:KK * CC].rearrange("p (k c) -> p k c", k=KK))
                yield

    for _ in cast_natural(win_d, winb_d, D, INW, "winb"):
        pass
    cast_flush()

    def cast_rest():
        yield from cast_natural(wout_d, woutb_d, D, D, "woutb")
        yield from cast_w1()
        yield from cast_natural(w2_d, w2b_d, DFF, D, "w2b")

    cast_gen = cast_rest()
    n_cast_total = (D // 128) * ((D + CW - 1) // CW) + (KC // min(4, KC)) * (DFF // (CW // min(4, KC))) + (DFF // 128) * ((D + CW - 1) // CW)

    def cast_some(n):
        for _ in range(n):
            try:
                next(cast_gen)
            except StopIteration:
                cast_flush()
                return

    def norm_transpose_block(xt, b_xt, half, nn, hT, b_hT, tcol, pst, b_pst, junk, junk_bufs, ssv, b_ss, rsv, b_rs,
                             tmpv, b_tmp, undo, act_ok=True):
        P.op("act", lambda e: e.activation(out=junk, in_=xt, func=AF.Square, accum_out=ssv),
             reads=[b_xt], writes=list(junk_bufs) + [b_ss])
        rstd_ops(ssv, b_ss, rsv, b_rs, D, tmpv, b_tmp)
        P.op("dve", lambda e: e.tensor_scalar(out=xt, in0=xt, scalar1=rsv, scalar2=None, op0=ALU.mult),
             reads=[b_xt, b_rs], writes=[b_xt])
        G4 = 4
        for k0 in range(0, KC, G4):
            pi = (k0 // G4) % len(pst)
            for j in range(G4):
                k = k0 + j
                P.op("pe", lambda e, k=k, j=j, pi=pi: e.transpose(pst[pi][:, j * 128:(j + 1) * 128],
                                                                   xt[:, k * 128:(k + 1) * 128], identF),
                     reads=[b_xt, b_cA], writes=[b_pst[pi]])
            for j in range(G4):
                k = k0 + j
                if j % 2 == 0 and act_ok:
                    P.op("act", lambda e, k=k, j=j, pi=pi: e.activation(
                        out=hT[:, k, tcol:tcol + 128], in_=pst[pi][:, j * 128:(j + 1) * 128], func=AF.Identity,
                        bias=modT[:, sh[nn], half, k:k + 1], scale=scl[:, nn, half, k:k + 1]),
                        reads=[b_pst[pi], b_modT, b_scl], writes=[b_hT])
                else:
                    P.op("dve", lambda e, k=k, j=j, pi=pi: e.tensor_scalar(
                        out=hT[:, k, tcol:tcol + 128], in0=pst[pi][:, j * 128:(j + 1) * 128],
                        scalar1=scl[:, nn, half, k:k + 1], scalar2=modT[:, sh[nn], half, k:k + 1],
                        op0=ALU.mult, op1=ALU.add), reads=[b_pst[pi], b_modT, b_scl], writes=[b_hT])
        if undo:
            P.op("dve", lambda e: e.reciprocal(out=tmpv, in_=rsv), reads=[b_rs], writes=[b_tmp], tiny=True)
            P.op("pool", lambda e: e.tensor_scalar(out=xt, in0=xt, scalar1=tmpv, scalar2=1.0, op0=ALU.mult,
                                                   op1=ALU.mult),
                 reads=[b_xt, b_tmp], writes=[b_xt])

    TA = cfg.TA
    NBA = TA // 128
    if True:
        xin = [sb([128, D], F32) for i in range(2)]
        junk = sb([128, D], BF16)
        h1T = sb([128, KC, TA], BF16)
        NWS = 3
        wsl = [sb([128, KC, 512], BF16) for i in range(NWS)]
        NST = 4
        stg = [sb([128, 512], BF16) for i in range(NST)]
        stg32 = sb([32, 512], F32)
        pst = [ps([128, 512]) for i in range(2)]
        pacc = [ps([128, 512]) for i in range(4)]
        b_xin = [P.buf("xin%d" % i) for i in range(2)]
        b_junk = P.buf("junkA")
        b_h1T = P.buf("h1T")
        b_wsl = [P.buf("winS%d" % i) for i in range(NWS)]
        b_stg = [P.buf("stgA%d" % i) for i in range(NST)]
        b_stg32 = P.buf("stgA32")
        b_pst = [P.buf("pstA%d" % i) for i in range(2)]
        b_pacc = [P.buf("paccA%d" % i) for i in range(4)]
        b_ss = P.buf("ssA"); b_rs = P.buf("rsA"); b_tmp = P.buf("tmpA")
        ssv, rsv, tmpv = small[:, 1:2], small[:, 2:3], small[:, 3:4]

        groups = [("q", cfg.o_q, AW, qT_d, None), ("k", cfg.o_k, KVW, kT_d, None), ("v", cfg.o_v, KVW, None, v_d),
                  ("qg", cfg.o_qg, GKW, qgT_d, None), ("kg", cfg.o_kg, GKW, kgT_d, kg_d),
                  ("vg", cfg.o_vg, GVW, None, vg_d), ("rg", cfg.o_rg, GVW, None, rg_d),
                  ("ab", cfg.o_af, 32, "afab", None)]
        slabs = []
        for name, c0, ncols, fm, tm in groups:
            for s0 in range(0, ncols, 512):
                slabs.append((name, c0 + s0, min(512, ncols - s0), s0, fm, tm))
        winv = winb_d.rearrange("(k p) n -> p k n", p=128)
        cnt_w = [0]; cnt_s = [0]; cnt_p = [0]
        n_groups_A = (TOK // TA) * sum((((nc_ + 127) // 128) if fm_ is not None else 0) + (NBA if tm_ is not None else 0)
                                       for (_, _, nc_, _, fm_, tm_) in slabs)
        cast_rate_A = cfg.cast_frac_A * n_cast_total / n_groups_A
        cast_acc = [0.0]
        cast_mode["q"] = cfg.cast_q_A
        cast_mode["engs"] = ("act",)

        def cast_tick():
            cast_acc[0] += cast_rate_A
            while cast_acc[0] >= 1.0:
                cast_acc[0] -= 1.0
                cast_some(1)

        for t in range(TOK // TA):
            tok0 = t * TA
            half = tok0 // L
            for b in range(NBA):
                xi = (t * NBA + b) % 2
                P.dma("sp", lambda e, xi=xi, r0=tok0 + b * 128: e.dma_start(out=xin[xi], in_=x_d[r0:r0 + 128, :]),
                      "xin%d" % xi, writes=[b_xin[xi]])
                norm_transpose_block(xin[xi], b_xin[xi], half, 0, h1T, b_h1T, b * 128, pst, b_pst, junk, [b_junk],
                                     ssv, b_ss, rsv, b_rs, tmpv, b_tmp, undo=False, act_ok=False)
            for (name, col0, ncols, s0, fm, tm) in slabs:
                wi = cnt_w[0] % NWS
                cnt_w[0] += 1
                P.dma("sp", lambda e, wi=wi, col0=col0, ncols=ncols: e.dma_start(
                    out=wsl[wi][:, :, 0:ncols], in_=winv[:, :, col0:col0 + ncols]), "winS%d" % wi,
                    reads=[B["winb"]], writes=[b_wsl[wi]])
                if fm is not None:
                    for c in range(0, ncols, 128):
                        m = min(128, ncols - c)
                        pi = cnt_p[0] % 4
                        cnt_p[0] += 1
                        for k in range(KC):
                            P.op("pe", lambda e, pi=pi, wi=wi, k=k, c=c, m=m: e.matmul(
                                pacc[pi][0:m, 0:TA], wsl[wi][:, k, c:c + m], h1T[:, k, :],
                                start=(k == 0), stop=(k == KC - 1)),
                                reads=[b_wsl[wi], b_h1T], writes=[b_pacc[pi]])
                        if fm == "afab":
                            P.op("act", lambda e, pi=pi: e.activation(out=stg32[:, 0:TA], in_=pacc[pi][0:32, 0:TA],
                                                                      func=AF.Copy),
                                 reads=[b_pacc[pi]], writes=[b_stg32])
                            P.dma("pool", lambda e, tok0=tok0: e.dma_start(out=afT_d[:, tok0:tok0 + TA],
                                                                            in_=stg32[0:16, 0:TA]),
                                  "stgA32", reads=[b_stg32], writes=[B["afT"]])
                            P.dma("pool", lambda e, tok0=tok0: e.dma_start(out=abT_d[:, tok0:tok0 + TA],
                                                                            in_=stg32[16:32, 0:TA]),
                                  "stgA32", reads=[b_stg32], writes=[B["abT"]])
                        else:
                            si = cnt_s[0] % NST
                            cnt_s[0] += 1
                            P.op("dve", lambda e, pi=pi, si=si: e.tensor_copy(out=stg[si][:, 0:TA],
                                                                              in_=pacc[pi][:, 0:TA]),
                                 reads=[b_pacc[pi]], writes=[b_stg[si]])
                            r0 = s0 + c
                            P.dma("pool", lambda e, si=si, fm=fm, r0=r0, tok0=tok0: e.dma_start(
                                out=fm[r0:r0 + 128, tok0:tok0 + TA], in_=stg[si][:, 0:TA]), "stgA%d" % si,
                                reads=[b_stg[si]], writes=[B[name + "T"]])
                        cast_tick()
                if tm is not None:
                    for b in range(NBA):
                        pi = cnt_p[0] % 4
                        cnt_p[0] += 1
                        for k in range(KC):
                            P.op("pe", lambda e, pi=pi, wi=wi, k=k, b=b, ncols=ncols: e.matmul(
                                pacc[pi][:, 0:ncols], h1T[:, k, b * 128:(b + 1) * 128], wsl[wi][:, k, 0:ncols],
                                start=(k == 0), stop=(k == KC - 1)),
                                reads=[b_wsl[wi], b_h1T], writes=[b_pacc[pi]])
                        si = cnt_s[0] % NST
                        cnt_s[0] += 1
                        P.op("dve", lambda e, pi=pi, si=si, ncols=ncols: e.tensor_copy(
                            out=stg[si][:, 0:ncols], in_=pacc[pi][:, 0:ncols]),
                            reads=[b_pacc[pi]], writes=[b_stg[si]])
                        r0 = tok0 + b * 128
                        P.dma("pool", lambda e, si=si, tm=tm, r0=r0, s0=s0, ncols=ncols: e.dma_start(
                            out=tm[r0:r0 + 128, s0:s0 + ncols], in_=stg[si][:, 0:ncols]), "stgA%d" % si,
                            reads=[b_stg[si]], writes=[B[name]])
                        cast_tick()
        end_phase(MARK1)

    if cfg.stop_after == "A":
        P.emit()
        es.close()
        return nc

    if True:
        relb = sb([33, NHA], F32)
        cOH = sb([33, 512], F32)
        tv = sb([NHA, 512], F32)
        hank = [sb([128, 384], F32) for i in range(2)]
        bias = sb([128, NHA, 384], F32)
        sinkbc = sb([128, NHA], F32)
        kTs = [sb([128, TOK], BF16) for i in range(2)]
        vs = [sb([128, NBT, 128], BF16) for i in range(2)]
        qTs = [sb([128, TOK], BF16) for i in range(2)]
        GB = 4
        s_sb = [sb([128, 384], F32) for i in range(GB)]
        p_sb = [sb([128, 384], F32) for i in range(GB)]
        pn_sb = [[sb([128, 384], BF16) for i in range(GB)] for par in range(2)]
        pT_sb = [sb([128, 384], BF16) for i in range(GB)]
        ostg = [sb([128, 512], BF16) for i in range(2)]
        sm = sb([128, GB, 8], F32)
        ps_s2 = [ps([128, 384]) for i in range(2)]
        ps_s = [ps_s2[i % 2] for i in range(GB)]
        ps_t = [ps([128, 384], BF16) for i in range(GB)]
        ps_ob = [ps([128, 512]) for i in range(2)]
        ps_o = [ps_ob[i % 2][:, (i // 2) * 128:(i // 2 + 1) * 128] for i in range(GB)]
        ps_x = ps_ob[0]
        b_relb = P.buf("relb"); b_cOH = P.buf("cOH"); b_tv = P.buf("tv")
        b_hank = [P.buf("hank%d" % i) for i in range(2)]
        b_bias = P.buf("bias"); b_sink = P.buf("sinkbc")
        b_kTs = [P.buf("kTs%d" % i) for i in range(2)]
        b_vs = [P.buf("vs%d" % i) for i in range(2)]
        b_qTs = [P.buf("qTs%d" % i) for i in range(2)]
        b_s = [P.buf("s_sb%d" % i) for i in range(GB)]
        b_p = [P.buf("p_sb%d" % i) for i in range(GB)]
        b_pn = [[P.buf("pn%d_%d" % (par, i)) for i in range(GB)] for par in range(2)]
        b_pT = [P.buf("pT%d" % i) for i in range(GB)]
        b_ostg = [P.buf("ostg%d" % i) for i in range(2)]
        b_sm = [P.buf("smA%d" % i) for i in range(GB)]
        b_ps_s2 = [P.buf("ps_s%d" % i) for i in range(2)]
        b_ps_s = [b_ps_s2[i % 2] for i in range(GB)]
        b_ps_t = [P.buf("ps_t%d" % i) for i in range(GB)]
        b_ps_ob = [P.buf("ps_o%d" % i) for i in range(2)]
        b_ps_o = [b_ps_ob[i % 2] for i in range(GB)]
        b_ps_x = b_ps_ob[0]

        P.dma("sp", lambda e: e.dma_start(out=relb, in_=relb_d), "relb", writes=[b_relb])
        P.dma("sp", lambda e: e.dma_start(out=cOH, in_=cOH_d), "cOH", writes=[b_cOH])
        P.dma("sp", lambda e: e.dma_start(out=sinkbc, in_=sink_d.partition_broadcast(128)), "sinkbc",
              writes=[b_sink])
        P.op("pe", lambda e: e.matmul(ps_x[0:NHA, :], relb, cOH, start=True, stop=True),
             reads=[b_relb, b_cOH], writes=[b_ps_x])
        P.op("act", lambda e: e.activation(out=tv, in_=ps_x[0:NHA, :], func=AF.Copy), reads=[b_ps_x], writes=[b_tv])
        P.dma("sp", lambda e: e.dma_start(out=tvec_d, in_=tv), "tv", reads=[b_tv], writes=[B["tvec"]])
        for h in range(NHA):
            hi = h % 2
            P.dma("sp", lambda e, h=h, hi=hi: e.dma_start(
                out=hank[hi], in_=bass.AP(tvec_d.tensor, h * 512, [[1, 128], [1, 384]])), "hank%d" % hi,
                reads=[B["tvec"]], writes=[b_hank[hi]])
            P.op("pe", lambda e, hi=hi: e.matmul(ps_x[:, 0:384], Jm, hank[hi], start=True, stop=True),
                 reads=[b_cA, b_hank[hi]], writes=[b_ps_x])
            P.op("act", lambda e, h=h: e.activation(out=bias[:, h, :], in_=ps_x[:, 0:384], func=AF.Copy),
                 reads=[b_ps_x], writes=[b_bias])

        dbg_dump("bias", bias, [128, NHA, 384], F32, [b_bias])
        scale_a = 1.0 / math.sqrt(128.0)
        hcount = 0
        cast_mode["q"] = "sp"
        cast_mode["engs"] = ("dve",)
        n_b1_slots = NHA * (NBT // 4)
        cast_per_b1 = int(math.ceil((1.0 - cfg.cast_frac_A) * n_cast_total / n_b1_slots)) + 1
        for kvh in range(NKV):
            ki = kvh % 2
            P.dma("sp", lambda e, ki=ki, kvh=kvh: e.dma_start(out=kTs[ki], in_=kT_d[kvh * 128:(kvh + 1) * 128, :]),
                  "kTs%d" % ki, reads=[B["kT"]], writes=[b_kTs[ki]])
            P.dma("sp", lambda e, ki=ki, kvh=kvh: e.dma_start(
                out=vs[ki], in_=v_d[:, kvh * 128:(kvh + 1) * 128].rearrange("(n p) d -> p n d", p=128)),
                "vs%d" % ki, reads=[B["v"]], writes=[b_vs[ki]])
            for g in range(GRP):
                h = kvh * GRP + g
                qi = hcount % 2
                hcount += 1
                P.dma("sp", lambda e, qi=qi, h=h: e.dma_start(out=qTs[qi], in_=qT_d[h * 128:(h + 1) * 128, :]),
                      "qTs%d" % qi, reads=[B["qT"]], writes=[b_qTs[qi]])

                def win(gn):
                    jlo = 0 if gn > 0 else 1
                    jhi = 2 if gn < NBT - 1 else 1
                    return jlo, jhi

                def front_steps(gn, par, h=h, qi=qi, ki=ki):
                    r = gn % GB
                    jlo, jhi = win(gn)
                    W = (jhi - jlo + 1) * 128
                    k0 = (gn - 1 + jlo) * 128
                    cross = None
                    if gn == NB - 1:
                        cross = (2 - jlo) * 128
                    elif gn == NB:
                        cross = 0
                    st = []
                    def s2():
                        P.op("pe", lambda e: e.matmul(ps_s[r][:, 0:W], qTs[qi][:, gn * 128:(gn + 1) * 128],
                                                      kTs[ki][:, k0:k0 + W], start=True, stop=True),
                             reads=[b_qTs[qi], b_kTs[ki]], writes=[b_ps_s[r]])
                        P.op("dve", lambda e: e.scalar_tensor_tensor(
                            out=s_sb[r][:, 0:W], in0=ps_s[r][:, 0:W], scalar=scale_a,
                            in1=bias[:, h, jlo * 128:jlo * 128 + W], op0=ALU.mult, op1=ALU.add),
                            reads=[b_ps_s[r], b_bias], writes=[b_s[r]])
                        if cross is not None:
                            P.op("dve", lambda e: e.tensor_scalar(
                                out=s_sb[r][:, cross:cross + 128], in0=s_sb[r][:, cross:cross + 128],
                                scalar1=flags[:, 1:2], scalar2=None, op0=ALU.add),
                                reads=[b_s[r], b_flags], writes=[b_s[r]])
                    st.append(s2)
                    st.append(lambda: P.op("dve", lambda e: e.tensor_reduce(out=sm[:, r, 0:1], in_=s_sb[r][:, 0:W],
                                                                            axis=AX.X, op=ALU.max, negate=True),
                                           reads=[b_s[r]], writes=[b_sm[r]], tiny=True))
                    st.append(lambda: P.op("act", lambda e: e.activation(
                        out=p_sb[r][:, 0:W], in_=s_sb[r][:, 0:W], func=AF.Exp, bias=sm[:, r, 0:1], scale=1.0,
                        accum_out=sm[:, r, 1:2]), reads=[b_s[r], b_sm[r]], writes=[b_p[r], b_sm[r]]))
                    st.append(lambda: P.op("act", lambda e: e.activation(
                        out=sm[:, r, 2:3], in_=sm[:, r, 0:1], func=AF.Exp, bias=sinkbc[:, h:h + 1], scale=1.0),
                        reads=[b_sm[r], b_sink], writes=[b_sm[r]], tiny=True))
                    st.append(lambda: P.op("dve", lambda e: e.tensor_tensor(
                        out=sm[:, r, 3:4], in0=sm[:, r, 1:2], in1=sm[:, r, 2:3], op=ALU.add),
                        reads=[b_sm[r]], writes=[b_sm[r]], tiny=True))
                    st.append(lambda: P.op("dve", lambda e: e.reciprocal(out=sm[:, r, 4:5], in_=sm[:, r, 3:4]),
                                           reads=[b_sm[r]], writes=[b_sm[r]], tiny=True))
                    st.append(lambda: P.op("act", lambda e: e.activation(
                        out=pn_sb[par][r][:, 0:W], in_=p_sb[r][:, 0:W], func=AF.Copy, scale=sm[:, r, 4:5]),
                        reads=[b_p[r], b_sm[r]], writes=[b_pn[par][r]]))
                    return st

                def back_steps(gn, par, h=h, qi=qi, ki=ki):
                    r = gn % GB
                    jlo, jhi = win(gn)
                    nj = jhi - jlo + 1
                    W = nj * 128
                    st = []

                    def t1():
                        for j in range(nj):
                            P.op("pe", lambda e, j=j: e.transpose(ps_t[r][:, j * 128:(j + 1) * 128],
                                                                  pn_sb[par][r][:, j * 128:(j + 1) * 128], identB),
                                 reads=[b_pn[par][r], b_identB], writes=[b_ps_t[r]])
                    st.append(t1)
                    st.append(lambda: P.op("act", lambda e: e.activation(out=pT_sb[r][:, 0:W], in_=ps_t[r][:, 0:W],
                                                                         func=AF.Copy),
                                           reads=[b_ps_t[r]], writes=[b_pT[r]]))

                    def t3():
                        for j in range(nj):
                            kb = gn - 1 + jlo + j
                            P.op("pe", lambda e, j=j, kb=kb: e.matmul(ps_o[r], vs[ki][:, kb, :],
                                                                       pT_sb[r][:, j * 128:(j + 1) * 128],
                                                                       start=(j == 0), stop=(j == nj - 1)),
                                 reads=[b_vs[ki], b_pT[r]], writes=[b_ps_o[r]])
                    st.append(t3)

                    def t4():
                        oi = (gn // 4) % 2
                        c = (gn % 4) * 128
                        P.op("dve", lambda e: e.tensor_copy(out=ostg[oi][:, c:c + 128], in_=ps_o[r]),
                             reads=[b_ps_o[r]], writes=[b_ostg[oi]])
                        if gn % 4 == 3:
                            c0 = (gn - 3) * 128
                            P.dma("pool", lambda e: e.dma_start(out=mixT_d[h * 128:(h + 1) * 128, c0:c0 + 512],
                                                                in_=ostg[oi]), "ostg%d" % oi,
                                  reads=[b_ostg[oi]], writes=[B["mixT"]])
                    st.append(t4)
                    return st

                def emit_interleaved(lists):
                    for k in range(len(lists[0])):
                        for l in lists:
                            l[k]()

                NG = NBT // GB
                emit_interleaved([front_steps(gn, 0) for gn in range(0, GB)])
                for g in range(NG):
                    if g + 1 < NG:
                        emit_interleaved([front_steps(gn, (g + 1) % 2) for gn in range((g + 1) * GB, (g + 2) * GB)])
                    emit_interleaved([back_steps(gn, g % 2) for gn in range(g * GB, (g + 1) * GB)])
                    cast_some(cast_per_b1)
        cast_some(10 ** 9)
        P.scope_bufs.extend(cast_bufs)
        end_phase(MARK0)

    if cfg.stop_after == "B1":
        P.emit()
        es.close()
        return nc

    if True:
        afT1 = [sb([17, TOK], F32) for i in range(2)]
        wab = sb([17, 2, GKW], F32)
        gngbc = sb([128, 256], F32)
        qgTs = sb([128, TOK], BF16)
        kgTs = sb([128, TOK], BF16)
        kgs = sb([128, NBT, 128], BF16)
        vgs = sb([128, NBT, 256], BF16)
        srg = sb([128, NBT, 256], BF16)
        lsb = [sb([128, NBT, 128], F32) for i in range(2)]
        ost = sb([128, NBT, 256], F32)
        etmp = sb([128, 512], F32)
        E1 = [[sb([128, 128], F32) for r in range(2)] for d in range(2)]
        E2 = [[sb([128, 128], F32) for r in range(2)] for d in range(2)]
        E3 = [[sb([128, 128], F32) for r in range(2)] for d in range(2)]
        QtT = [[sb([128, 128], BF16) for r in range(2)] for d in range(2)]
        KtT = [[sb([128, 128], BF16) for r in range(2)] for d in range(2)]
        Kd = [[sb([128, 128], BF16) for r in range(2)] for d in range(2)]
        ATs = [[sb([128, 128], BF16) for r in range(2)] for d in range(2)]
        Sst = [sb([128, 256], F32) for d in range(2)]
        Sbf = [[sb([128, 256], BF16) for r in range(2)] for d in range(2)]
        t1 = [sb([128, 256], F32) for d in range(2)]
        ysb = [sb([128, 256], BF16) for d in range(2)]
        junkg = sb([128, 256], BF16)
        ystg = [[sb([128, 2, 512], BF16) for r in range(2)] for d in range(2)]
        smg = sb([128, 2, 8], F32)
        ps_zy = ps([128, 512])
        ps_z = ps_zy
        ps_y = ps_zy.bitcast(BF16)[:, 0:256]
        ps_bd = [ps([128, 256]) for d in range(2)]
        ps_a2 = [ps([128, 128]) for d in range(2)]
        ps_og = [ps([128, 256]) for d in range(2)]
        ps_u = ps([128, 256])
        b_afT1 = [P.buf("afT1_%d" % i) for i in range(2)]
        b_wab = P.buf("wab"); b_gng = P.buf("gngbc")
        b_qgTs = P.buf("qgTs"); b_kgTs = P.buf("kgTs"); b_kgs = P.buf("kgs"); b_vgs = P.buf("vgs")
        b_srg = P.buf("srg")
        b_l = [P.buf("lsb%d" % i) for i in range(2)]
        b_ost = [P.buf("ost%d" % n) for n in range(NBT)]
        b_etmp = P.buf("etmp")
        mk = lambda nm: [[P.buf("%s%d%d" % (nm, d, r)) for r in range(2)] for d in range(2)]
        b_E1 = mk("E1"); b_E2 = mk("E2"); b_E3 = mk("E3"); b_Qt = mk("Qt"); b_Kt = mk("Kt"); b_Kd = mk("Kd")
        b_AT = mk("AT"); b_Sbf = mk("Sbf"); b_ystg = mk("ystg")
        b_S = [P.buf("S%d" % d) for d in range(2)]
        b_t1 = [P.buf("t1_%d" % d) for d in range(2)]; b_ysb = [P.buf("ysb%d" % d) for d in range(2)]
        b_junkg = P.buf("junkg"); b_smg = [P.buf("smg%d" % d) for d in range(2)]
        b_ps_z = P.buf("ps_zy"); b_ps_bd = [P.buf("ps_bd%d" % d) for d in range(2)]
        b_ps_a2 = [P.buf("ps_a%d" % d) for d in range(2)]; b_ps_og = [P.buf("ps_og%d" % d) for d in range(2)]
        b_ps_u = P.buf("ps_u"); b_ps_y = b_ps_z

        for i, (src, bn) in enumerate(((afT_d, "afT"), (abT_d, "abT"))):
            P.op("pool", lambda e, i=i: e.memset(afT1[i], 1.0), writes=[b_afT1[i]])
            P.dma("sp", lambda e, i=i, src=src: e.dma_start(out=afT1[i][0:16, :], in_=src), "afT1_%d" % i,
                  reads=[B[bn]], writes=[b_afT1[i]])
        P.dma("sp", lambda e: e.dma_start(out=wab[:, 0, :], in_=wabf_d), "wab", writes=[b_wab])
        P.dma("sp", lambda e: e.dma_start(out=wab[:, 1, :], in_=wabb_d), "wab", writes=[b_wab])
        P.dma("sp", lambda e: e.dma_start(out=gngbc, in_=gng_d.partition_broadcast(128)), "gngbc", writes=[b_gng])
        scale_g = 1.0 / math.sqrt(128.0)
        T1 = [cA[:, 2, :], cA[:, 4, :]]
        T2 = [cA[:, 3, :], cA[:, 5, :]]
        MK = [cA[:, 6, :], cA[:, 7, :]]

        for h in range(NHG):
            P.dma("sp", lambda e, h=h: e.dma_start(out=qgTs, in_=qgT_d[h * 128:(h + 1) * 128, :]), "qgTs",
                  reads=[B["qgT"]], writes=[b_qgTs])
            P.dma("sp", lambda e, h=h: e.dma_start(out=kgTs, in_=kgT_d[h * 128:(h + 1) * 128, :]), "kgTs",
                  reads=[B["kgT"]], writes=[b_kgTs])
            P.dma("sp", lambda e, h=h: e.dma_start(
                out=kgs, in_=kg_d[:, h * 128:(h + 1) * 128].rearrange("(n p) d -> p n d", p=128)), "kgs",
                reads=[B["kg"]], writes=[b_kgs])
            P.dma("sp", lambda e, h=h: e.dma_start(
                out=vgs, in_=vg_d[:, h * 256:(h + 1) * 256].rearrange("(n p) d -> p n d", p=128)), "vgs",
                reads=[B["vg"]], writes=[b_vgs])
            P.dma("sp", lambda e, h=h: e.dma_start(
                out=srg, in_=rg_d[:, h * 256:(h + 1) * 256].rearrange("(n p) d -> p n d", p=128)), "srg",
                reads=[B["rg"]], writes=[b_srg])
            for d in range(2):
                for g4 in range(0, NBT, 4):
                    for j in range(4):
                        gn = g4 + j
                        P.op("pe", lambda e, d=d, gn=gn, j=j, h=h: e.matmul(
                            ps_z[:, j * 128:(j + 1) * 128], afT1[d][:, gn * 128:(gn + 1) * 128],
                            wab[:, d, h * 128:(h + 1) * 128], start=True, stop=True),
                            reads=[b_afT1[d], b_wab], writes=[b_ps_z])
                    P.op("act", lambda e: e.activation(out=etmp, in_=ps_z, func=AF.Exp, scale=-1.0),
                         reads=[b_ps_z], writes=[b_etmp])
                    P.op("act", lambda e, d=d, g4=g4: e.activation(
                        out=lsb[d][:, g4:g4 + 4, :], in_=etmp.rearrange("p (a b) -> p a b", a=4), func=AF.Ln,
                        bias=1.0, scale=1.0), reads=[b_etmp], writes=[b_l[d]])
            P.op("act", lambda e: e.activation(out=srg, in_=srg, func=AF.Silu), reads=[b_srg], writes=[b_srg])

            def prep(d, gn, r):
                P.op("pe", lambda e: e.matmul(ps_bd[d][:, 0:128], lsb[d][:, gn, :], T1[d], start=True, stop=True),
                     reads=[b_l[d], b_cA], writes=[b_ps_bd[d]])
                P.op("pe", lambda e: e.matmul(ps_bd[d][:, 128:256], T2[d], lsb[d][:, gn, :], start=True, stop=True),
                     reads=[b_l[d], b_cA], writes=[b_ps_bd[d]])
                P.op("act", lambda e: e.activation(out=E1[d][r], in_=ps_bd[d][:, 0:128], func=AF.Exp),
                     reads=[b_ps_bd[d]], writes=[b_E1[d][r]])
                P.op("act", lambda e: e.activation(out=E2[d][r], in_=ps_bd[d][:, 0:128], func=AF.Exp, scale=-1.0),
                     reads=[b_ps_bd[d]], writes=[b_E2[d][r]])
                P.op("act", lambda e: e.activation(out=E3[d][r], in_=ps_bd[d][:, 128:256], func=AF.Exp),
                     reads=[b_ps_bd[d]], writes=[b_E3[d][r]])
                P.op("dve", lambda e: e.scalar_tensor_tensor(
                    out=QtT[d][r], in0=qgTs[:, gn * 128:(gn + 1) * 128], scalar=scale_g, in1=E1[d][r],
                    op0=ALU.mult, op1=ALU.mult), reads=[b_qgTs, b_E1[d][r]], writes=[b_Qt[d][r]])
                P.op("pool", lambda e: e.tensor_tensor(out=KtT[d][r], in0=kgTs[:, gn * 128:(gn + 1) * 128],
                                                       in1=E2[d][r], op=ALU.mult),
                     reads=[b_kgTs, b_E2[d][r]], writes=[b_Kt[d][r]])
                P.op("pool", lambda e: e.tensor_tensor(out=Kd[d][r], in0=kgs[:, gn, :], in1=E3[d][r], op=ALU.mult),
                     reads=[b_kgs, b_E3[d][r]], writes=[b_Kd[d][r]])

            def post_front(d, gn):
                P.op("act", lambda e: e.activation(out=junkg, in_=ost[:, gn, :], func=AF.Square,
                                                   accum_out=smg[:, d, 0:1]),
                     reads=[b_ost[gn]], writes=[b_junkg, b_smg[d]])
                rstd_ops(smg[:, d, 0:1], b_smg[d], smg[:, d, 1:2], b_smg[d], 256, smg[:, d, 2:3], b_smg[d])
                P.op("dve", lambda e: e.scalar_tensor_tensor(out=t1[d], in0=ost[:, gn, :], scalar=smg[:, d, 1:2],
                                                             in1=gngbc, op0=ALU.mult, op1=ALU.mult),
                     reads=[b_ost[gn], b_smg[d], b_gng], writes=[b_t1[d]])
                P.op("pool", lambda e: e.tensor_tensor(out=ysb[d], in0=t1[d], in1=srg[:, gn, :], op=ALU.mult),
                     reads=[b_t1[d], b_srg], writes=[b_ysb[d]])

            def post_back(d, gn, h=h):
                for c in range(2):
                    P.op("pe", lambda e, c=c: e.transpose(ps_y[:, c * 128:(c + 1) * 128],
                                                          ysb[d][:, c * 128:(c + 1) * 128], identB),
                         reads=[b_ysb[d], b_identB], writes=[b_ps_y])
                yi = (gn // 4) % 2
                col = (gn % 4) * 128
                P.op("act", lambda e: e.activation(out=ystg[d][yi][:, :, col:col + 128],
                                                   in_=ps_y.rearrange("p (c t) -> p c t", c=2), func=AF.Copy),
                     reads=[b_ps_y], writes=[b_ystg[d][yi]])
                done = (gn % 4 == 3) if d == 0 else (gn % 4 == 0)
                if done:
                    c0 = (gn // 4) * 4 * 128
                    r0 = AW + h * 256
                    P.dma("pool", lambda e: e.dma_start(
                        out=mixT_d[r0:r0 + 256, c0:c0 + 512].rearrange("(c p) t -> p c t", p=128),
                        in_=ystg[d][yi]), "ystg%d%d" % (d, yi), reads=[b_ystg[d][yi]], writes=[B["mixT"]])

            def mainA(d, gn, r):
                P.op("pe", lambda e: e.matmul(ps_a2[d], KtT[d][r], QtT[d][r], start=True, stop=True),
                     reads=[b_Kt[d][r], b_Qt[d][r]], writes=[b_ps_a2[d]])
                P.op("dve", lambda e: e.tensor_tensor(out=ATs[d][r], in0=ps_a2[d], in1=MK[d], op=ALU.mult),
                     reads=[b_ps_a2[d], b_cA], writes=[b_AT[d][r]])

            def mainB(d, gn, r, first, step, pend):
                P.op("pe", lambda e: e.matmul(ps_og[d], ATs[d][r], vgs[:, gn, :], start=True, stop=first),
                     reads=[b_AT[d][r], b_vgs], writes=[b_ps_og[d]])
                if not first:
                    pr = (step - 1) % 2
                    P.op("pe", lambda e: e.matmul(ps_og[d], QtT[d][r], Sbf[d][pr], start=False, stop=True),
                         reads=[b_Qt[d][r], b_Sbf[d][pr]], writes=[b_ps_og[d]])
                P.op("pe", lambda e: e.matmul(ps_u, Kd[d][r], vgs[:, gn, :], start=True, stop=True),
                     reads=[b_Kd[d][r], b_vgs], writes=[b_ps_u])
                dec = E1[d][r][:, 127:128] if d == 0 else E1[d][r][:, 0:1]
                if first:
                    P.op("dve", lambda e: e.tensor_copy(out=Sst[d], in_=ps_u), reads=[b_ps_u], writes=[b_S[d]])
                else:
                    P.op("dve", lambda e: e.scalar_tensor_tensor(out=Sst[d], in0=Sst[d], scalar=dec, in1=ps_u,
                                                                 op0=ALU.mult, op1=ALU.add),
                         reads=[b_S[d], b_E1[d][r], b_ps_u], writes=[b_S[d]])
                boundary = (gn == NB - 1) if d == 0 else (gn == NB)
                if boundary:
                    P.op("dve", lambda e: e.tensor_scalar(out=Sst[d], in0=Sst[d], scalar1=flags[:, 0:1], scalar2=None,
                                                          op0=ALU.mult), reads=[b_S[d], b_flags], writes=[b_S[d]])
                sr = step % 2
                P.op("pool", lambda e: e.tensor_copy(out=Sbf[d][sr], in_=Sst[d]), reads=[b_S[d]],
                     writes=[b_Sbf[d][sr]])
                first_visit = (gn < NB) if d == 0 else (gn >= NB)
                if first_visit:
                    P.op("act", lambda e: e.activation(out=ost[:, gn, :], in_=ps_og[d], func=AF.Copy),
                         reads=[b_ps_og[d]], writes=[b_ost[gn]])
                else:
                    P.op("dve", lambda e: e.tensor_tensor(out=ost[:, gn, :], in0=ps_og[d], in1=ost[:, gn, :],
                                                          op=ALU.add), reads=[b_ps_og[d], b_ost[gn]],
                         writes=[b_ost[gn]])
                    pend.append((d, gn))

            blk = lambda d, i: i if d == 0 else NBT - 1 - i
            for d in range(2):
                prep(d, blk(d, 0), 0)
            pend = []
            for i in range(NBT):
                cur = pend
                pend = []
                for (pd, pg) in cur:
                    post_front(pd, pg)
                for d in range(2):
                    if i + 1 < NBT:
                        prep(d, blk(d, i + 1), (i + 1) % 2)
                for d in range(2):
                    mainA(d, blk(d, i), i % 2)
                for d in range(2):
                    mainB(d, blk(d, i), i % 2, i == 0, i, pend)
                for (pd, pg) in cur:
                    post_back(pd, pg)
            for (pd, pg) in pend:
                post_front(pd, pg)
            for (pd, pg) in pend:
                post_back(pd, pg)
        end_phase(MARK0)

    if cfg.stop_after == "B2":
        P.emit()
        es.close()
        return nc

    TC = cfg.TC
    NBC = TC // 128
    HFC = min(16, FC)
    FSPLIT = FC // HFC
    KG = min(8, KC)
    if True:
        x2 = [sb([128, D], F32) for b in range(NBC)]
        h2T = sb([128, KC, TC], BF16)
        RX = max(KC * TC, HFC * TC + 2 * KC * 128, D)
        regX = sb([128, RX], BF16)
        mixTs = regX[:, 0:KC * TC].rearrange("p (k t) -> p k t", k=KC)
        hid = regX[:, 0:HFC * TC].rearrange("p (f t) -> p f t", f=HFC)
        junkC = regX[:, 0:D]
        w1s = [regX[:, HFC * TC + i * KC * 128: HFC * TC + (i + 1) * KC * 128].rearrange("p (k j) -> p k j", k=KC)
               for i in range(2)]
        w1s.append(sb([128, KC, 128], BF16))
        NW2 = 4
        w2s = [sb([128, KG, 512], BF16) for i in range(NW2)]
        gs = [sb([128, 512], F32) for i in range(2)]
        tmpc = [sb([128, 512], F32) for i in range(2)]
        r32 = [sb([128, TC], F32) for i in range(2)]
        fgbc = sb([128, D], F32)
        py = [ps([128, 512]) for b in range(NBC)]
        ph = [ps([128, TC]) for i in range(2)]
        pstc = [ps([128, 512]) for i in range(8 - NBC - 2)]
        b_x2 = [P.buf("x2_%d" % b) for b in range(NBC)]
        b_h2T = P.buf("h2T")
        b_mixTs = P.buf("mixTs")
        b_hid = [P.buf("hid%d" % f) for f in range(HFC)]
        b_w1s = [P.buf("w1s%d" % i) for i in range(3)]
        b_w2s = [P.buf("w2s%d" % i) for i in range(NW2)]
        b_gs = [P.buf("gs%d" % i) for i in range(2)]
        b_tmpc = [P.buf("tmpc%d" % i) for i in range(2)]
        b_r32 = [P.buf("r32_%d" % i) for i in range(2)]
        b_fgbc = P.buf("fgbc")
        b_py = [P.buf("py%d" % b) for b in range(NBC)]
        b_ph = [P.buf("ph%d" % i) for i in range(2)]
        b_pstc = [P.buf("pstc%d" % i) for i in range(len(pstc))]
        b_ss = P.buf("ssC"); b_rs = P.buf("rsC"); b_tmp = P.buf("tmpC")
        ssv, rsv, tmpv = small[:, 1:2], small[:, 2:3], small[:, 3:4]
        P.dma("sp", lambda e: e.dma_start(out=fgbc, in_=fg_d.partition_broadcast(128)), "fgbc", writes=[b_fgbc])
        c_w1 = [0]; c_w2 = [0]; c_gs = [0]; c_tmp = [0]; c_ph = [0]; c_r = [0]
        mixv = mixT_d.rearrange("(k p) t -> p k t", p=128)

        def evac_add(b, j, gidx, half):
            gi = c_gs[0] % 2
            c_gs[0] += 1
            P.dma("pool", lambda e: e.dma_start(
                out=gs[gi], in_=gfull_d[gidx, half:half + 1, j * 512:(j + 1) * 512].partition_broadcast(128)),
                "gs%d" % gi, reads=[B["gfull"]], writes=[b_gs[gi]])
            return gi

        for t in range(TOK // TC):
            tok0 = t * TC
            half = tok0 // L
            for b in range(NBC):
                P.dma("sp", lambda e, b=b, r0=tok0 + b * 128: e.dma_start(out=x2[b], in_=x_d[r0:r0 + 128, :]),
                      "x2_%d" % b, writes=[b_x2[b]])
            P.dma("sp", lambda e, tok0=tok0: e.dma_start(out=mixTs, in_=mixv[:, :, tok0:tok0 + TC]), "mixTs",
                  reads=[B["mixT"]], writes=[b_mixTs] + b_hid + b_w1s[0:2])
            for j in range(D // 512):
                gi = evac_add(None, j, 0, half)
                for kg in range(KC // KG):
                    wi = c_w2[0] % NW2
                    c_w2[0] += 1
                    P.dma("sp", lambda e, wi=wi, kg=kg, j=j: e.dma_start(
                        out=w2s[wi], in_=woutb_d[kg * KG * 128:(kg + 1) * KG * 128, j * 512:(j + 1) * 512].rearrange(
                            "(f p) n -> p f n", p=128)), "w2s%d" % wi, reads=[B["woutb"]], writes=[b_w2s[wi]])
                    for b in range(NBC):
                        for f in range(KG):
                            k = kg * KG + f
                            P.op("pe", lambda e, b=b, f=f, k=k, wi=wi: e.matmul(
                                py[b], mixTs[:, k, b * 128:(b + 1) * 128], w2s[wi][:, f, :],
                                start=(k == 0), stop=(k == KC - 1)),
                                reads=[b_mixTs, b_w2s[wi]], writes=[b_py[b]])
                for b in range(NBC):
                    ti = c_tmp[0] % 2
                    c_tmp[0] += 1
                    P.op("dve", lambda e, b=b, ti=ti, gi=gi: e.tensor_tensor(out=tmpc[ti], in0=py[b], in1=gs[gi],
                                                                             op=ALU.mult),
                         reads=[b_py[b], b_gs[gi]], writes=[b_tmpc[ti]])
                    P.op("pool", lambda e, b=b, ti=ti, j=j: e.tensor_tensor(
                        out=x2[b][:, j * 512:(j + 1) * 512], in0=x2[b][:, j * 512:(j + 1) * 512], in1=tmpc[ti],
                        op=ALU.add), reads=[b_x2[b], b_tmpc[ti]], writes=[b_x2[b]])
            for b in range(NBC):
                norm_transpose_block(x2[b], b_x2[b], half, 1, h2T, b_h2T, b * 128, pstc, b_pstc, junkC,
                                     [b_mixTs], ssv, b_ss, rsv, b_rs, tmpv, b_tmp, undo=True)
            for fh in range(FSPLIT):
                for fi in range(HFC):
                    fc = fh * HFC + fi
                    wi = c_w1[0] % 3
                    c_w1[0] += 1
                    extra = [b_mixTs] if (fh == 0 and fi < 3 and wi < 2) else []
                    P.dma("sp", lambda e, wi=wi, fc=fc: e.dma_start(out=w1s[wi], in_=w1t_d[fc]), "w1s%d" % wi,
                          reads=[B["w1t"]], writes=[b_w1s[wi]] + extra)
                    pi = c_ph[0] % 2
                    c_ph[0] += 1
                    for k in range(KC):
                        P.op("pe", lambda e, wi=wi, k=k, pi=pi: e.matmul(ph[pi], w1s[wi][:, k, :], h2T[:, k, :],
                                                                         start=(k == 0), stop=(k == KC - 1)),
                             reads=[b_w1s[wi], b_h2T], writes=[b_ph[pi]])
                    ri = c_r[0] % 2
                    c_r[0] += 1
                    P.op("act", lambda e, pi=pi, ri=ri: e.activation(out=r32[ri], in_=ph[pi], func=AF.Relu),
                         reads=[b_ph[pi]], writes=[b_r32[ri]])
                    sq_eng = "dve" if fi % 2 == 0 else "pool"
                    P.op(sq_eng, lambda e, fi=fi, ri=ri: e.tensor_tensor(out=hid[:, fi, :], in0=r32[ri], in1=r32[ri],
                                                                         op=ALU.mult),
                         reads=[b_r32[ri]], writes=[b_hid[fi]])
                for j in range(D // 512):
                    gi = evac_add(None, j, 1, half)
                    for g in range(HFC // KG):
                        wi = c_w2[0] % NW2
                        c_w2[0] += 1
                        f0 = fh * HFC + g * KG
                        P.dma("sp", lambda e, wi=wi, f0=f0, j=j: e.dma_start(
                            out=w2s[wi], in_=w2b_d[f0 * 128:(f0 + KG) * 128, j * 512:(j + 1) * 512].rearrange(
                                "(f p) n -> p f n", p=128)), "w2s%d" % wi, reads=[B["w2b"]], writes=[b_w2s[wi]])
                        for b in range(NBC):
                            for f in range(KG):
                                fi = g * KG + f
                                P.op("pe", lambda e, b=b, f=f, fi=fi, wi=wi: e.matmul(
                                    py[b], hid[:, fi, b * 128:(b + 1) * 128], w2s[wi][:, f, :],
                                    start=(fi == 0), stop=(fi == HFC - 1)),
                                    reads=[b_hid[fi], b_w2s[wi]], writes=[b_py[b]])
                    for b in range(NBC):
                        ti = c_tmp[0] % 2
                        c_tmp[0] += 1
                        P.op("dve", lambda e, b=b, ti=ti, gi=gi: e.tensor_tensor(out=tmpc[ti], in0=py[b], in1=gs[gi],
                                                                                 op=ALU.mult),
                             reads=[b_py[b], b_gs[gi]], writes=[b_tmpc[ti]])
                        P.op("pool", lambda e, b=b, ti=ti, j=j: e.tensor_tensor(
                            out=x2[b][:, j * 512:(j + 1) * 512], in0=x2[b][:, j * 512:(j + 1) * 512], in1=tmpc[ti],
                            op=ALU.add), reads=[b_x2[b], b_tmpc[ti]], writes=[b_x2[b]])
            for b in range(NBC):
                P.op("act", lambda e, b=b: e.activation(out=junkC, in_=x2[b], func=AF.Square, accum_out=ssv),
                     reads=[b_x2[b]], writes=[b_mixTs, b_ss])
                rstd_ops(ssv, b_ss, rsv, b_rs, D, tmpv, b_tmp)
                P.op("dve", lambda e, b=b: e.scalar_tensor_tensor(out=x2[b], in0=x2[b], scalar=rsv, in1=fgbc,
                                                                  op0=ALU.mult, op1=ALU.mult),
                     reads=[b_x2[b], b_rs, b_fgbc], writes=[b_x2[b]])
                P.dma("pool", lambda e, b=b, r0=tok0 + b * 128: e.dma_start(out=y_d[r0:r0 + 128, :], in_=x2[b]),
                      "x2_%d" % b, reads=[b_x2[b]], writes=[B["y"]])
        end_phase(MARK0)

    P.emit()
    es.close()
    return nc


def shard_inputs(cfg, inp):
    D, L = cfg.D, cfg.L
    cA, cOH = host_consts()
    f = lambda a: np.ascontiguousarray(np.asarray(a, dtype=np.float32))
    xp, xs = f(inp["x_prompt"]), f(inp["x_sample"])
    cp, cs = f(inp["c_prompt"]), f(inp["c_sample"])
    common = {
        "w_ada": f(inp["w_ada"][0]), "b_ada": f(inp["b_ada"][0]).reshape(1, -1),
        "n1g": f(inp["norm1_g"][0]).reshape(1, -1), "w_in": f(inp["w_in"][0]),
        "wab_f": f(np.concatenate([inp["gla_wa_fwd"][0], inp["gla_ba_fwd"][0][None, :]], axis=0)),
        "wab_b": f(np.concatenate([inp["gla_wa_bwd"][0], inp["gla_ba_bwd"][0][None, :]], axis=0)),
        "gng": f(inp["gla_norm_g"][0]).reshape(1, -1), "sink": f(inp["attn_sink"][0]).reshape(1, -1),
        "relb": f(np.concatenate([inp["rel_bias"], np.ones((1, inp["rel_bias"].shape[1]), np.float32)], axis=0)),
        "w_out": f(inp["w_out"][0]), "n2g": f(inp["norm2_g"][0]).reshape(1, -1),
        "w1": f(inp["w_mlp_in"][0]), "w2": f(inp["w_mlp_out"][0]), "fg": f(inp["final_g"]).reshape(1, -1),
        "cA": cA, "cOH": cOH,
    }
    maps = []
    npc = xp.shape[0] // 2
    for c in range(8):
        m = dict(common)
        fl = np.zeros((128, 2), np.float32)
        if c < npc:
            m["x"] = np.ascontiguousarray(xp[2 * c:2 * c + 2].reshape(2 * L, D))
            m["c2"] = np.ascontiguousarray(cp[2 * c:2 * c + 2])
            fl[:, 0] = 0.0
            fl[:, 1] = NEG
        else:
            s = c - npc
            m["x"] = np.ascontiguousarray(xs[s].reshape(2 * L, D))
            m["c2"] = np.ascontiguousarray(np.stack([cs[s], cs[s]], axis=0))
            fl[:, 0] = 1.0
            fl[:, 1] = 0.0
        m["flags"] = fl
        maps.append(m)
    return maps


_CACHE = {}


def kernel(**inputs):
    cfg = Cfg()
    nc = build_program(cfg)
    maps = shard_inputs(cfg, inputs)
    res = run_bass_kernel_spmd(nc, maps, core_ids=list(range(8)))
    D, L = cfg.D, cfg.L
    npc = inputs["x_prompt"].shape[0] // 2
    yp = np.stack([res.results[c]["y"].reshape(2, L, D) for c in range(npc)], axis=0).reshape(-1, L, D)
    ys = np.stack([res.results[c]["y"].reshape(2 * L, D) for c in range(npc, 8)], axis=0)
    return (np.ascontiguousarray(yp, dtype=np.float32), np.ascontiguousarray(ys, dtype=np.float32))
```

```python
import contextlib
import math

import numpy as np
import concourse.bass as bass
import concourse.mybir as mybir
from concourse.bass_utils import run_bass_kernel_spmd

F32 = mybir.dt.float32
BF16 = mybir.dt.bfloat16
AF = mybir.ActivationFunctionType
ALU = mybir.AluOpType
AX = mybir.AxisListType

ENGS = ("pe", "act", "dve", "pool", "sp")
EPS = 1e-6
NEG = -30000.0


class Buf:
    __slots__ = ("name", "last_w", "readers", "multi", "writers")

    def __init__(self, name, multi=False, carry=()):
        self.name = name
        self.last_w = None
        self.readers = list(carry)
        self.multi = multi
        self.writers = []


class Op:
    __slots__ = ("eng", "fn", "deps", "is_dma", "dsem", "dval", "needs_inc", "ms", "idx", "tiny")

    def __init__(self, eng, fn, is_dma=False):
        self.eng = eng
        self.fn = fn
        self.deps = []
        self.is_dma = is_dma
        self.dsem = None
        self.dval = 0
        self.needs_inc = False
        self.ms = None
        self.idx = None
        self.tiny = False


class Prog:
    def __init__(self, nc):
        self.nc = nc
        self.ops = []
        self.dma_sems = {}
        self.carry = []
        self.scope_bufs = []

    def buf(self, name, multi=False):
        b = Buf(name, multi, self.carry)
        self.scope_bufs.append(b)
        return b

    def release_scope(self):
        pend = list(self.carry)
        for b in self.scope_bufs:
            if b.last_w is not None:
                pend.append(b.last_w)
            pend.extend(b.readers)
            pend.extend(b.writers)
        last = {}
        dmas = {}
        for o in pend:
            if o.is_dma:
                k = o.dsem
                if k not in dmas or dmas[k].dval < o.dval:
                    dmas[k] = o
            else:
                if o.eng not in last or last[o.eng].idx < o.idx:
                    last[o.eng] = o
        self.carry = list(last.values()) + list(dmas.values())
        self.scope_bufs = []

    def _add(self, op, reads, writes):
        deps = []
        for b in reads:
            if b.multi:
                deps.extend(b.writers)
            elif b.last_w is not None:
                deps.append(b.last_w)
        for b in writes:
            if b.multi:
                deps.extend(b.readers)
                b.writers.append(op)
            else:
                if b.last_w is not None:
                    deps.append(b.last_w)
                deps.extend(b.readers)
                b.last_w = op
                b.readers = []
        for b in reads:
            if not b.multi or True:
                b.readers.append(op)
        op.deps = deps
        op.idx = len(self.ops)
        self.ops.append(op)
        return op

    def op(self, eng, fn, reads=(), writes=(), tiny=False):
        o = Op(eng, fn)
        o.tiny = tiny
        return self._add(o, reads, writes)

    def dma(self, queue, fn, semkey, reads=(), writes=()):
        op = Op(queue, fn, is_dma=True)
        ent = self.dma_sems.setdefault(semkey, [0])
        ent[0] += 16
        op.dsem = semkey
        op.dval = ent[0]
        return self._add(op, reads, writes)

    def emit(self):
        nc = self.nc
        for op in self.ops:
            for d in op.deps:
                if (not d.is_dma) and (d.eng != op.eng or d.tiny or op.is_dma):
                    d.needs_inc = True
        cnt = {e: 0 for e in ENGS}
        for op in self.ops:
            if (not op.is_dma) and op.needs_inc:
                cnt[op.eng] += 1
                op.ms = cnt[op.eng]
        with contextlib.ExitStack() as st:
            esem = {e: st.enter_context(nc.semaphore("ms_" + e)) for e in ENGS}
            dsem = {}
            for i, k in enumerate(self.dma_sems):
                dsem[k] = st.enter_context(nc.semaphore("d%d" % i))
            block = st.enter_context(nc.Block())
            per_eng = {e: [] for e in ENGS}
            for op in self.ops:
                per_eng[op.eng].append(op)
            final = [(k, v[0]) for k, v in self.dma_sems.items()]

            def run(eng_name, engine):
                waited_e = {e: 0 for e in ENGS}
                waited_d = {}
                for op in per_eng[eng_name]:
                    need_e = {}
                    need_d = {}
                    for d in op.deps:
                        if d.is_dma:
                            if need_d.get(d.dsem, 0) < d.dval:
                                need_d[d.dsem] = d.dval
                        elif d.eng != eng_name or d.tiny or op.is_dma:
                            if need_e.get(d.eng, 0) < d.ms:
                                need_e[d.eng] = d.ms
                    for e, v in need_e.items():
                        if waited_e[e] < v:
                            engine.wait_ge(esem[e], v)
                            waited_e[e] = v
                    for k, v in need_d.items():
                        if waited_d.get(k, 0) < v:
                            engine.wait_ge(dsem[k], v)
                            waited_d[k] = v
                    ins = op.fn(engine)
                    if op.is_dma:
                        ins.then_inc(dsem[op.dsem], 16)
                    elif op.needs_inc:
                        ins.then_inc(esem[eng_name], 1)
                if eng_name == "sp":
                    for k, v in final:
                        if waited_d.get(k, 0) < v:
                            engine.wait_ge(dsem[k], v)

            block.tensor(lambda e: run("pe", e))
            block.scalar(lambda e: run("act", e))
            block.vector(lambda e: run("dve", e))
            block.gpsimd(lambda e: run("pool", e))
            block.sync(lambda e: run("sp", e))
        return cnt


class Cfg:
    def __init__(self, D=4096, L=2048, TA=512, TC=512, debug=False, stop_after=None, ada_fp32r=False,
                 cast_frac_A=0.6, cast_pool=False, cast_q_A="sp"):
        self.cast_q_A = cast_q_A
        self.stop_after = stop_after
        self.ada_fp32r = ada_fp32r
        self.cast_frac_A = cast_frac_A
        self.cast_pool = cast_pool
        self.D = D
        self.L = L
        self.HD = 128
        self.NHA = D // 2 // 128
        self.NKV = max(self.NHA // 4, 1)
        self.GRP = self.NHA // self.NKV
        self.DV = 256
        self.DK = 128
        self.NHG = D // 2 // 256
        self.RANK = 16
        self.DFF = 4 * D
        self.KC = D // 128
        self.AW = self.NHA * 128
        self.KVW = self.NKV * 128
        self.GKW = self.NHG * 128
        self.GVW = self.NHG * 256
        self.INW = self.AW + 2 * self.KVW + 2 * self.GKW + 2 * self.GVW + 32
        self.NB = L // 128
        self.TOK = 2 * L
        self.NBT = 2 * self.NB
        self.TA = TA
        self.TC = TC
        self.FC = self.DFF // 128
        self.debug = debug
        o = 0
        self.o_q = o; o += self.AW
        self.o_k = o; o += self.KVW
        self.o_v = o; o += self.KVW
        self.o_qg = o; o += self.GKW
        self.o_kg = o; o += self.GKW
        self.o_vg = o; o += self.GVW
        self.o_rg = o; o += self.GVW
        self.o_af = o; o += 16
        self.o_ab = o; o += 16
        assert o == self.INW


def t5_bucket_np(rel):
    half = 16
    max_exact = 8
    ret = np.where(rel > 0, half, 0)
    n = np.abs(rel)
    nf = np.maximum(n, 1).astype(np.float32)
    large = max_exact + (np.log(nf / max_exact) / math.log(128 / max_exact) * (half - max_exact)).astype(np.int32)
    large = np.minimum(large, half - 1)
    return ret + np.where(n < max_exact, n, large)


def host_consts():
    i = np.arange(128)
    s = i[:, None]
    c = i[None, :]
    g = -1.0 / 16.0
    cA = np.zeros((128, 8, 128), np.float32)
    cA[:, 0, :] = np.eye(128)
    cA[:, 1, :] = np.eye(128)[::-1]
    cA[:, 2, :] = np.where(s <= c, g, 0.0)
    cA[:, 3, :] = np.where(s > c, g, 0.0)
    cA[:, 4, :] = np.where(s >= c, g, 0.0)
    cA[:, 5, :] = np.where(s < c, g, 0.0)
    cA[:, 6, :] = np.where(s <= c, 1.0, 0.0)
    cA[:, 7, :] = np.where(s >= c, 1.0, 0.0)
    r = np.arange(512)
    rel = r - 255
    bk = t5_bucket_np(rel)
    cOH = np.zeros((33, 512), np.float32)
    cOH[bk, r] = 1.0
    cOH[:32, 511] = 0.0
    cOH[32, :] = np.where(np.abs(rel) <= 128, 0.0, NEG)
    cOH[32, 511] = NEG
    return cA, cOH


class Arena:
    def __init__(self, big, nwords):
        self.big = big
        self.n = nwords
        self.top = 0

    def alloc(self, shape, dt):
        nel = 1
        for s in shape[1:]:
            nel *= s
        nbytes = nel * (2 if dt == BF16 else 4)
        words = (nbytes + 3) // 4
        a = self.top
        self.top += (words + 7) // 8 * 8
        assert self.top <= self.n, "SBUF arena overflow: %d > %d words" % (self.top, self.n)
        v = self.big[0:shape[0], a:a + words]
        if dt == BF16:
            v = v.bitcast(BF16)
        if len(shape) == 3:
            v = v.rearrange("p (a b) -> p a b", a=shape[1])
        elif len(shape) == 4:
            v = v.rearrange("p (a b c) -> p a b c", a=shape[1], b=shape[2])
        return v


def build_program(cfg):
    nc = bass.Bass("TRN2", target_bir_lowering=False)
    D, L, KC, TOK, NB, NBT = cfg.D, cfg.L, cfg.KC, cfg.TOK, cfg.NB, cfg.NBT
    NHA, NKV, GRP, NHG = cfg.NHA, cfg.NKV, cfg.GRP, cfg.NHG
    AW, KVW, GKW, GVW, INW, DFF, FC = cfg.AW, cfg.KVW, cfg.GKW, cfg.GVW, cfg.INW, cfg.DFF, cfg.FC

    def din(name, shape, dt=F32):
        return nc.dram_tensor(name, list(shape), dt, kind="ExternalInput").ap()

    def dscr(name, shape, dt):
        dbg = cfg.debug is True or (cfg.debug and name in cfg.debug)
        return nc.dram_tensor(name, list(shape), dt, kind="ExternalOutput" if dbg else "Internal").ap()

    x_d = din("x", [TOK, D])
    c2_d = din("c2", [2, D])
    flags_d = din("flags", [128, 2])
    wada_d = din("w_ada", [D, 6 * D])
    bada_d = din("b_ada", [1, 6 * D])
    n1g_d = din("n1g", [1, D])
    win_d = din("w_in", [D, INW])
    wabf_d = din("wab_f", [17, GKW])
    wabb_d = din("wab_b", [17, GKW])
    gng_d = din("gng", [1, 256])
    sink_d = din("sink", [1, NHA])
    relb_d = din("relb", [33, NHA])
    wout_d = din("w_out", [D, D])
    n2g_d = din("n2g", [1, D])
    w1_d = din("w1", [D, DFF])
    w2_d = din("w2", [DFF, D])
    fg_d = din("fg", [1, D])
    cA_d = din("cA", [128, 8, 128])
    cOH_d = din("cOH", [33, 512])
    y_d = nc.dram_tensor("y", [TOK, D], F32, kind="ExternalOutput").ap()

    modraw_d = dscr("modraw", [2, 6 * D], F32)
    gfull_d = dscr("gfull", [2, 2, D], F32)
    winb_d = dscr("winb", [D, INW], BF16)
    woutb_d = dscr("woutb", [D, D], BF16)
    w1t_d = dscr("w1t", [FC, 128, KC, 128], BF16)
    w2b_d = dscr("w2b", [DFF, D], BF16)
    qT_d = dscr("qT", [AW, TOK], BF16)
    kT_d = dscr("kT", [KVW, TOK], BF16)
    v_d = dscr("v", [TOK, KVW], BF16)
    qgT_d = dscr("qgT", [GKW, TOK], BF16)
    kgT_d = dscr("kgT", [GKW, TOK], BF16)
    kg_d = dscr("kg", [TOK, GKW], BF16)
    vg_d = dscr("vg", [TOK, GVW], BF16)
    rg_d = dscr("rg", [TOK, GVW], BF16)
    afT_d = dscr("afT", [16, TOK], F32)
    abT_d = dscr("abT", [16, TOK], F32)
    tvec_d = dscr("tvec", [NHA, 512], F32)
    mixT_d = dscr("mixT", [D, TOK], BF16)

    P = Prog(nc)
    es = contextlib.ExitStack()
    NWORDS = 53000
    big = es.enter_context(nc.sbuf_tensor("big", [128, NWORDS], F32))
    pbig = es.enter_context(nc.psum_tensor("pbig", [128, 4096], F32))
    ar = Arena(big, NWORDS)
    pstate = [0]

    def sb(shape, dt):
        return ar.alloc(list(shape), dt)

    def ps(shape, dt=F32):
        b = pstate[0]
        pstate[0] += 1
        assert b < 8, "PSUM banks exhausted"
        v = pbig[0:shape[0], b * 512:(b + 1) * 512]
        if dt == BF16:
            v = v.bitcast(BF16)
        return v[:, 0:shape[1]]

    def end_phase(mark):
        ar.top = mark
        pstate[0] = 0
        P.release_scope()

    B = {n: Buf(n, multi=True) for n in
         ["modraw", "gfull", "winb", "woutb", "w1t", "w2b", "qT", "kT", "v", "qgT", "kgT", "kg", "vg", "rg",
          "afT", "abT", "tvec", "mixT", "y"]}

    dbg_n = [0]

    def dbg_dump(name, ap, shape, dt, reads):
        if not (cfg.debug and (cfg.debug is True or name in cfg.debug)):
            return
        dten = nc.dram_tensor("dbg_" + name, list(shape), dt, kind="ExternalOutput").ap()
        dbg_n[0] += 1
        P.dma("sp", lambda e: e.dma_start(out=dten, in_=ap), "dbg%d" % dbg_n[0], reads=reads, writes=[])

    cA = sb([128, 8, 128], F32)
    identF = cA[:, 0, :]
    Jm = cA[:, 1, :]
    identB = sb([128, 128], BF16)
    flags = sb([128, 2], F32)
    modT = sb([128, 4, 2, KC], F32)
    bT = sb([128, 4, KC], F32)
    ngT = sb([128, 2, KC], F32)
    scl = sb([128, 2, 2, KC], F32)
    small = sb([128, 64], F32)
    negh = small[:, 0:1]
    b_cA = P.buf("cA"); b_identB = P.buf("identB"); b_flags = P.buf("flags")
    b_modT = P.buf("modT"); b_bT = P.buf("bT"); b_ngT = P.buf("ngT"); b_scl = P.buf("scl")
    b_negh = P.buf("negh")
    P.scope_bufs = []
    MARK0 = ar.top

    P.dma("sp", lambda e: e.dma_start(out=cA, in_=cA_d), "cA", writes=[b_cA])
    P.dma("sp", lambda e: e.dma_start(out=flags, in_=flags_d), "flags", writes=[b_flags])
    P.op("dve", lambda e: e.tensor_copy(out=identB, in_=identF), reads=[b_cA], writes=[b_identB])
    P.op("dve", lambda e: e.memset(negh, -0.5), writes=[b_negh])

    def rstd_ops(ss_ap, b_ss, out_ap, b_out, n, tmp_ap, b_tmp):
        P.op("dve", lambda e: e.tensor_scalar(out=tmp_ap, in0=ss_ap, scalar1=1.0 / n, scalar2=EPS,
                                              op0=ALU.mult, op1=ALU.add), reads=[b_ss], writes=[b_tmp], tiny=True)
        P.op("pool", lambda e: e.tensor_tensor(out=out_ap, in0=tmp_ap, in1=negh, op=ALU.pow),
             reads=[b_tmp, b_negh], writes=[b_out], tiny=True)

    if True:
        NW = 256
        KH = max(KC // 2, 1)
        cT = sb([128, KC, 2], F32)
        cTb = sb([128, KC, 2], BF16)
        wslot = [sb([128, KC, NW], F32) for i in range(3)]
        wslotb = [sb([128, KC, NW], BF16) for i in range(3)]
        mst = [sb([2, NW], F32) for i in range(2)]
        gr = sb([2, D], F32)
        gb = sb([2, D], F32)
        pm = [ps([2, NW]) for i in range(2)]
        b_cT = P.buf("cT"); b_cTb = P.buf("cTb")
        b_ws = [P.buf("wa%d" % i) for i in range(3)]
        b_wsb = [[P.buf("wab%d_%d" % (i, hh)) for hh in range(2)] for i in range(3)]
        b_mst = [P.buf("mst%d" % i) for i in range(2)]
        b_pm = [P.buf("pm%d" % i) for i in range(2)]
        b_gr = P.buf("gr"); b_gb = P.buf("gb")
        for h in range(2):
            P.dma("sp", lambda e, h=h: e.dma_start(out=cT[:, :, h], in_=c2_d[h:h + 1, :].rearrange("o (k p) -> p (o k)", p=128),
                                                   allow_slow_non_contiguous=True), "cT", writes=[b_cT])
        P.op("act", lambda e: e.activation(out=cTb, in_=cT, func=AF.Silu), reads=[b_cT], writes=[b_cTb])
        wv = wada_d.rearrange("(k p) n -> p k n", p=128)
        NT = 6 * D // NW
        for n in range(NT):
            s = n % 3
            m2 = n % 2
            for hh in range(KC // KH):
                P.dma("sp", lambda e, s=s, n=n, hh=hh: e.dma_start(
                    out=wslot[s][:, hh * KH:(hh + 1) * KH, :], in_=wv[:, hh * KH:(hh + 1) * KH, n * NW:(n + 1) * NW]),
                    "wa%d" % s, writes=[b_ws[s]])
            for hh in range(KC // KH):
                if hh % 2 == 0:
                    P.op("act", lambda e, s=s, hh=hh: e.activation(out=wslotb[s][:, hh * KH:(hh + 1) * KH, :],
                                                                   in_=wslot[s][:, hh * KH:(hh + 1) * KH, :], func=AF.Copy),
                         reads=[b_ws[s]], writes=[b_wsb[s][hh % 2]])
                else:
                    P.op("dve", lambda e, s=s, hh=hh: e.tensor_copy(out=wslotb[s][:, hh * KH:(hh + 1) * KH, :],
                                                                    in_=wslot[s][:, hh * KH:(hh + 1) * KH, :]),
                         reads=[b_ws[s]], writes=[b_wsb[s][hh % 2]])
            for k in range(KC):
                P.op("pe", lambda e, s=s, k=k, m2=m2: e.matmul(pm[m2], cTb[:, k, :], wslotb[s][:, k, :],
                                                        start=(k == 0), stop=(k == KC - 1)),
                     reads=[b_cTb, b_wsb[s][(k // KH) % 2]], writes=[b_pm[m2]])
            P.op("act", lambda e, m2=m2: e.activation(out=mst[m2], in_=pm[m2], func=AF.Copy),
                 reads=[b_pm[m2]], writes=[b_mst[m2]])
            P.dma("pool", lambda e, m2=m2, n=n: e.dma_start(out=modraw_d[:, n * NW:(n + 1) * NW], in_=mst[m2]),
                  "mst%d" % m2, reads=[b_mst[m2]], writes=[B["modraw"]])
        for i, sec in enumerate((0, 1, 3, 4)):
            for h in range(2):
                P.dma("sp", lambda e, i=i, sec=sec, h=h: e.dma_start(
                    out=modT[:, i, h, :], in_=modraw_d[h:h + 1, sec * D:(sec + 1) * D].rearrange("o (k p) -> p (o k)", p=128),
                    allow_slow_non_contiguous=True), "modT", reads=[B["modraw"]], writes=[b_modT])
            P.dma("sp", lambda e, i=i, sec=sec: e.dma_start(
                out=bT[:, i, :], in_=bada_d[:, sec * D:(sec + 1) * D].rearrange("o (k p) -> p (o k)", p=128),
                allow_slow_non_contiguous=True), "bT", writes=[b_bT])
        P.dma("sp", lambda e: e.dma_start(out=ngT[:, 0, :], in_=n1g_d.rearrange("o (k p) -> p (o k)", p=128),
                                          allow_slow_non_contiguous=True), "ngT", writes=[b_ngT])
        P.dma("sp", lambda e: e.dma_start(out=ngT[:, 1, :], in_=n2g_d.rearrange("o (k p) -> p (o k)", p=128),
                                          allow_slow_non_contiguous=True), "ngT", writes=[b_ngT])
        for i in range(4):
            for h in range(2):
                P.op("dve", lambda e, i=i, h=h: e.tensor_tensor(out=modT[:, i, h, :], in0=modT[:, i, h, :],
                                                                in1=bT[:, i, :], op=ALU.add),
                     reads=[b_modT, b_bT], writes=[b_modT], tiny=True)
        for nn, si in ((0, 1), (1, 3)):
            for h in range(2):
                P.op("dve", lambda e, nn=nn, si=si, h=h: e.scalar_tensor_tensor(
                    out=scl[:, nn, h, :], in0=modT[:, si, h, :], scalar=1.0, in1=ngT[:, nn, :],
                    op0=ALU.add, op1=ALU.mult), reads=[b_modT, b_ngT], writes=[b_scl], tiny=True)
        for gi, sec in enumerate((2, 5)):
            P.dma("sp", lambda e, sec=sec: e.dma_start(out=gr, in_=modraw_d[:, sec * D:(sec + 1) * D]),
                  "gr", reads=[B["modraw"]], writes=[b_gr])
            P.dma("sp", lambda e, sec=sec: e.dma_start(out=gb, in_=bada_d[:, sec * D:(sec + 1) * D].partition_broadcast(2)),
                  "gb", writes=[b_gb])
            P.op("dve", lambda e: e.tensor_tensor(out=gr, in0=gr, in1=gb, op=ALU.add),
                 reads=[b_gr, b_gb], writes=[b_gr])
            P.dma("sp", lambda e, gi=gi: e.dma_start(out=gfull_d[gi], in_=gr), "gr", reads=[b_gr],
                  writes=[B["gfull"]])
        end_phase(MARK0)

    sh = {0: 0, 1: 2}

    CW = 1024 if D >= 1024 else D
    NSL = 4
    w32 = [sb([128, CW], F32) for i in range(NSL)]
    w16 = [sb([128, CW], BF16) for i in range(NSL)]
    b_w32 = [P.buf("w32_%d" % i) for i in range(NSL)]
    b_w16 = [P.buf("w16_%d" % i) for i in range(NSL)]
    cast_bufs = b_w32 + b_w16
    P.scope_bufs = []
    MARK1 = ar.top
    cnt = [0]
    cast_engs = ("act", "dve", "pool") if cfg.cast_pool else ("act", "dve")

    cast_mode = {"q": "sp", "engs": ("act", "dve")}
    cast_pending = [None]

    def cast_flush():
        if cast_pending[0] is not None:
            cast_pending[0]()
            cast_pending[0] = None

    def cast_piece(src_ap, in_view_fn, out_view_fn, dst_fn, dst_buf, load_view_fn=None):
        i = cnt[0] % NSL
        engs = cast_mode["engs"]
        eng = engs[cnt[0] % len(engs)]
        cnt[0] += 1
        lv = w32[i] if load_view_fn is None else load_view_fn(w32[i])
        P.dma(cast_mode["q"], lambda e: e.dma_start(out=lv, in_=src_ap), "w32_%d" % i, writes=[b_w32[i]])
        ov = out_view_fn(w16[i])
        iv = in_view_fn(w32[i])

        def second():
            if eng == "act":
                P.op("act", lambda e: e.activation(out=ov, in_=iv, func=AF.Copy), reads=[b_w32[i]], writes=[b_w16[i]])
            else:
                P.op(eng, lambda e: e.tensor_copy(out=ov, in_=iv), reads=[b_w32[i]], writes=[b_w16[i]])
            P.dma("pool", lambda e: dst_fn(e, w16[i]), "w16_%d" % i, reads=[b_w16[i]], writes=[dst_buf])
        prev = cast_pending[0]
        cast_pending[0] = second
        if prev is not None:
            prev()

    def cast_natural(src_d, dst_d, rows, cols, bname):
        for r0 in range(0, rows, 128):
            for c0 in range(0, cols, CW):
                cw = min(CW, cols - c0)
                cast_piece(src_d[r0:r0 + 128, c0:c0 + cw], lambda t, cw=cw: t[:, 0:cw], lambda t, cw=cw: t[:, 0:cw],
                           lambda e, t, r0=r0, c0=c0, cw=cw: e.dma_start(out=dst_d[r0:r0 + 128, c0:c0 + cw],
                                                                          in_=t[:, 0:cw]), B[bname],
                           load_view_fn=lambda t, cw=cw: t[:, 0:cw])
                yield

    def cast_w1():
        KK = min(4, KC)
        CC = CW // KK
        NF = CC // 128
        for k0 in range(0, KC, KK):
            for c0 in range(0, DFF, CC):
                f0 = c0 // 128
                cast_piece(w1_d[k0 * 128:(k0 + KK) * 128, c0:c0 + CC].rearrange("(k p) c -> p k c", p=128),
                           lambda t: t[:, 0:KK * CC].rearrange("p (k f j) -> p f k j", k=KK, f=NF),
                           lambda t: t[:, 0:KK * CC].rearrange("p (f k j) -> p f k j", f=NF, k=KK),
                           lambda e, t, k0=k0, f0=f0: e.dma_start(
                               out=w1t_d[f0:f0 + NF, :, k0:k0 + KK, :].rearrange("f p k j -> p f k j"),
                               in_=t[:, 0:KK * CC].rearrange("p (f k j) -> p f k j", f=NF, k=KK)), B["w1t"],
                           load_view_fn=lambda t: t[:, 0:KK * CC].rearrange("p (k c) -> p k c", k=KK))
                yield

    for _ in cast_natural(win_d, winb_d, D, INW, "winb"):
        pass
    cast_flush()

    def cast_rest():
        yield from cast_natural(wout_d, woutb_d, D, D, "woutb")
        yield from cast_w1()
        yield from cast_natural(w2_d, w2b_d, DFF, D, "w2b")

    cast_gen = cast_rest()
    n_cast_total = (D // 128) * ((D + CW - 1) // CW) + (KC // min(4, KC)) * (DFF // (CW // min(4, KC))) + (DFF // 128) * ((D + CW - 1) // CW)

    def cast_some(n):
        for _ in range(n):
            try:
                next(cast_gen)
            except StopIteration:
                cast_flush()
                return

    def norm_transpose_block(xt, b_xt, half, nn, hT, b_hT, tcol, pst, b_pst, junk, junk_bufs, ssv, b_ss, rsv, b_rs,
                             tmpv, b_tmp, undo, act_ok=True):
        P.op("act", lambda e: e.activation(out=junk, in_=xt, func=AF.Square, accum_out=ssv),
             reads=[b_xt], writes=list(junk_bufs) + [b_ss])
        rstd_ops(ssv, b_ss, rsv, b_rs, D, tmpv, b_tmp)
        P.op("dve", lambda e: e.tensor_scalar(out=xt, in0=xt, scalar1=rsv, scalar2=None, op0=ALU.mult),
             reads=[b_xt, b_rs], writes=[b_xt])
        G4 = 4
        for k0 in range(0, KC, G4):
            pi = (k0 // G4) % len(pst)
            for j in range(G4):
                k = k0 + j
                P.op("pe", lambda e, k=k, j=j, pi=pi: e.transpose(pst[pi][:, j * 128:(j + 1) * 128],
                                                                   xt[:, k * 128:(k + 1) * 128], identF),
                     reads=[b_xt, b_cA], writes=[b_pst[pi]])
            for j in range(G4):
                k = k0 + j
                if j % 2 == 0 and act_ok:
                    P.op("act", lambda e, k=k, j=j, pi=pi: e.activation(
                        out=hT[:, k, tcol:tcol + 128], in_=pst[pi][:, j * 128:(j + 1) * 128], func=AF.Identity,
                        bias=modT[:, sh[nn], half, k:k + 1], scale=scl[:, nn, half, k:k + 1]),
                        reads=[b_pst[pi], b_modT, b_scl], writes=[b_hT])
                else:
                    P.op("dve", lambda e, k=k, j=j, pi=pi: e.tensor_scalar(
                        out=hT[:, k, tcol:tcol + 128], in0=pst[pi][:, j * 128:(j + 1) * 128],
                        scalar1=scl[:, nn, half, k:k + 1], scalar2=modT[:, sh[nn], half, k:k + 1],
                        op0=ALU.mult, op1=ALU.add), reads=[b_pst[pi], b_modT, b_scl], writes=[b_hT])
        if undo:
            P.op("dve", lambda e: e.reciprocal(out=tmpv, in_=rsv), reads=[b_rs], writes=[b_tmp], tiny=True)
            P.op("pool", lambda e: e.tensor_scalar(out=xt, in0=xt, scalar1=tmpv, scalar2=1.0, op0=ALU.mult,
                                                   op1=ALU.mult),
                 reads=[b_xt, b_tmp], writes=[b_xt])

    TA = cfg.TA
    NBA = TA // 128
    if True:
        xin = [sb([128, D], F32) for i in range(2)]
        junk = sb([128, D], BF16)
        h1T = sb([128, KC, TA], BF16)
        NWS = 3
        wsl = [sb([128, KC, 512], BF16) for i in range(NWS)]
        NST = 4
        stg = [sb([128, 512], BF16) for i in range(NST)]
        stg32 = sb([32, 512], F32)
        pst = [ps([128, 512]) for i in range(2)]
        pacc = [ps([128, 512]) for i in range(4)]
        b_xin = [P.buf("xin%d" % i) for i in range(2)]
        b_junk = P.buf("junkA")
        b_h1T = P.buf("h1T")
        b_wsl = [P.buf("winS%d" % i) for i in range(NWS)]
        b_stg = [P.buf("stgA%d" % i) for i in range(NST)]
        b_stg32 = P.buf("stgA32")
        b_pst = [P.buf("pstA%d" % i) for i in range(2)]
        b_pacc = [P.buf("paccA%d" % i) for i in range(4)]
        b_ss = P.buf("ssA"); b_rs = P.buf("rsA"); b_tmp = P.buf("tmpA")
        ssv, rsv, tmpv = small[:, 1:2], small[:, 2:3], small[:, 3:4]

        groups = [("q", cfg.o_q, AW, qT_d, None), ("k", cfg.o_k, KVW, kT_d, None), ("v", cfg.o_v, KVW, None, v_d),
                  ("qg", cfg.o_qg, GKW, qgT_d, None), ("kg", cfg.o_kg, GKW, kgT_d, kg_d),
                  ("vg", cfg.o_vg, GVW, None, vg_d), ("rg", cfg.o_rg, GVW, None, rg_d),
                  ("ab", cfg.o_af, 32, "afab", None)]
        slabs = []
        for name, c0, ncols, fm, tm in groups:
            for s0 in range(0, ncols, 512):
                slabs.append((name, c0 + s0, min(512, ncols - s0), s0, fm, tm))
        winv = winb_d.rearrange("(k p) n -> p k n", p=128)
        cnt_w = [0]; cnt_s = [0]; cnt_p = [0]
        n_groups_A = (TOK // TA) * sum((((nc_ + 127) // 128) if fm_ is not None else 0) + (NBA if tm_ is not None else 0)
                                       for (_, _, nc_, _, fm_, tm_) in slabs)
        cast_rate_A = cfg.cast_frac_A * n_cast_total / n_groups_A
        cast_acc = [0.0]
        cast_mode["q"] = cfg.cast_q_A
        cast_mode["engs"] = ("act",)

        def cast_tick():
            cast_acc[0] += cast_rate_A
            while cast_acc[0] >= 1.0:
                cast_acc[0] -= 1.0
                cast_some(1)

        for t in range(TOK // TA):
            tok0 = t * TA
            half = tok0 // L
            for b in range(NBA):
                xi = (t * NBA + b) % 2
                P.dma("sp", lambda e, xi=xi, r0=tok0 + b * 128: e.dma_start(out=xin[xi], in_=x_d[r0:r0 + 128, :]),
                      "xin%d" % xi, writes=[b_xin[xi]])
                norm_transpose_block(xin[xi], b_xin[xi], half, 0, h1T, b_h1T, b * 128, pst, b_pst, junk, [b_junk],
                                     ssv, b_ss, rsv, b_rs, tmpv, b_tmp, undo=False, act_ok=False)
            for (name, col0, ncols, s0, fm, tm) in slabs:
                wi = cnt_w[0] % NWS
                cnt_w[0] += 1
                P.dma("sp", lambda e, wi=wi, col0=col0, ncols=ncols: e.dma_start(
                    out=wsl[wi][:, :, 0:ncols], in_=winv[:, :, col0:col0 + ncols]), "winS%d" % wi,
                    reads=[B["winb"]], writes=[b_wsl[wi]])
                if fm is not None:
                    for c in range(0, ncols, 128):
                        m = min(128, ncols - c)
                        pi = cnt_p[0] % 4
                        cnt_p[0] += 1
                        for k in range(KC):
                            P.op("pe", lambda e, pi=pi, wi=wi, k=k, c=c, m=m: e.matmul(
                                pacc[pi][0:m, 0:TA], wsl[wi][:, k, c:c + m], h1T[:, k, :],
                                start=(k == 0), stop=(k == KC - 1)),
                                reads=[b_wsl[wi], b_h1T], writes=[b_pacc[pi]])
                        if fm == "afab":
                            P.op("act", lambda e, pi=pi: e.activation(out=stg32[:, 0:TA], in_=pacc[pi][0:32, 0:TA],
                                                                      func=AF.Copy),
                                 reads=[b_pacc[pi]], writes=[b_stg32])
                            P.dma("pool", lambda e, tok0=tok0: e.dma_start(out=afT_d[:, tok0:tok0 + TA],
                                                                            in_=stg32[0:16, 0:TA]),
                                  "stgA32", reads=[b_stg32], writes=[B["afT"]])
                            P.dma("pool", lambda e, tok0=tok0: e.dma_start(out=abT_d[:, tok0:tok0 + TA],
                                                                            in_=stg32[16:32, 0:TA]),
                                  "stgA32", reads=[b_stg32], writes=[B["abT"]])
                        else:
                            si = cnt_s[0] % NST
                            cnt_s[0] += 1
                            P.op("dve", lambda e, pi=pi, si=si: e.tensor_copy(out=stg[si][:, 0:TA],
                                                                              in_=pacc[pi][:, 0:TA]),
                                 reads=[b_pacc[pi]], writes=[b_stg[si]])
                            r0 = s0 + c
                            P.dma("pool", lambda e, si=si, fm=fm, r0=r0, tok0=tok0: e.dma_start(
                                out=fm[r0:r0 + 128, tok0:tok0 + TA], in_=stg[si][:, 0:TA]), "stgA%d" % si,
                                reads=[b_stg[si]], writes=[B[name + "T"]])
                        cast_tick()
                if tm is not None:
                    for b in range(NBA):
                        pi = cnt_p[0] % 4
                        cnt_p[0] += 1
                        for k in range(KC):
                            P.op("pe", lambda e, pi=pi, wi=wi, k=k, b=b, ncols=ncols: e.matmul(
                                pacc[pi][:, 0:ncols], h1T[:, k, b * 128:(b + 1) * 128], wsl[wi][:, k, 0:ncols],
                                start=(k == 0), stop=(k == KC - 1)),
                                reads=[b_wsl[wi], b_h1T], writes=[b_pacc[pi]])
                        si = cnt_s[0] % NST
                        cnt_s[0] += 1
                        P.op("dve", lambda e, pi=pi, si=si, ncols=ncols: e.tensor_copy(
                            out=stg[si][:, 0:ncols], in_=pacc[pi][:, 0:ncols]),
                            reads=[b_pacc[pi]], writes=[b_stg[si]])
                        r0 = tok0 + b * 128
                        P.dma("pool", lambda e, si=si, tm=tm, r0=r0, s0=s0, ncols=ncols: e.dma_start(
                            out=tm[r0:r0 + 128, s0:s0 + ncols], in_=stg[si][:, 0:ncols]), "stgA%d" % si,
                            reads=[b_stg[si]], writes=[B[name]])
                        cast_tick()
        end_phase(MARK1)

    if cfg.stop_after == "A":
        P.emit()
        es.close()
        return nc

    if True:
        relb = sb([33, NHA], F32)
        cOH = sb([33, 512], F32)
        tv = sb([NHA, 512], F32)
        hank = [sb([128, 384], F32) for i in range(2)]
        bias = sb([128, NHA, 384], F32)
        sinkbc = sb([128, NHA], F32)
        kTs = [sb([128, TOK], BF16) for i in range(2)]
        vs = [sb([128, NBT, 128], BF16) for i in range(2)]
        qTs = [sb([128, TOK], BF16) for i in range(2)]
        GB = 4
        s_sb = [sb([128, 384], F32) for i in range(GB)]
        p_sb = [sb([128, 384], F32) for i in range(GB)]
        pn_sb = [[sb([128, 384], BF16) for i in range(GB)] for par in range(2)]
        pT_sb = [sb([128, 384], BF16) for i in range(GB)]
        ostg = [sb([128, 512], BF16) for i in range(2)]
        sm = sb([128, GB, 8], F32)
        ps_s2 = [ps([128, 384]) for i in range(2)]
        ps_s = [ps_s2[i % 2] for i in range(GB)]
        ps_t = [ps([128, 384], BF16) for i in range(GB)]
        ps_ob = [ps([128, 512]) for i in range(2)]
        ps_o = [ps_ob[i % 2][:, (i // 2) * 128:(i // 2 + 1) * 128] for i in range(GB)]
        ps_x = ps_ob[0]
        b_relb = P.buf("relb"); b_cOH = P.buf("cOH"); b_tv = P.buf("tv")
        b_hank = [P.buf("hank%d" % i) for i in range(2)]
        b_bias = P.buf("bias"); b_sink = P.buf("sinkbc")
        b_kTs = [P.buf("kTs%d" % i) for i in range(2)]
        b_vs = [P.buf("vs%d" % i) for i in range(2)]
        b_qTs = [P.buf("qTs%d" % i) for i in range(2)]
        b_s = [P.buf("s_sb%d" % i) for i in range(GB)]
        b_p = [P.buf("p_sb%d" % i) for i in range(GB)]
        b_pn = [[P.buf("pn%d_%d" % (par, i)) for i in range(GB)] for par in range(2)]
        b_pT = [P.buf("pT%d" % i) for i in range(GB)]
        b_ostg = [P.buf("ostg%d" % i) for i in range(2)]
        b_sm = [P.buf("smA%d" % i) for i in range(GB)]
        b_ps_s2 = [P.buf("ps_s%d" % i) for i in range(2)]
        b_ps_s = [b_ps_s2[i % 2] for i in range(GB)]
        b_ps_t = [P.buf("ps_t%d" % i) for i in range(GB)]
        b_ps_ob = [P.buf("ps_o%d" % i) for i in range(2)]
        b_ps_o = [b_ps_ob[i % 2] for i in range(GB)]
        b_ps_x = b_ps_ob[0]

        P.dma("sp", lambda e: e.dma_start(out=relb, in_=relb_d), "relb", writes=[b_relb])
        P.dma("sp", lambda e: e.dma_start(out=cOH, in_=cOH_d), "cOH", writes=[b_cOH])
        P.dma("sp", lambda e: e.dma_start(out=sinkbc, in_=sink_d.partition_broadcast(128)), "sinkbc",
              writes=[b_sink])
        P.op("pe", lambda e: e.matmul(ps_x[0:NHA, :], relb, cOH, start=True, stop=True),
             reads=[b_relb, b_cOH], writes=[b_ps_x])
        P.op("act", lambda e: e.activation(out=tv, in_=ps_x[0:NHA, :], func=AF.Copy), reads=[b_ps_x], writes=[b_tv])
        P.dma("sp", lambda e: e.dma_start(out=tvec_d, in_=tv), "tv", reads=[b_tv], writes=[B["tvec"]])
        for h in range(NHA):
            hi = h % 2
            P.dma("sp", lambda e, h=h, hi=hi: e.dma_start(
                out=hank[hi], in_=bass.AP(tvec_d.tensor, h * 512, [[1, 128], [1, 384]])), "hank%d" % hi,
                reads=[B["tvec"]], writes=[b_hank[hi]])
            P.op("pe", lambda e, hi=hi: e.matmul(ps_x[:, 0:384], Jm, hank[hi], start=True, stop=True),
                 reads=[b_cA, b_hank[hi]], writes=[b_ps_x])
            P.op("act", lambda e, h=h: e.activation(out=bias[:, h, :], in_=ps_x[:, 0:384], func=AF.Copy),
                 reads=[b_ps_x], writes=[b_bias])

        dbg_dump("bias", bias, [128, NHA, 384], F32, [b_bias])
        scale_a = 1.0 / math.sqrt(128.0)
        hcount = 0
        cast_mode["q"] = "sp"
        cast_mode["engs"] = ("dve",)
        n_b1_slots = NHA * (NBT // 4)
        cast_per_b1 = int(math.ceil((1.0 - cfg.cast_frac_A) * n_cast_total / n_b1_slots)) + 1
        for kvh in range(NKV):
            ki = kvh % 2
            P.dma("sp", lambda e, ki=ki, kvh=kvh: e.dma_start(out=kTs[ki], in_=kT_d[kvh * 128:(kvh + 1) * 128, :]),
                  "kTs%d" % ki, reads=[B["kT"]], writes=[b_kTs[ki]])
            P.dma("sp", lambda e, ki=ki, kvh=kvh: e.dma_start(
                out=vs[ki], in_=v_d[:, kvh * 128:(kvh + 1) * 128].rearrange("(n p) d -> p n d", p=128)),
                "vs%d" % ki, reads=[B["v"]], writes=[b_vs[ki]])
            for g in range(GRP):
                h = kvh * GRP + g
                qi = hcount % 2
                hcount += 1
                P.dma("sp", lambda e, qi=qi, h=h: e.dma_start(out=qTs[qi], in_=qT_d[h * 128:(h + 1) * 128, :]),
                      "qTs%d" % qi, reads=[B["qT"]], writes=[b_qTs[qi]])

                def win(gn):
                    jlo = 0 if gn > 0 else 1
                    jhi = 2 if gn < NBT - 1 else 1
                    return jlo, jhi

                def front_steps(gn, par, h=h, qi=qi, ki=ki):
                    r = gn % GB
                    jlo, jhi = win(gn)
                    W = (jhi - jlo + 1) * 128
                    k0 = (gn - 1 + jlo) * 128
                    cross = None
                    if gn == NB - 1:
                        cross = (2 - jlo) * 128
                    elif gn == NB:
                        cross = 0
                    st = []
                    def s2():
                        P.op("pe", lambda e: e.matmul(ps_s[r][:, 0:W], qTs[qi][:, gn * 128:(gn + 1) * 128],
                                                      kTs[ki][:, k0:k0 + W], start=True, stop=True),
                             reads=[b_qTs[qi], b_kTs[ki]], writes=[b_ps_s[r]])
                        P.op("dve", lambda e: e.scalar_tensor_tensor(
                            out=s_sb[r][:, 0:W], in0=ps_s[r][:, 0:W], scalar=scale_a,
                            in1=bias[:, h, jlo * 128:jlo * 128 + W], op0=ALU.mult, op1=ALU.add),
                            reads=[b_ps_s[r], b_bias], writes=[b_s[r]])
                        if cross is not None:
                            P.op("dve", lambda e: e.tensor_scalar(
                                out=s_sb[r][:, cross:cross + 128], in0=s_sb[r][:, cross:cross + 128],
                                scalar1=flags[:, 1:2], scalar2=None, op0=ALU.add),
                                reads=[b_s[r], b_flags], writes=[b_s[r]])
                    st.append(s2)
                    st.append(lambda: P.op("dve", lambda e: e.tensor_reduce(out=sm[:, r, 0:1], in_=s_sb[r][:, 0:W],
                                                                            axis=AX.X, op=ALU.max, negate=True),
                                           reads=[b_s[r]], writes=[b_sm[r]], tiny=True))
                    st.append(lambda: P.op("act", lambda e: e.activation(
                        out=p_sb[r][:, 0:W], in_=s_sb[r][:, 0:W], func=AF.Exp, bias=sm[:, r, 0:1], scale=1.0,
                        accum_out=sm[:, r, 1:2]), reads=[b_s[r], b_sm[r]], writes=[b_p[r], b_sm[r]]))
                    st.append(lambda: P.op("act", lambda e: e.activation(
                        out=sm[:, r, 2:3], in_=sm[:, r, 0:1], func=AF.Exp, bias=sinkbc[:, h:h + 1], scale=1.0),
                        reads=[b_sm[r], b_sink], writes=[b_sm[r]], tiny=True))
                    st.append(lambda: P.op("dve", lambda e: e.tensor_tensor(
                        out=sm[:, r, 3:4], in0=sm[:, r, 1:2], in1=sm[:, r, 2:3], op=ALU.add),
                        reads=[b_sm[r]], writes=[b_sm[r]], tiny=True))
                    st.append(lambda: P.op("dve", lambda e: e.reciprocal(out=sm[:, r, 4:5], in_=sm[:, r, 3:4]),
                                           reads=[b_sm[r]], writes=[b_sm[r]], tiny=True))
                    st.append(lambda: P.op("act", lambda e: e.activation(
                        out=pn_sb[par][r][:, 0:W], in_=p_sb[r][:, 0:W], func=AF.Copy, scale=sm[:, r, 4:5]),
                        reads=[b_p[r], b_sm[r]], writes=[b_pn[par][r]]))
                    return st

                def back_steps(gn, par, h=h, qi=qi, ki=ki):
                    r = gn % GB
                    jlo, jhi = win(gn)
                    nj = jhi - jlo + 1
                    W = nj * 128
                    st = []

                    def t1():
                        for j in range(nj):
                            P.op("pe", lambda e, j=j: e.transpose(ps_t[r][:, j * 128:(j + 1) * 128],
                                                                  pn_sb[par][r][:, j * 128:(j + 1) * 128], identB),
                                 reads=[b_pn[par][r], b_identB], writes=[b_ps_t[r]])
                    st.append(t1)
                    st.append(lambda: P.op("act", lambda e: e.activation(out=pT_sb[r][:, 0:W], in_=ps_t[r][:, 0:W],
                                                                         func=AF.Copy),
                                           reads=[b_ps_t[r]], writes=[b_pT[r]]))

                    def t3():
                        for j in range(nj):
                            kb = gn - 1 + jlo + j
                            P.op("pe", lambda e, j=j, kb=kb: e.matmul(ps_o[r], vs[ki][:, kb, :],
                                                                       pT_sb[r][:, j * 128:(j + 1) * 128],
                                                                       start=(j == 0), stop=(j == nj - 1)),
                                 reads=[b_vs[ki], b_pT[r]], writes=[b_ps_o[r]])
                    st.append(t3)

                    def t4():
                        oi = (gn // 4) % 2
                        c = (gn % 4) * 128
                        P.op("dve", lambda e: e.tensor_copy(out=ostg[oi][:, c:c + 128], in_=ps_o[r]),
                             reads=[b_ps_o[r]], writes=[b_ostg[oi]])
                        if gn % 4 == 3:
                            c0 = (gn - 3) * 128
                            P.dma("pool", lambda e: e.dma_start(out=mixT_d[h * 128:(h + 1) * 128, c0:c0 + 512],
                                                                in_=ostg[oi]), "ostg%d" % oi,
                                  reads=[b_ostg[oi]], writes=[B["mixT"]])
                    st.append(t4)
                    return st

                def emit_interleaved(lists):
                    for k in range(len(lists[0])):
                        for l in lists:
                            l[k]()

                def emit_zip(fronts, backs):
                    nf = len(fronts[0]) if fronts else 0
                    nb_ = len(backs[0]) if backs else 0
                    for k in range(max(nf, nb_)):
                        if k < nf:
                            for l in fronts:
                                l[k]()
                        if k < nb_:
                            for l in backs:
                                l[k]()

                NG = NBT // GB
                emit_interleaved([front_steps(gn, 0) for gn in range(0, GB)])
                for g in range(NG):
                    fr = ([front_steps(gn, (g + 1) % 2) for gn in range((g + 1) * GB, (g + 2) * GB)]
                          if g + 1 < NG else [])
                    bk = [back_steps(gn, g % 2) for gn in range(g * GB, (g + 1) * GB)]
                    emit_zip(fr, bk)
                    cast_some(cast_per_b1)
        cast_some(10 ** 9)
        P.scope_bufs.extend(cast_bufs)
        end_phase(MARK0)

    if cfg.stop_after == "B1":
        P.emit()
        es.close()
        return nc

    if True:
        afT1 = [sb([17, TOK], F32) for i in range(2)]
        wab = sb([17, 2, GKW], F32)
        gngbc = sb([128, 256], F32)
        qgTs = sb([128, TOK], BF16)
        kgTs = sb([128, TOK], BF16)
        kgs = sb([128, NBT, 128], BF16)
        vgs = sb([128, NBT, 256], BF16)
        srg = sb([128, NBT, 256], BF16)
        lsb = [sb([128, NBT, 128], F32) for i in range(2)]
        ost = sb([128, NBT, 256], F32)
        etmp = sb([128, 512], F32)
        E1 = [[sb([128, 128], F32) for r in range(2)] for d in range(2)]
        E2 = [[sb([128, 128], F32) for r in range(2)] for d in range(2)]
        E3 = [[sb([128, 128], F32) for r in range(2)] for d in range(2)]
        QtT = [[sb([128, 128], BF16) for r in range(2)] for d in range(2)]
        KtT = [[sb([128, 128], BF16) for r in range(2)] for d in range(2)]
        Kd = [[sb([128, 128], BF16) for r in range(2)] for d in range(2)]
        ATs = [[sb([128, 128], BF16) for r in range(2)] for d in range(2)]
        Sst = [sb([128, 256], F32) for d in range(2)]
        Sbf = [[sb([128, 256], BF16) for r in range(2)] for d in range(2)]
        t1 = [sb([128, 256], F32) for d in range(2)]
        ysb = [sb([128, 256], BF16) for d in range(2)]
        junkg = sb([128, 256], BF16)
        ystg = [[sb([128, 2, 512], BF16) for r in range(2)] for d in range(2)]
        smg = sb([128, 2, 8], F32)
        ps_zy = ps([128, 512])
        ps_z = ps_zy
        ps_y = ps_zy.bitcast(BF16)[:, 0:256]
        ps_bd = [ps([128, 256]) for d in range(2)]
        ps_a2 = [ps([128, 128]) for d in range(2)]
        ps_og = [ps([128, 256]) for d in range(2)]
        ps_u = ps([128, 256])
        b_afT1 = [P.buf("afT1_%d" % i) for i in range(2)]
        b_wab = P.buf("wab"); b_gng = P.buf("gngbc")
        b_qgTs = P.buf("qgTs"); b_kgTs = P.buf("kgTs"); b_kgs = P.buf("kgs"); b_vgs = P.buf("vgs")
        b_srg = P.buf("srg")
        b_l = [P.buf("lsb%d" % i) for i in range(2)]
        b_ost = [P.buf("ost%d" % n) for n in range(NBT)]
        b_etmp = P.buf("etmp")
        mk = lambda nm: [[P.buf("%s%d%d" % (nm, d, r)) for r in range(2)] for d in range(2)]
        b_E1 = mk("E1"); b_E2 = mk("E2"); b_E3 = mk("E3"); b_Qt = mk("Qt"); b_Kt = mk("Kt"); b_Kd = mk("Kd")
        b_AT = mk("AT"); b_Sbf = mk("Sbf"); b_ystg = mk("ystg")
        b_S = [P.buf("S%d" % d) for d in range(2)]
        b_t1 = [P.buf("t1_%d" % d) for d in range(2)]; b_ysb = [P.buf("ysb%d" % d) for d in range(2)]
        b_junkg = P.buf("junkg"); b_smg = [P.buf("smg%d" % d) for d in range(2)]
        b_ps_z = P.buf("ps_zy"); b_ps_bd = [P.buf("ps_bd%d" % d) for d in range(2)]
        b_ps_a2 = [P.buf("ps_a%d" % d) for d in range(2)]; b_ps_og = [P.buf("ps_og%d" % d) for d in range(2)]
        b_ps_u = P.buf("ps_u"); b_ps_y = b_ps_z

        for i, (src, bn) in enumerate(((afT_d, "afT"), (abT_d, "abT"))):
            P.op("pool", lambda e, i=i: e.memset(afT1[i], 1.0), writes=[b_afT1[i]])
            P.dma("sp", lambda e, i=i, src=src: e.dma_start(out=afT1[i][0:16, :], in_=src), "afT1_%d" % i,
                  reads=[B[bn]], writes=[b_afT1[i]])
        P.dma("sp", lambda e: e.dma_start(out=wab[:, 0, :], in_=wabf_d), "wab", writes=[b_wab])
        P.dma("sp", lambda e: e.dma_start(out=wab[:, 1, :], in_=wabb_d), "wab", writes=[b_wab])
        P.dma("sp", lambda e: e.dma_start(out=gngbc, in_=gng_d.partition_broadcast(128)), "gngbc", writes=[b_gng])
        scale_g = 1.0 / math.sqrt(128.0)
        T1 = [cA[:, 2, :], cA[:, 4, :]]
        T2 = [cA[:, 3, :], cA[:, 5, :]]
        MK = [cA[:, 6, :], cA[:, 7, :]]

        for h in range(NHG):
            P.dma("sp", lambda e, h=h: e.dma_start(out=qgTs, in_=qgT_d[h * 128:(h + 1) * 128, :]), "qgTs",
                  reads=[B["qgT"]], writes=[b_qgTs])
            P.dma("sp", lambda e, h=h: e.dma_start(out=kgTs, in_=kgT_d[h * 128:(h + 1) * 128, :]), "kgTs",
                  reads=[B["kgT"]], writes=[b_kgTs])
            P.dma("sp", lambda e, h=h: e.dma_start(
                out=kgs, in_=kg_d[:, h * 128:(h + 1) * 128].rearrange("(n p) d -> p n d", p=128)), "kgs",
                reads=[B["kg"]], writes=[b_kgs])
            P.dma("sp", lambda e, h=h: e.dma_start(
                out=vgs, in_=vg_d[:, h * 256:(h + 1) * 256].rearrange("(n p) d -> p n d", p=128)), "vgs",
                reads=[B["vg"]], writes=[b_vgs])
            P.dma("sp", lambda e, h=h: e.dma_start(
                out=srg, in_=rg_d[:, h * 256:(h + 1) * 256].rearrange("(n p) d -> p n d", p=128)), "srg",
                reads=[B["rg"]], writes=[b_srg])
            for d in range(2):
                for g4 in range(0, NBT, 4):
                    for j in range(4):
                        gn = g4 + j
                        P.op("pe", lambda e, d=d, gn=gn, j=j, h=h: e.matmul(
                            ps_z[:, j * 128:(j + 1) * 128], afT1[d][:, gn * 128:(gn + 1) * 128],
                            wab[:, d, h * 128:(h + 1) * 128], start=True, stop=True),
                            reads=[b_afT1[d], b_wab], writes=[b_ps_z])
                    P.op("act", lambda e: e.activation(out=etmp, in_=ps_z, func=AF.Exp, scale=-1.0),
                         reads=[b_ps_z], writes=[b_etmp])
                    P.op("act", lambda e, d=d, g4=g4: e.activation(
                        out=lsb[d][:, g4:g4 + 4, :], in_=etmp.rearrange("p (a b) -> p a b", a=4), func=AF.Ln,
                        bias=1.0, scale=1.0), reads=[b_etmp], writes=[b_l[d]])
            P.op("act", lambda e: e.activation(out=srg, in_=srg, func=AF.Silu), reads=[b_srg], writes=[b_srg])

            def prep(d, gn, r):
                P.op("pe", lambda e: e.matmul(ps_bd[d][:, 0:128], lsb[d][:, gn, :], T1[d], start=True, stop=True),
                     reads=[b_l[d], b_cA], writes=[b_ps_bd[d]])
                P.op("pe", lambda e: e.matmul(ps_bd[d][:, 128:256], T2[d], lsb[d][:, gn, :], start=True, stop=True),
                     reads=[b_l[d], b_cA], writes=[b_ps_bd[d]])
                P.op("act", lambda e: e.activation(out=E1[d][r], in_=ps_bd[d][:, 0:128], func=AF.Exp),
                     reads=[b_ps_bd[d]], writes=[b_E1[d][r]])
                P.op("act", lambda e: e.activation(out=E2[d][r], in_=ps_bd[d][:, 0:128], func=AF.Exp, scale=-1.0),
                     reads=[b_ps_bd[d]], writes=[b_E2[d][r]])
                P.op("act", lambda e: e.activation(out=E3[d][r], in_=ps_bd[d][:, 128:256], func=AF.Exp),
                     reads=[b_ps_bd[d]], writes=[b_E3[d][r]])
                P.op("dve", lambda e: e.scalar_tensor_tensor(
                    out=QtT[d][r], in0=qgTs[:, gn * 128:(gn + 1) * 128], scalar=scale_g, in1=E1[d][r],
                    op0=ALU.mult, op1=ALU.mult), reads=[b_qgTs, b_E1[d][r]], writes=[b_Qt[d][r]])
                P.op("pool", lambda e: e.tensor_tensor(out=KtT[d][r], in0=kgTs[:, gn * 128:(gn + 1) * 128],
                                                       in1=E2[d][r], op=ALU.mult),
                     reads=[b_kgTs, b_E2[d][r]], writes=[b_Kt[d][r]])
                P.op("pool", lambda e: e.tensor_tensor(out=Kd[d][r], in0=kgs[:, gn, :], in1=E3[d][r], op=ALU.mult),
                     reads=[b_kgs, b_E3[d][r]], writes=[b_Kd[d][r]])

            def post_front(d, gn):
                P.op("act", lambda e: e.activation(out=junkg, in_=ost[:, gn, :], func=AF.Square,
                                                   accum_out=smg[:, d, 0:1]),
                     reads=[b_ost[gn]], writes=[b_junkg, b_smg[d]])
                rstd_ops(smg[:, d, 0:1], b_smg[d], smg[:, d, 1:2], b_smg[d], 256, smg[:, d, 2:3], b_smg[d])
                P.op("dve", lambda e: e.scalar_tensor_tensor(out=t1[d], in0=ost[:, gn, :], scalar=smg[:, d, 1:2],
                                                             in1=gngbc, op0=ALU.mult, op1=ALU.mult),
                     reads=[b_ost[gn], b_smg[d], b_gng], writes=[b_t1[d]])
                P.op("pool", lambda e: e.tensor_tensor(out=ysb[d], in0=t1[d], in1=srg[:, gn, :], op=ALU.mult),
                     reads=[b_t1[d], b_srg], writes=[b_ysb[d]])

            def post_back(d, gn, h=h):
                for c in range(2):
                    P.op("pe", lambda e, c=c: e.transpose(ps_y[:, c * 128:(c + 1) * 128],
                                                          ysb[d][:, c * 128:(c + 1) * 128], identB),
                         reads=[b_ysb[d], b_identB], writes=[b_ps_y])
                yi = (gn // 4) % 2
                col = (gn % 4) * 128
                P.op("act", lambda e: e.activation(out=ystg[d][yi][:, :, col:col + 128],
                                                   in_=ps_y.rearrange("p (c t) -> p c t", c=2), func=AF.Copy),
                     reads=[b_ps_y], writes=[b_ystg[d][yi]])
                done = (gn % 4 == 3) if d == 0 else (gn % 4 == 0)
                if done:
                    c0 = (gn // 4) * 4 * 128
                    r0 = AW + h * 256
                    P.dma("pool", lambda e: e.dma_start(
                        out=mixT_d[r0:r0 + 256, c0:c0 + 512].rearrange("(c p) t -> p c t", p=128),
                        in_=ystg[d][yi]), "ystg%d%d" % (d, yi), reads=[b_ystg[d][yi]], writes=[B["mixT"]])

            def mainA(d, gn, r):
                P.op("pe", lambda e: e.matmul(ps_a2[d], KtT[d][r], QtT[d][r], start=True, stop=True),
                     reads=[b_Kt[d][r], b_Qt[d][r]], writes=[b_ps_a2[d]])
                P.op("dve", lambda e: e.tensor_tensor(out=ATs[d][r], in0=ps_a2[d], in1=MK[d], op=ALU.mult),
                     reads=[b_ps_a2[d], b_cA], writes=[b_AT[d][r]])

            def mainB(d, gn, r, first, step, pend):
                P.op("pe", lambda e: e.matmul(ps_og[d], ATs[d][r], vgs[:, gn, :], start=True, stop=first),
                     reads=[b_AT[d][r], b_vgs], writes=[b_ps_og[d]])
                if not first:
                    pr = (step - 1) % 2
                    P.op("pe", lambda e: e.matmul(ps_og[d], QtT[d][r], Sbf[d][pr], start=False, stop=True),
                         reads=[b_Qt[d][r], b_Sbf[d][pr]], writes=[b_ps_og[d]])
                P.op("pe", lambda e: e.matmul(ps_u, Kd[d][r], vgs[:, gn, :], start=True, stop=True),
                     reads=[b_Kd[d][r], b_vgs], writes=[b_ps_u])
                dec = E1[d][r][:, 127:128] if d == 0 else E1[d][r][:, 0:1]
                if first:
                    P.op("dve", lambda e: e.tensor_copy(out=Sst[d], in_=ps_u), reads=[b_ps_u], writes=[b_S[d]])
                else:
                    P.op("dve", lambda e: e.scalar_tensor_tensor(out=Sst[d], in0=Sst[d], scalar=dec, in1=ps_u,
                                                                 op0=ALU.mult, op1=ALU.add),
                         reads=[b_S[d], b_E1[d][r], b_ps_u], writes=[b_S[d]])
                boundary = (gn == NB - 1) if d == 0 else (gn == NB)
                if boundary:
                    P.op("dve", lambda e: e.tensor_scalar(out=Sst[d], in0=Sst[d], scalar1=flags[:, 0:1], scalar2=None,
                                                          op0=ALU.mult), reads=[b_S[d], b_flags], writes=[b_S[d]])
                sr = step % 2
                P.op("pool", lambda e: e.tensor_copy(out=Sbf[d][sr], in_=Sst[d]), reads=[b_S[d]],
                     writes=[b_Sbf[d][sr]])
                first_visit = (gn < NB) if d == 0 else (gn >= NB)
                if first_visit:
                    P.op("act", lambda e: e.activation(out=ost[:, gn, :], in_=ps_og[d], func=AF.Copy),
                         reads=[b_ps_og[d]], writes=[b_ost[gn]])
                else:
                    P.op("dve", lambda e: e.tensor_tensor(out=ost[:, gn, :], in0=ps_og[d], in1=ost[:, gn, :],
                                                          op=ALU.add), reads=[b_ps_og[d], b_ost[gn]],
                         writes=[b_ost[gn]])
                    pend.append((d, gn))

            blk = lambda d, i: i if d == 0 else NBT - 1 - i
            for d in range(2):
                prep(d, blk(d, 0), 0)
            pend = []
            for i in range(NBT):
                cur = pend
                pend = []
                for (pd, pg) in cur:
                    post_front(pd, pg)
                for d in range(2):
                    if i + 1 < NBT:
                        prep(d, blk(d, i + 1), (i + 1) % 2)
                for d in range(2):
                    mainA(d, blk(d, i), i % 2)
                for d in range(2):
                    mainB(d, blk(d, i), i % 2, i == 0, i, pend)
                for (pd, pg) in cur:
                    post_back(pd, pg)
            for (pd, pg) in pend:
                post_front(pd, pg)
            for (pd, pg) in pend:
                post_back(pd, pg)
        end_phase(MARK0)

    if cfg.stop_after == "B2":
        P.emit()
        es.close()
        return nc

    TC = cfg.TC
    NBC = TC // 128
    HFC = min(16, FC)
    FSPLIT = FC // HFC
    KG = min(8, KC)
    if True:
        x2 = [sb([128, D], F32) for b in range(NBC)]
        h2T = sb([128, KC, TC], BF16)
        RX = max(KC * TC, HFC * TC + 2 * KC * 128, D)
        regX = sb([128, RX], BF16)
        mixTs = regX[:, 0:KC * TC].rearrange("p (k t) -> p k t", k=KC)
        hid = regX[:, 0:HFC * TC].rearrange("p (f t) -> p f t", f=HFC)
        junkC = regX[:, 0:D]
        w1s = [regX[:, HFC * TC + i * KC * 128: HFC * TC + (i + 1) * KC * 128].rearrange("p (k j) -> p k j", k=KC)
               for i in range(2)]
        w1s.append(sb([128, KC, 128], BF16))
        NW2 = 4
        w2s = [sb([128, KG, 512], BF16) for i in range(NW2)]
        gs = [sb([128, 512], F32) for i in range(2)]
        tmpc = [sb([128, 512], F32) for i in range(2)]
        r32 = [sb([128, TC], F32) for i in range(2)]
        fgbc = sb([128, D], F32)
        py = [ps([128, 512]) for b in range(NBC)]
        ph = [ps([128, TC]) for i in range(2)]
        pstc = [ps([128, 512]) for i in range(8 - NBC - 2)]
        b_x2 = [P.buf("x2_%d" % b) for b in range(NBC)]
        b_h2T = P.buf("h2T")
        b_mixTs = P.buf("mixTs")
        b_hid = [P.buf("hid%d" % f) for f in range(HFC)]
        b_w1s = [P.buf("w1s%d" % i) for i in range(3)]
        b_w2s = [P.buf("w2s%d" % i) for i in range(NW2)]
        b_gs = [P.buf("gs%d" % i) for i in range(2)]
        b_tmpc = [P.buf("tmpc%d" % i) for i in range(2)]
        b_r32 = [P.buf("r32_%d" % i) for i in range(2)]
        b_fgbc = P.buf("fgbc")
        b_py = [P.buf("py%d" % b) for b in range(NBC)]
        b_ph = [P.buf("ph%d" % i) for i in range(2)]
        b_pstc = [P.buf("pstc%d" % i) for i in range(len(pstc))]
        b_ss = P.buf("ssC"); b_rs = P.buf("rsC"); b_tmp = P.buf("tmpC")
        ssv, rsv, tmpv = small[:, 1:2], small[:, 2:3], small[:, 3:4]
        P.dma("sp", lambda e: e.dma_start(out=fgbc, in_=fg_d.partition_broadcast(128)), "fgbc", writes=[b_fgbc])
        c_w1 = [0]; c_w2 = [0]; c_gs = [0]; c_tmp = [0]; c_ph = [0]; c_r = [0]
        mixv = mixT_d.rearrange("(k p) t -> p k t", p=128)

        def evac_add(b, j, gidx, half):
            gi = c_gs[0] % 2
            c_gs[0] += 1
            P.dma("pool", lambda e: e.dma_start(
                out=gs[gi], in_=gfull_d[gidx, half:half + 1, j * 512:(j + 1) * 512].partition_broadcast(128)),
                "gs%d" % gi, reads=[B["gfull"]], writes=[b_gs[gi]])
            return gi

        for t in range(TOK // TC):
            tok0 = t * TC
            half = tok0 // L
            for b in range(NBC):
                P.dma("sp", lambda e, b=b, r0=tok0 + b * 128: e.dma_start(out=x2[b], in_=x_d[r0:r0 + 128, :]),
                      "x2_%d" % b, writes=[b_x2[b]])
            P.dma("sp", lambda e, tok0=tok0: e.dma_start(out=mixTs, in_=mixv[:, :, tok0:tok0 + TC]), "mixTs",
                  reads=[B["mixT"]], writes=[b_mixTs] + b_hid + b_w1s[0:2])
            for j in range(D // 512):
                gi = evac_add(None, j, 0, half)
                for kg in range(KC // KG):
                    wi = c_w2[0] % NW2
                    c_w2[0] += 1
                    P.dma("sp", lambda e, wi=wi, kg=kg, j=j: e.dma_start(
                        out=w2s[wi], in_=woutb_d[kg * KG * 128:(kg + 1) * KG * 128, j * 512:(j + 1) * 512].rearrange(
                            "(f p) n -> p f n", p=128)), "w2s%d" % wi, reads=[B["woutb"]], writes=[b_w2s[wi]])
                    for b in range(NBC):
                        for f in range(KG):
                            k = kg * KG + f
                            P.op("pe", lambda e, b=b, f=f, k=k, wi=wi: e.matmul(
                                py[b], mixTs[:, k, b * 128:(b + 1) * 128], w2s[wi][:, f, :],
                                start=(k == 0), stop=(k == KC - 1)),
                                reads=[b_mixTs, b_w2s[wi]], writes=[b_py[b]])
                for b in range(NBC):
                    ti = c_tmp[0] % 2
                    c_tmp[0] += 1
                    P.op("dve", lambda e, b=b, ti=ti, gi=gi: e.tensor_tensor(out=tmpc[ti], in0=py[b], in1=gs[gi],
                                                                             op=ALU.mult),
                         reads=[b_py[b], b_gs[gi]], writes=[b_tmpc[ti]])
                    P.op("pool", lambda e, b=b, ti=ti, j=j: e.tensor_tensor(
                        out=x2[b][:, j * 512:(j + 1) * 512], in0=x2[b][:, j * 512:(j + 1) * 512], in1=tmpc[ti],
                        op=ALU.add), reads=[b_x2[b], b_tmpc[ti]], writes=[b_x2[b]])
            for b in range(NBC):
                norm_transpose_block(x2[b], b_x2[b], half, 1, h2T, b_h2T, b * 128, pstc, b_pstc, junkC,
                                     [b_mixTs], ssv, b_ss, rsv, b_rs, tmpv, b_tmp, undo=True)
            for fh in range(FSPLIT):
                for fi in range(HFC):
                    fc = fh * HFC + fi
                    wi = c_w1[0] % 3
                    c_w1[0] += 1
                    extra = [b_mixTs] if (fh == 0 and fi < 3 and wi < 2) else []
                    P.dma("sp", lambda e, wi=wi, fc=fc: e.dma_start(out=w1s[wi], in_=w1t_d[fc]), "w1s%d" % wi,
                          reads=[B["w1t"]], writes=[b_w1s[wi]] + extra)
                    pi = c_ph[0] % 2
                    c_ph[0] += 1
                    for k in range(KC):
                        P.op("pe", lambda e, wi=wi, k=k, pi=pi: e.matmul(ph[pi], w1s[wi][:, k, :], h2T[:, k, :],
                                                                         start=(k == 0), stop=(k == KC - 1)),
                             reads=[b_w1s[wi], b_h2T], writes=[b_ph[pi]])
                    ri = c_r[0] % 2
                    c_r[0] += 1
                    P.op("act", lambda e, pi=pi, ri=ri: e.activation(out=r32[ri], in_=ph[pi], func=AF.Relu),
                         reads=[b_ph[pi]], writes=[b_r32[ri]])
                    sq_eng = "dve" if fi % 2 == 0 else "pool"
                    P.op(sq_eng, lambda e, fi=fi, ri=ri: e.tensor_tensor(out=hid[:, fi, :], in0=r32[ri], in1=r32[ri],
                                                                         op=ALU.mult),
                         reads=[b_r32[ri]], writes=[b_hid[fi]])
                for j in range(D // 512):
                    gi = evac_add(None, j, 1, half)
                    for g in range(HFC // KG):
                        wi = c_w2[0] % NW2
                        c_w2[0] += 1
                        f0 = fh * HFC + g * KG
                        P.dma("sp", lambda e, wi=wi, f0=f0, j=j: e.dma_start(
                            out=w2s[wi], in_=w2b_d[f0 * 128:(f0 + KG) * 128, j * 512:(j + 1) * 512].rearrange(
                                "(f p) n -> p f n", p=128)), "w2s%d" % wi, reads=[B["w2b"]], writes=[b_w2s[wi]])
                        for b in range(NBC):
                            for f in range(KG):
                                fi = g * KG + f
                                P.op("pe", lambda e, b=b, f=f, fi=fi, wi=wi: e.matmul(
                                    py[b], hid[:, fi, b * 128:(b + 1) * 128], w2s[wi][:, f, :],
                                    start=(fi == 0), stop=(fi == HFC - 1)),
                                    reads=[b_hid[fi], b_w2s[wi]], writes=[b_py[b]])
                    for b in range(NBC):
                        ti = c_tmp[0] % 2
                        c_tmp[0] += 1
                        P.op("dve", lambda e, b=b, ti=ti, gi=gi: e.tensor_tensor(out=tmpc[ti], in0=py[b], in1=gs[gi],
                                                                                 op=ALU.mult),
                             reads=[b_py[b], b_gs[gi]], writes=[b_tmpc[ti]])
                        P.op("pool", lambda e, b=b, ti=ti, j=j: e.tensor_tensor(
                            out=x2[b][:, j * 512:(j + 1) * 512], in0=x2[b][:, j * 512:(j + 1) * 512], in1=tmpc[ti],
                            op=ALU.add), reads=[b_x2[b], b_tmpc[ti]], writes=[b_x2[b]])
            for b in range(NBC):
                P.op("act", lambda e, b=b: e.activation(out=junkC, in_=x2[b], func=AF.Square, accum_out=ssv),
                     reads=[b_x2[b]], writes=[b_mixTs, b_ss])
                rstd_ops(ssv, b_ss, rsv, b_rs, D, tmpv, b_tmp)
                P.op("dve", lambda e, b=b: e.scalar_tensor_tensor(out=x2[b], in0=x2[b], scalar=rsv, in1=fgbc,
                                                                  op0=ALU.mult, op1=ALU.mult),
                     reads=[b_x2[b], b_rs, b_fgbc], writes=[b_x2[b]])
                P.dma("pool", lambda e, b=b, r0=tok0 + b * 128: e.dma_start(out=y_d[r0:r0 + 128, :], in_=x2[b]),
                      "x2_%d" % b, reads=[b_x2[b]], writes=[B["y"]])
        end_phase(MARK0)

    P.emit()
    es.close()
    return nc


def shard_inputs(cfg, inp):
    D, L = cfg.D, cfg.L
    cA, cOH = host_consts()
    f = lambda a: np.ascontiguousarray(np.asarray(a, dtype=np.float32))
    xp, xs = f(inp["x_prompt"]), f(inp["x_sample"])
    cp, cs = f(inp["c_prompt"]), f(inp["c_sample"])
    common = {
        "w_ada": f(inp["w_ada"][0]), "b_ada": f(inp["b_ada"][0]).reshape(1, -1),
        "n1g": f(inp["norm1_g"][0]).reshape(1, -1), "w_in": f(inp["w_in"][0]),
        "wab_f": f(np.concatenate([inp["gla_wa_fwd"][0], inp["gla_ba_fwd"][0][None, :]], axis=0)),
        "wab_b": f(np.concatenate([inp["gla_wa_bwd"][0], inp["gla_ba_bwd"][0][None, :]], axis=0)),
        "gng": f(inp["gla_norm_g"][0]).reshape(1, -1), "sink": f(inp["attn_sink"][0]).reshape(1, -1),
        "relb": f(np.concatenate([inp["rel_bias"], np.ones((1, inp["rel_bias"].shape[1]), np.float32)], axis=0)),
        "w_out": f(inp["w_out"][0]), "n2g": f(inp["norm2_g"][0]).reshape(1, -1),
        "w1": f(inp["w_mlp_in"][0]), "w2": f(inp["w_mlp_out"][0]), "fg": f(inp["final_g"]).reshape(1, -1),
        "cA": cA, "cOH": cOH,
    }
    maps = []
    npc = xp.shape[0] // 2
    for c in range(8):
        m = dict(common)
        fl = np.zeros((128, 2), np.float32)
        if c < npc:
            m["x"] = np.ascontiguousarray(xp[2 * c:2 * c + 2].reshape(2 * L, D))
            m["c2"] = np.ascontiguousarray(cp[2 * c:2 * c + 2])
            fl[:, 0] = 0.0
            fl[:, 1] = NEG
        else:
            s = c - npc
            m["x"] = np.ascontiguousarray(xs[s].reshape(2 * L, D))
            m["c2"] = np.ascontiguousarray(np.stack([cs[s], cs[s]], axis=0))
            fl[:, 0] = 1.0
            fl[:, 1] = 0.0
        m["flags"] = fl
        maps.append(m)
    return maps


_CACHE = {}


def kernel(**inputs):
    cfg = Cfg()
    nc = build_program(cfg)
    maps = shard_inputs(cfg, inputs)
    res = run_bass_kernel_spmd(nc, maps, core_ids=list(range(8)))
    D, L = cfg.D, cfg.L
    npc = inputs["x_prompt"].shape[0] // 2
    yp = np.stack([res.results[c]["y"].reshape(2, L, D) for c in range(npc)], axis=0).reshape(-1, L, D)
    ys = np.stack([res.results[c]["y"].reshape(2 * L, D) for c in range(npc, 8)], axis=0)
    return (np.ascontiguousarray(yp, dtype=np.float32), np.ascontiguousarray(ys, dtype=np.float32))
```

```python
import contextlib
import math

import numpy as np
import concourse.bass as bass
import concourse.mybir as mybir
from concourse.bass_utils import run_bass_kernel_spmd

F32 = mybir.dt.float32
BF16 = mybir.dt.bfloat16
AF = mybir.ActivationFunctionType
ALU = mybir.AluOpType
AX = mybir.AxisListType

ENGS = ("pe", "act", "dve", "pool", "sp")
EPS = 1e-6
NEG = -30000.0


class Buf:
    __slots__ = ("name", "last_w", "readers", "multi", "writers")

    def __init__(self, name, multi=False, carry=()):
        self.name = name
        self.last_w = None
        self.readers = list(carry)
        self.multi = multi
        self.writers = []


class Op:
    __slots__ = ("eng", "fn", "deps", "is_dma", "dsem", "dval", "needs_inc", "ms", "idx", "tiny")

    def __init__(self, eng, fn, is_dma=False):
        self.eng = eng
        self.fn = fn
        self.deps = []
        self.is_dma = is_dma
        self.dsem = None
        self.dval = 0
        self.needs_inc = False
        self.ms = None
        self.idx = None
        self.tiny = False


class Prog:
    def __init__(self, nc):
        self.nc = nc
        self.ops = []
        self.dma_sems = {}
        self.carry = []
        self.scope_bufs = []

    def buf(self, name, multi=False):
        b = Buf(name, multi, self.carry)
        self.scope_bufs.append(b)
        return b

    def release_scope(self):
        pend = list(self.carry)
        for b in self.scope_bufs:
            if b.last_w is not None:
                pend.append(b.last_w)
            pend.extend(b.readers)
            pend.extend(b.writers)
        last = {}
        dmas = {}
        for o in pend:
            if o.is_dma:
                k = o.dsem
                if k not in dmas or dmas[k].dval < o.dval:
                    dmas[k] = o
            else:
                if o.eng not in last or last[o.eng].idx < o.idx:
                    last[o.eng] = o
        self.carry = list(last.values()) + list(dmas.values())
        self.scope_bufs = []

    def _add(self, op, reads, writes):
        deps = []
        for b in reads:
            if b.multi:
                deps.extend(b.writers)
            elif b.last_w is not None:
                deps.append(b.last_w)
        for b in writes:
            if b.multi:
                deps.extend(b.readers)
                b.writers.append(op)
            else:
                if b.last_w is not None:
                    deps.append(b.last_w)
                deps.extend(b.readers)
                b.last_w = op
                b.readers = []
        for b in reads:
            if not b.multi or True:
                b.readers.append(op)
        op.deps = deps
        op.idx = len(self.ops)
        self.ops.append(op)
        return op

    def op(self, eng, fn, reads=(), writes=(), tiny=False):
        o = Op(eng, fn)
        o.tiny = tiny
        return self._add(o, reads, writes)

    def dma(self, queue, fn, semkey, reads=(), writes=()):
        op = Op(queue, fn, is_dma=True)
        ent = self.dma_sems.setdefault(semkey, [0])
        ent[0] += 16
        op.dsem = semkey
        op.dval = ent[0]
        return self._add(op, reads, writes)

    def emit(self):
        nc = self.nc
        for op in self.ops:
            for d in op.deps:
                if (not d.is_dma) and (d.eng != op.eng or d.tiny or op.is_dma):
                    d.needs_inc = True
        cnt = {e: 0 for e in ENGS}
        for op in self.ops:
            if (not op.is_dma) and op.needs_inc:
                cnt[op.eng] += 1
                op.ms = cnt[op.eng]
        with contextlib.ExitStack() as st:
            esem = {e: st.enter_context(nc.semaphore("ms_" + e)) for e in ENGS}
            dsem = {}
            for i, k in enumerate(self.dma_sems):
                dsem[k] = st.enter_context(nc.semaphore("d%d" % i))
            block = st.enter_context(nc.Block())
            per_eng = {e: [] for e in ENGS}
            for op in self.ops:
                per_eng[op.eng].append(op)
            final = [(k, v[0]) for k, v in self.dma_sems.items()]

            def run(eng_name, engine):
                waited_e = {e: 0 for e in ENGS}
                waited_d = {}
                for op in per_eng[eng_name]:
                    need_e = {}
                    need_d = {}
                    for d in op.deps:
                        if d.is_dma:
                            if need_d.get(d.dsem, 0) < d.dval:
                                need_d[d.dsem] = d.dval
                        elif d.eng != eng_name or d.tiny or op.is_dma:
                            if need_e.get(d.eng, 0) < d.ms:
                                need_e[d.eng] = d.ms
                    for e, v in need_e.items():
                        if waited_e[e] < v:
                            engine.wait_ge(esem[e], v)
                            waited_e[e] = v
                    for k, v in need_d.items():
                        if waited_d.get(k, 0) < v:
                            engine.wait_ge(dsem[k], v)
                            waited_d[k] = v
                    ins = op.fn(engine)
                    if op.is_dma:
                        ins.then_inc(dsem[op.dsem], 16)
                    elif op.needs_inc:
                        ins.then_inc(esem[eng_name], 1)
                if eng_name == "sp":
                    for k, v in final:
                        if waited_d.get(k, 0) < v:
                            engine.wait_ge(dsem[k], v)

            block.tensor(lambda e: run("pe", e))
            block.scalar(lambda e: run("act", e))
            block.vector(lambda e: run("dve", e))
            block.gpsimd(lambda e: run("pool", e))
            block.sync(lambda e: run("sp", e))
        return cnt


class Cfg:
    def __init__(self, D=4096, L=2048, TA=512, TC=512, debug=False, stop_after=None, ada_fp32r=False,
                 cast_frac_A=0.6, cast_pool=False, cast_q_A="sp"):
        self.cast_q_A = cast_q_A
        self.stop_after = stop_after
        self.ada_fp32r = ada_fp32r
        self.cast_frac_A = cast_frac_A
        self.cast_pool = cast_pool
        self.D = D
        self.L = L
        self.HD = 128
        self.NHA = D // 2 // 128
        self.NKV = max(self.NHA // 4, 1)
        self.GRP = self.NHA // self.NKV
        self.DV = 256
        self.DK = 128
        self.NHG = D // 2 // 256
        self.RANK = 16
        self.DFF = 4 * D
        self.KC = D // 128
        self.AW = self.NHA * 128
        self.KVW = self.NKV * 128
        self.GKW = self.NHG * 128
        self.GVW = self.NHG * 256
        self.INW = self.AW + 2 * self.KVW + 2 * self.GKW + 2 * self.GVW + 32
        self.NB = L // 128
        self.TOK = 2 * L
        self.NBT = 2 * self.NB
        self.TA = TA
        self.TC = TC
        self.FC = self.DFF // 128
        self.debug = debug
        o = 0
        self.o_q = o; o += self.AW
        self.o_k = o; o += self.KVW
        self.o_v = o; o += self.KVW
        self.o_qg = o; o += self.GKW
        self.o_kg = o; o += self.GKW
        self.o_vg = o; o += self.GVW
        self.o_rg = o; o += self.GVW
        self.o_af = o; o += 16
        self.o_ab = o; o += 16
        assert o == self.INW


def t5_bucket_np(rel):
    half = 16
    max_exact = 8
    ret = np.where(rel > 0, half, 0)
    n = np.abs(rel)
    nf = np.maximum(n, 1).astype(np.float32)
    large = max_exact + (np.log(nf / max_exact) / math.log(128 / max_exact) * (half - max_exact)).astype(np.int32)
    large = np.minimum(large, half - 1)
    return ret + np.where(n < max_exact, n, large)


def host_consts():
    i = np.arange(128)
    s = i[:, None]
    c = i[None, :]
    g = -1.0 / 16.0
    cA = np.zeros((128, 8, 128), np.float32)
    cA[:, 0, :] = np.eye(128)
    cA[:, 1, :] = np.eye(128)[::-1]
    cA[:, 2, :] = np.where(s <= c, g, 0.0)
    cA[:, 3, :] = np.where(s > c, g, 0.0)
    cA[:, 4, :] = np.where(s >= c, g, 0.0)
    cA[:, 5, :] = np.where(s < c, g, 0.0)
    cA[:, 6, :] = np.where(s <= c, 1.0, 0.0)
    cA[:, 7, :] = np.where(s >= c, 1.0, 0.0)
    r = np.arange(512)
    rel = r - 255
    bk = t5_bucket_np(rel)
    cOH = np.zeros((33, 512), np.float32)
    cOH[bk, r] = 1.0
    cOH[:32, 511] = 0.0
    cOH[32, :] = np.where(np.abs(rel) <= 128, 0.0, NEG)
    cOH[32, 511] = NEG
    return cA, cOH


class Arena:
    def __init__(self, big, nwords):
        self.big = big
        self.n = nwords
        self.top = 0

    def alloc(self, shape, dt):
        nel = 1
        for s in shape[1:]:
            nel *= s
        nbytes = nel * (2 if dt == BF16 else 4)
        words = (nbytes + 3) // 4
        a = self.top
        self.top += (words + 7) // 8 * 8
        assert self.top <= self.n, "SBUF arena overflow: %d > %d words" % (self.top, self.n)
        v = self.big[0:shape[0], a:a + words]
        if dt == BF16:
            v = v.bitcast(BF16)
        if len(shape) == 3:
            v = v.rearrange("p (a b) -> p a b", a=shape[1])
        elif len(shape) == 4:
            v = v.rearrange("p (a b c) -> p a b c", a=shape[1], b=shape[2])
        return v


def build_program(cfg):
    nc = bass.Bass("TRN2", target_bir_lowering=False)
    D, L, KC, TOK, NB, NBT = cfg.D, cfg.L, cfg.KC, cfg.TOK, cfg.NB, cfg.NBT
    NHA, NKV, GRP, NHG = cfg.NHA, cfg.NKV, cfg.GRP, cfg.NHG
    AW, KVW, GKW, GVW, INW, DFF, FC = cfg.AW, cfg.KVW, cfg.GKW, cfg.GVW, cfg.INW, cfg.DFF, cfg.FC

    def din(name, shape, dt=F32):
        return nc.dram_tensor(name, list(shape), dt, kind="ExternalInput").ap()

    def dscr(name, shape, dt):
        dbg = cfg.debug is True or (cfg.debug and name in cfg.debug)
        return nc.dram_tensor(name, list(shape), dt, kind="ExternalOutput" if dbg else "Internal").ap()

    x_d = din("x", [TOK, D])
    c2_d = din("c2", [2, D])
    flags_d = din("flags", [128, 2])
    wada_d = din("w_ada", [D, 6 * D])
    bada_d = din("b_ada", [1, 6 * D])
    n1g_d = din("n1g", [1, D])
    win_d = din("w_in", [D, INW])
    wabf_d = din("wab_f", [17, GKW])
    wabb_d = din("wab_b", [17, GKW])
    gng_d = din("gng", [1, 256])
    sink_d = din("sink", [1, NHA])
    relb_d = din("relb", [33, NHA])
    wout_d = din("w_out", [D, D])
    n2g_d = din("n2g", [1, D])
    w1_d = din("w1", [D, DFF])
    w2_d = din("w2", [DFF, D])
    fg_d = din("fg", [1, D])
    cA_d = din("cA", [128, 8, 128])
    cOH_d = din("cOH", [33, 512])
    y_d = nc.dram_tensor("y", [TOK, D], F32, kind="ExternalOutput").ap()

    modraw_d = dscr("modraw", [2, 6 * D], F32)
    gfull_d = dscr("gfull", [2, 2, D], F32)
    winb_d = dscr("winb", [D, INW], BF16)
    woutb_d = dscr("woutb", [D, D], BF16)
    w1t_d = dscr("w1t", [FC, 128, KC, 128], BF16)
    w2b_d = dscr("w2b", [DFF, D], BF16)
    qT_d = dscr("qT", [AW, TOK], BF16)
    kT_d = dscr("kT", [KVW, TOK], BF16)
    v_d = dscr("v", [TOK, KVW], BF16)
    qgT_d = dscr("qgT", [GKW, TOK], BF16)
    kgT_d = dscr("kgT", [GKW, TOK], BF16)
    kg_d = dscr("kg", [TOK, GKW], BF16)
    vg_d = dscr("vg", [TOK, GVW], BF16)
    rg_d = dscr("rg", [TOK, GVW], BF16)
    afT_d = dscr("afT", [16, TOK], F32)
    abT_d = dscr("abT", [16, TOK], F32)
    tvec_d = dscr("tvec", [NHA, 512], F32)
    mixT_d = dscr("mixT", [D, TOK], BF16)

    P = Prog(nc)
    es = contextlib.ExitStack()
    NWORDS = 53000
    big = es.enter_context(nc.sbuf_tensor("big", [128, NWORDS], F32))
    pbig = es.enter_context(nc.psum_tensor("pbig", [128, 4096], F32))
    ar = Arena(big, NWORDS)
    pstate = [0]

    def sb(shape, dt):
        return ar.alloc(list(shape), dt)

    def ps(shape, dt=F32):
        b = pstate[0]
        pstate[0] += 1
        assert b < 8, "PSUM banks exhausted"
        v = pbig[0:shape[0], b * 512:(b + 1) * 512]
        if dt == BF16:
            v = v.bitcast(BF16)
        return v[:, 0:shape[1]]

    def end_phase(mark):
        ar.top = mark
        pstate[0] = 0
        P.release_scope()

    B = {n: Buf(n, multi=True) for n in
         ["modraw", "gfull", "winb", "woutb", "w1t", "w2b", "qT", "kT", "v", "qgT", "kgT", "kg", "vg", "rg",
          "afT", "abT", "tvec", "mixT", "y"]}

    dbg_n = [0]

    def dbg_dump(name, ap, shape, dt, reads):
        if not (cfg.debug and (cfg.debug is True or name in cfg.debug)):
            return
        dten = nc.dram_tensor("dbg_" + name, list(shape), dt, kind="ExternalOutput").ap()
        dbg_n[0] += 1
        P.dma("sp", lambda e: e.dma_start(out=dten, in_=ap), "dbg%d" % dbg_n[0], reads=reads, writes=[])

    cA = sb([128, 8, 128], F32)
    identF = cA[:, 0, :]
    Jm = cA[:, 1, :]
    identB = sb([128, 128], BF16)
    flags = sb([128, 2], F32)
    modT = sb([128, 4, 2, KC], F32)
    bT = sb([128, 4, KC], F32)
    ngT = sb([128, 2, KC], F32)
    scl = sb([128, 2, 2, KC], F32)
    small = sb([128, 64], F32)
    negh = small[:, 0:1]
    b_cA = P.buf("cA"); b_identB = P.buf("identB"); b_flags = P.buf("flags")
    b_modT = P.buf("modT"); b_bT = P.buf("bT"); b_ngT = P.buf("ngT"); b_scl = P.buf("scl")
    b_negh = P.buf("negh")
    P.scope_bufs = []
    MARK0 = ar.top

    P.dma("sp", lambda e: e.dma_start(out=cA, in_=cA_d), "cA", writes=[b_cA])
    P.dma("sp", lambda e: e.dma_start(out=flags, in_=flags_d), "flags", writes=[b_flags])
    P.op("dve", lambda e: e.tensor_copy(out=identB, in_=identF), reads=[b_cA], writes=[b_identB])
    P.op("dve", lambda e: e.memset(negh, -0.5), writes=[b_negh])

    def rstd_ops(ss_ap, b_ss, out_ap, b_out, n, tmp_ap, b_tmp):
        P.op("dve", lambda e: e.tensor_scalar(out=tmp_ap, in0=ss_ap, scalar1=1.0 / n, scalar2=EPS,
                                              op0=ALU.mult, op1=ALU.add), reads=[b_ss], writes=[b_tmp], tiny=True)
        P.op("pool", lambda e: e.tensor_tensor(out=out_ap, in0=tmp_ap, in1=negh, op=ALU.pow),
             reads=[b_tmp, b_negh], writes=[b_out], tiny=True)

    if True:
        NW = 256
        KH = max(KC // 2, 1)
        cT = sb([128, KC, 2], F32)
        cTb = sb([128, KC, 2], BF16)
        wslot = [sb([128, KC, NW], F32) for i in range(3)]
        wslotb = [sb([128, KC, NW], BF16) for i in range(3)]
        mst = [sb([2, NW], F32) for i in range(2)]
        gr = sb([2, D], F32)
        gb = sb([2, D], F32)
        pm = [ps([2, NW]) for i in range(2)]
        b_cT = P.buf("cT"); b_cTb = P.buf("cTb")
        b_ws = [P.buf("wa%d" % i) for i in range(3)]
        b_wsb = [[P.buf("wab%d_%d" % (i, hh)) for hh in range(2)] for i in range(3)]
        b_mst = [P.buf("mst%d" % i) for i in range(2)]
        b_pm = [P.buf("pm%d" % i) for i in range(2)]
        b_gr = P.buf("gr"); b_gb = P.buf("gb")
        for h in range(2):
            P.dma("sp", lambda e, h=h: e.dma_start(out=cT[:, :, h], in_=c2_d[h:h + 1, :].rearrange("o (k p) -> p (o k)", p=128),
                                                   allow_slow_non_contiguous=True), "cT", writes=[b_cT])
        P.op("act", lambda e: e.activation(out=cTb, in_=cT, func=AF.Silu), reads=[b_cT], writes=[b_cTb])
        wv = wada_d.rearrange("(k p) n -> p k n", p=128)
        NT = 6 * D // NW
        for n in range(NT):
            s = n % 3
            m2 = n % 2
            for hh in range(KC // KH):
                P.dma("sp", lambda e, s=s, n=n, hh=hh: e.dma_start(
                    out=wslot[s][:, hh * KH:(hh + 1) * KH, :], in_=wv[:, hh * KH:(hh + 1) * KH, n * NW:(n + 1) * NW]),
                    "wa%d" % s, writes=[b_ws[s]])
            for hh in range(KC // KH):
                if hh % 2 == 0:
                    P.op("act", lambda e, s=s, hh=hh: e.activation(out=wslotb[s][:, hh * KH:(hh + 1) * KH, :],
                                                                   in_=wslot[s][:, hh * KH:(hh + 1) * KH, :], func=AF.Copy),
                         reads=[b_ws[s]], writes=[b_wsb[s][hh % 2]])
                else:
                    P.op("dve", lambda e, s=s, hh=hh: e.tensor_copy(out=wslotb[s][:, hh * KH:(hh + 1) * KH, :],
                                                                    in_=wslot[s][:, hh * KH:(hh + 1) * KH, :]),
                         reads=[b_ws[s]], writes=[b_wsb[s][hh % 2]])
            for k in range(KC):
                P.op("pe", lambda e, s=s, k=k, m2=m2: e.matmul(pm[m2], cTb[:, k, :], wslotb[s][:, k, :],
                                                        start=(k == 0), stop=(k == KC - 1)),
                     reads=[b_cTb, b_wsb[s][(k // KH) % 2]], writes=[b_pm[m2]])
            P.op("act", lambda e, m2=m2: e.activation(out=mst[m2], in_=pm[m2], func=AF.Copy),
                 reads=[b_pm[m2]], writes=[b_mst[m2]])
            P.dma("pool", lambda e, m2=m2, n=n: e.dma_start(out=modraw_d[:, n * NW:(n + 1) * NW], in_=mst[m2]),
                  "mst%d" % m2, reads=[b_mst[m2]], writes=[B["modraw"]])
        for i, sec in enumerate((0, 1, 3, 4)):
            for h in range(2):
                P.dma("sp", lambda e, i=i, sec=sec, h=h: e.dma_start(
                    out=modT[:, i, h, :], in_=modraw_d[h:h + 1, sec * D:(sec + 1) * D].rearrange("o (k p) -> p (o k)", p=128),
                    allow_slow_non_contiguous=True), "modT", reads=[B["modraw"]], writes=[b_modT])
            P.dma("sp", lambda e, i=i, sec=sec: e.dma_start(
                out=bT[:, i, :], in_=bada_d[:, sec * D:(sec + 1) * D].rearrange("o (k p) -> p (o k)", p=128),
                allow_slow_non_contiguous=True), "bT", writes=[b_bT])
        P.dma("sp", lambda e: e.dma_start(out=ngT[:, 0, :], in_=n1g_d.rearrange("o (k p) -> p (o k)", p=128),
                                          allow_slow_non_contiguous=True), "ngT", writes=[b_ngT])
        P.dma("sp", lambda e: e.dma_start(out=ngT[:, 1, :], in_=n2g_d.rearrange("o (k p) -> p (o k)", p=128),
                                          allow_slow_non_contiguous=True), "ngT", writes=[b_ngT])
        for i in range(4):
            for h in range(2):
                P.op("dve", lambda e, i=i, h=h: e.tensor_tensor(out=modT[:, i, h, :], in0=modT[:, i, h, :],
                                                                in1=bT[:, i, :], op=ALU.add),
                     reads=[b_modT, b_bT], writes=[b_modT], tiny=True)
        for nn, si in ((0, 1), (1, 3)):
            for h in range(2):
                P.op("dve", lambda e, nn=nn, si=si, h=h: e.scalar_tensor_tensor(
                    out=scl[:, nn, h, :], in0=modT[:, si, h, :], scalar=1.0, in1=ngT[:, nn, :],
                    op0=ALU.add, op1=ALU.mult), reads=[b_modT, b_ngT], writes=[b_scl], tiny=True)
        for gi, sec in enumerate((2, 5)):
            P.dma("sp", lambda e, sec=sec: e.dma_start(out=gr, in_=modraw_d[:, sec * D:(sec + 1) * D]),
                  "gr", reads=[B["modraw"]], writes=[b_gr])
            P.dma("sp", lambda e, sec=sec: e.dma_start(out=gb, in_=bada_d[:, sec * D:(sec + 1) * D].partition_broadcast(2)),
                  "gb", writes=[b_gb])
            P.op("dve", lambda e: e.tensor_tensor(out=gr, in0=gr, in1=gb, op=ALU.add),
                 reads=[b_gr, b_gb], writes=[b_gr])
            P.dma("sp", lambda e, gi=gi: e.dma_start(out=gfull_d[gi], in_=gr), "gr", reads=[b_gr],
                  writes=[B["gfull"]])
        end_phase(MARK0)

    sh = {0: 0, 1: 2}

    CW = 1024 if D >= 1024 else D
    NSL = 4
    w32 = [sb([128, CW], F32) for i in range(NSL)]
    w16 = [sb([128, CW], BF16) for i in range(NSL)]
    b_w32 = [P.buf("w32_%d" % i) for i in range(NSL)]
    b_w16 = [P.buf("w16_%d" % i) for i in range(NSL)]
    cast_bufs = b_w32 + b_w16
    P.scope_bufs = []
    MARK1 = ar.top
    cnt = [0]
    cast_engs = ("act", "dve", "pool") if cfg.cast_pool else ("act", "dve")

    cast_mode = {"q": "sp", "engs": ("act", "dve")}
    cast_pending = [None]

    def cast_flush():
        if cast_pending[0] is not None:
            cast_pending[0]()
            cast_pending[0] = None

    def cast_piece(src_ap, in_view_fn, out_view_fn, dst_fn, dst_buf, load_view_fn=None):
        i = cnt[0] % NSL
        engs = cast_mode["engs"]
        eng = engs[cnt[0] % len(engs)]
        cnt[0] += 1
        lv = w32[i] if load_view_fn is None else load_view_fn(w32[i])
        P.dma(cast_mode["q"], lambda e: e.dma_start(out=lv, in_=src_ap), "w32_%d" % i, writes=[b_w32[i]])
        ov = out_view_fn(w16[i])
        iv = in_view_fn(w32[i])

        def second():
            if eng == "act":
                P.op("act", lambda e: e.activation(out=ov, in_=iv, func=AF.Copy), reads=[b_w32[i]], writes=[b_w16[i]])
            else:
                P.op(eng, lambda e: e.tensor_copy(out=ov, in_=iv), reads=[b_w32[i]], writes=[b_w16[i]])
            P.dma("pool", lambda e: dst_fn(e, w16[i]), "w16_%d" % i, reads=[b_w16[i]], writes=[dst_buf])
        prev = cast_pending[0]
        cast_pending[0] = second
        if prev is not None:
            prev()

    def cast_natural(src_d, dst_d, rows, cols, bname):
        for r0 in range(0, rows, 128):
            for c0 in range(0, cols, CW):
                cw = min(CW, cols - c0)
                cast_piece(src_d[r0:r0 + 128, c0:c0 + cw], lambda t, cw=cw: t[:, 0:cw], lambda t, cw=cw: t[:, 0:cw],
                           lambda e, t, r0=r0, c0=c0, cw=cw: e.dma_start(out=dst_d[r0:r0 + 128, c0:c0 + cw],
                                                                          in_=t[:, 0:cw]), B[bname],
                           load_view_fn=lambda t, cw=cw: t[:, 0:cw])
                yield

    def cast_w1():
        KK = min(4, KC)
        CC = CW // KK
        NF = CC // 128
        for k0 in range(0, KC, KK):
            for c0 in range(0, DFF, CC):
                f0 = c0 // 128
                cast_piece(w1_d[k0 * 128:(k0 + KK) * 128, c0:c0 + CC].rearrange("(k p) c -> p k c", p=128),
                           lambda t: t[:, 0:KK * CC].rearrange("p (k f j) -> p f k j", k=KK, f=NF),
                           lambda t: t[:, 0:KK * CC].rearrange("p (f k j) -> p f k j", f=NF, k=KK),
                           lambda e, t, k0=k0, f0=f0: e.dma_start(
                               out=w1t_d[f0:f0 + NF, :, k0:k0 + KK, :].rearrange("f p k j -> p f k j"),
                               in_=t[:, 0:KK * CC].rearrange("p (f k j) -> p f k j", f=NF, k=KK)), B["w1t"],
                           load_view_fn=lambda t: t[:, 0:KK * CC].rearrange("p (k c) -> p k c", k=KK))
                yield

    for _ in cast_natural(win_d, winb_d, D, INW, "winb"):
        pass
    cast_flush()

    def cast_rest():
        yield from cast_natural(wout_d, woutb_d, D, D, "woutb")
        yield from cast_w1()
        yield from cast_natural(w2_d, w2b_d, DFF, D, "w2b")

    cast_gen = cast_rest()
    n_cast_total = (D // 128) * ((D + CW - 1) // CW) + (KC // min(4, KC)) * (DFF // (CW // min(4, KC))) + (DFF // 128) * ((D + CW - 1) // CW)

    def cast_some(n):
        for _ in range(n):
            try:
                next(cast_gen)
            except StopIteration:
                cast_flush()
                return

    def norm_transpose_block(xt, b_xt, half, nn, hT, b_hT, tcol, pst, b_pst, junk, junk_bufs, ssv, b_ss, rsv, b_rs,
                             tmpv, b_tmp, undo, act_ok=True):
        P.op("act", lambda e: e.activation(out=junk, in_=xt, func=AF.Square, accum_out=ssv),
             reads=[b_xt], writes=list(junk_bufs) + [b_ss])
        rstd_ops(ssv, b_ss, rsv, b_rs, D, tmpv, b_tmp)
        P.op("dve", lambda e: e.tensor_scalar(out=xt, in0=xt, scalar1=rsv, scalar2=None, op0=ALU.mult),
             reads=[b_xt, b_rs], writes=[b_xt])
        G4 = 4
        for k0 in range(0, KC, G4):
            pi = (k0 // G4) % len(pst)
            for j in range(G4):
                k = k0 + j
                P.op("pe", lambda e, k=k, j=j, pi=pi: e.transpose(pst[pi][:, j * 128:(j + 1) * 128],
                                                                   xt[:, k * 128:(k + 1) * 128], identF),
                     reads=[b_xt, b_cA], writes=[b_pst[pi]])
            for j in range(G4):
                k = k0 + j
                if j % 2 == 0 and act_ok:
                    P.op("act", lambda e, k=k, j=j, pi=pi: e.activation(
                        out=hT[:, k, tcol:tcol + 128], in_=pst[pi][:, j * 128:(j + 1) * 128], func=AF.Identity,
                        bias=modT[:, sh[nn], half, k:k + 1], scale=scl[:, nn, half, k:k + 1]),
                        reads=[b_pst[pi], b_modT, b_scl], writes=[b_hT])
                else:
                    P.op("dve", lambda e, k=k, j=j, pi=pi: e.tensor_scalar(
                        out=hT[:, k, tcol:tcol + 128], in0=pst[pi][:, j * 128:(j + 1) * 128],
                        scalar1=scl[:, nn, half, k:k + 1], scalar2=modT[:, sh[nn], half, k:k + 1],
                        op0=ALU.mult, op1=ALU.add), reads=[b_pst[pi], b_modT, b_scl], writes=[b_hT])
        if undo:
            P.op("dve", lambda e: e.reciprocal(out=tmpv, in_=rsv), reads=[b_rs], writes=[b_tmp], tiny=True)
            P.op("pool", lambda e: e.tensor_scalar(out=xt, in0=xt, scalar1=tmpv, scalar2=1.0, op0=ALU.mult,
                                                   op1=ALU.mult),
                 reads=[b_xt, b_tmp], writes=[b_xt])

    TA = cfg.TA
    NBA = TA // 128
    if True:
        xin = [sb([128, D], F32) for i in range(2)]
        junk = sb([128, D], BF16)
        h1T = sb([128, KC, TA], BF16)
        NWS = 3
        wsl = [sb([128, KC, 512], BF16) for i in range(NWS)]
        NST = 4
        stg = [sb([128, 512], BF16) for i in range(NST)]
        stg32 = sb([32, 512], F32)
        pst = [ps([128, 512]) for i in range(2)]
        pacc = [ps([128, 512]) for i in range(4)]
        b_xin = [P.buf("xin%d" % i) for i in range(2)]
        b_junk = P.buf("junkA")
        b_h1T = P.buf("h1T")
        b_wsl = [P.buf("winS%d" % i) for i in range(NWS)]
        b_stg = [P.buf("stgA%d" % i) for i in range(NST)]
        b_stg32 = P.buf("stgA32")
        b_pst = [P.buf("pstA%d" % i) for i in range(2)]
        b_pacc = [P.buf("paccA%d" % i) for i in range(4)]
        b_ss = P.buf("ssA"); b_rs = P.buf("rsA"); b_tmp = P.buf("tmpA")
        ssv, rsv, tmpv = small[:, 1:2], small[:, 2:3], small[:, 3:4]

        groups = [("q", cfg.o_q, AW, qT_d, None), ("k", cfg.o_k, KVW, kT_d, None), ("v", cfg.o_v, KVW, None, v_d),
                  ("qg", cfg.o_qg, GKW, qgT_d, None), ("kg", cfg.o_kg, GKW, kgT_d, kg_d),
                  ("vg", cfg.o_vg, GVW, None, vg_d), ("rg", cfg.o_rg, GVW, None, rg_d),
                  ("ab", cfg.o_af, 32, "afab", None)]
        slabs = []
        for name, c0, ncols, fm, tm in groups:
            for s0 in range(0, ncols, 512):
                slabs.append((name, c0 + s0, min(512, ncols - s0), s0, fm, tm))
        winv = winb_d.rearrange("(k p) n -> p k n", p=128)
        cnt_w = [0]; cnt_s = [0]; cnt_p = [0]
        n_groups_A = (TOK // TA) * sum((((nc_ + 127) // 128) if fm_ is not None else 0) + (NBA if tm_ is not None else 0)
                                       for (_, _, nc_, _, fm_, tm_) in slabs)
        cast_rate_A = cfg.cast_frac_A * n_cast_total / n_groups_A
        cast_acc = [0.0]
        cast_mode["q"] = cfg.cast_q_A
        cast_mode["engs"] = ("act",)

        def cast_tick():
            cast_acc[0] += cast_rate_A
            while cast_acc[0] >= 1.0:
                cast_acc[0] -= 1.0
                cast_some(1)

        for t in range(TOK // TA):
            tok0 = t * TA
            half = tok0 // L
            for b in range(NBA):
                xi = (t * NBA + b) % 2
                P.dma("sp", lambda e, xi=xi, r0=tok0 + b * 128: e.dma_start(out=xin[xi], in_=x_d[r0:r0 + 128, :]),
                      "xin%d" % xi, writes=[b_xin[xi]])
                norm_transpose_block(xin[xi], b_xin[xi], half, 0, h1T, b_h1T, b * 128, pst, b_pst, junk, [b_junk],
                                     ssv, b_ss, rsv, b_rs, tmpv, b_tmp, undo=False, act_ok=False)
            for (name, col0, ncols, s0, fm, tm) in slabs:
                wi = cnt_w[0] % NWS
                cnt_w[0] += 1
                P.dma("sp", lambda e, wi=wi, col0=col0, ncols=ncols: e.dma_start(
                    out=wsl[wi][:, :, 0:ncols], in_=winv[:, :, col0:col0 + ncols]), "winS%d" % wi,
                    reads=[B["winb"]], writes=[b_wsl[wi]])
                if fm is not None:
                    for c in range(0, ncols, 128):
                        m = min(128, ncols - c)
                        pi = cnt_p[0] % 4
                        cnt_p[0] += 1
                        for k in range(KC):
                            P.op("pe", lambda e, pi=pi, wi=wi, k=k, c=c, m=m: e.matmul(
                                pacc[pi][0:m, 0:TA], wsl[wi][:, k, c:c + m], h1T[:, k, :],
                                start=(k == 0), stop=(k == KC - 1)),
                                reads=[b_wsl[wi], b_h1T], writes=[b_pacc[pi]])
                        if fm == "afab":
                            P.op("act", lambda e, pi=pi: e.activation(out=stg32[:, 0:TA], in_=pacc[pi][0:32, 0:TA],
                                                                      func=AF.Copy),
                                 reads=[b_pacc[pi]], writes=[b_stg32])
                            P.dma("pool", lambda e, tok0=tok0: e.dma_start(out=afT_d[:, tok0:tok0 + TA],
                                                                            in_=stg32[0:16, 0:TA]),
                                  "stgA32", reads=[b_stg32], writes=[B["afT"]])
                            P.dma("pool", lambda e, tok0=tok0: e.dma_start(out=abT_d[:, tok0:tok0 + TA],
                                                                            in_=stg32[16:32, 0:TA]),
                                  "stgA32", reads=[b_stg32], writes=[B["abT"]])
                        else:
                            si = cnt_s[0] % NST
                            cnt_s[0] += 1
                            P.op("dve", lambda e, pi=pi, si=si: e.tensor_copy(out=stg[si][:, 0:TA],
                                                                              in_=pacc[pi][:, 0:TA]),
                                 reads=[b_pacc[pi]], writes=[b_stg[si]])
                            r0 = s0 + c
                            P.dma("pool", lambda e, si=si, fm=fm, r0=r0, tok0=tok0: e.dma_start(
                                out=fm[r0:r0 + 128, tok0:tok0 + TA], in_=stg[si][:, 0:TA]), "stgA%d" % si,
                                reads=[b_stg[si]], writes=[B[name + "T"]])
                        cast_tick()
                if tm is not None:
                    for b in range(NBA):
                        pi = cnt_p[0] % 4
                        cnt_p[0] += 1
                        for k in range(KC):
                            P.op("pe", lambda e, pi=pi, wi=wi, k=k, b=b, ncols=ncols: e.matmul(
                                pacc[pi][:, 0:ncols], h1T[:, k, b * 128:(b + 1) * 128], wsl[wi][:, k, 0:ncols],
                                start=(k == 0), stop=(k == KC - 1)),
                                reads=[b_wsl[wi], b_h1T], writes=[b_pacc[pi]])
                        si = cnt_s[0] % NST
                        cnt_s[0] += 1
                        P.op("dve", lambda e, pi=pi, si=si, ncols=ncols: e.tensor_copy(
                            out=stg[si][:, 0:ncols], in_=pacc[pi][:, 0:ncols]),
                            reads=[b_pacc[pi]], writes=[b_stg[si]])
                        r0 = tok0 + b * 128
                        P.dma("pool", lambda e, si=si, tm=tm, r0=r0, s0=s0, ncols=ncols: e.dma_start(
                            out=tm[r0:r0 + 128, s0:s0 + ncols], in_=stg[si][:, 0:ncols]), "stgA%d" % si,
                            reads=[b_stg[si]], writes=[B[name]])
                        cast_tick()
        end_phase(MARK1)

    if cfg.stop_after == "A":
        P.emit()
        es.close()
        return nc

    if True:
        relb = sb([33, NHA], F32)
        cOH = sb([33, 512], F32)
        tv = sb([NHA, 512], F32)
        hank = [sb([128, 384], F32) for i in range(2)]
        bias = sb([128, NHA, 384], F32)
        sinkbc = sb([128, NHA], F32)
        kTs = [sb([128, TOK], BF16) for i in range(2)]
        vs = [sb([128, NBT, 128], BF16) for i in range(2)]
        qTs = [sb([128, TOK], BF16) for i in range(2)]
        GB = 4
        s_sb = [sb([128, 384], F32) for i in range(GB)]
        p_sb = [sb([128, 384], F32) for i in range(GB)]
        pn_sb = [[sb([128, 384], BF16) for i in range(GB)] for par in range(2)]
        pT_sb = [sb([128, 384], BF16) for i in range(GB)]
        ostg = [sb([128, 512], BF16) for i in range(2)]
        sm = sb([128, GB, 8], F32)
        ps_s2 = [ps([128, 384]) for i in range(2)]
        ps_s = [ps_s2[i % 2] for i in range(GB)]
        ps_t = [ps([128, 384], BF16) for i in range(GB)]
        ps_ob = [ps([128, 512]) for i in range(2)]
        ps_o = [ps_ob[i % 2][:, (i // 2) * 128:(i // 2 + 1) * 128] for i in range(GB)]
        ps_x = ps_ob[0]
        b_relb = P.buf("relb"); b_cOH = P.buf("cOH"); b_tv = P.buf("tv")
        b_hank = [P.buf("hank%d" % i) for i in range(2)]
        b_bias = P.buf("bias"); b_sink = P.buf("sinkbc")
        b_kTs = [P.buf("kTs%d" % i) for i in range(2)]
        b_vs = [P.buf("vs%d" % i) for i in range(2)]
        b_qTs = [P.buf("qTs%d" % i) for i in range(2)]
        b_s = [P.buf("s_sb%d" % i) for i in range(GB)]
        b_p = [P.buf("p_sb%d" % i) for i in range(GB)]
        b_pn = [[P.buf("pn%d_%d" % (par, i)) for i in range(GB)] for par in range(2)]
        b_pT = [P.buf("pT%d" % i) for i in range(GB)]
        b_ostg = [P.buf("ostg%d" % i) for i in range(2)]
        b_sm = [P.buf("smA%d" % i) for i in range(GB)]
        b_ps_s2 = [P.buf("ps_s%d" % i) for i in range(2)]
        b_ps_s = [b_ps_s2[i % 2] for i in range(GB)]
        b_ps_t = [P.buf("ps_t%d" % i) for i in range(GB)]
        b_ps_ob = [P.buf("ps_o%d" % i) for i in range(2)]
        b_ps_o = [b_ps_ob[i % 2] for i in range(GB)]
        b_ps_x = b_ps_ob[0]

        P.dma("sp", lambda e: e.dma_start(out=relb, in_=relb_d), "relb", writes=[b_relb])
        P.dma("sp", lambda e: e.dma_start(out=cOH, in_=cOH_d), "cOH", writes=[b_cOH])
        P.dma("sp", lambda e: e.dma_start(out=sinkbc, in_=sink_d.partition_broadcast(128)), "sinkbc",
              writes=[b_sink])
        P.op("pe", lambda e: e.matmul(ps_x[0:NHA, :], relb, cOH, start=True, stop=True),
             reads=[b_relb, b_cOH], writes=[b_ps_x])
        P.op("act", lambda e: e.activation(out=tv, in_=ps_x[0:NHA, :], func=AF.Copy), reads=[b_ps_x], writes=[b_tv])
        P.dma("sp", lambda e: e.dma_start(out=tvec_d, in_=tv), "tv", reads=[b_tv], writes=[B["tvec"]])
        for h in range(NHA):
            hi = h % 2
            P.dma("sp", lambda e, h=h, hi=hi: e.dma_start(
                out=hank[hi], in_=bass.AP(tvec_d.tensor, h * 512, [[1, 128], [1, 384]])), "hank%d" % hi,
                reads=[B["tvec"]], writes=[b_hank[hi]])
            P.op("pe", lambda e, hi=hi: e.matmul(ps_x[:, 0:384], Jm, hank[hi], start=True, stop=True),
                 reads=[b_cA, b_hank[hi]], writes=[b_ps_x])
            P.op("act", lambda e, h=h: e.activation(out=bias[:, h, :], in_=ps_x[:, 0:384], func=AF.Copy),
                 reads=[b_ps_x], writes=[b_bias])

        dbg_dump("bias", bias, [128, NHA, 384], F32, [b_bias])
        scale_a = 1.0 / math.sqrt(128.0)
        hcount = 0
        cast_mode["q"] = "sp"
        cast_mode["engs"] = ("dve",)
        n_b1_slots = NHA * (NBT // 4)
        cast_per_b1 = int(math.ceil((1.0 - cfg.cast_frac_A) * n_cast_total / n_b1_slots)) + 1
        def issue_kv(kvh):
            ki = kvh % 2
            P.dma("sp", lambda e: e.dma_start(out=kTs[ki], in_=kT_d[kvh * 128:(kvh + 1) * 128, :]),
                  "kTs%d" % ki, reads=[B["kT"]], writes=[b_kTs[ki]])
            P.dma("sp", lambda e: e.dma_start(
                out=vs[ki], in_=v_d[:, kvh * 128:(kvh + 1) * 128].rearrange("(n p) d -> p n d", p=128)),
                "vs%d" % ki, reads=[B["v"]], writes=[b_vs[ki]])

        def issue_q(h):
            qi = h % 2
            P.dma("sp", lambda e: e.dma_start(out=qTs[qi], in_=qT_d[h * 128:(h + 1) * 128, :]),
                  "qTs%d" % qi, reads=[B["qT"]], writes=[b_qTs[qi]])

        issue_kv(0)
        issue_q(0)
        for kvh in range(NKV):
            ki = kvh % 2
            for g in range(GRP):
                h = kvh * GRP + g
                qi = h % 2
                if h + 1 < NHA:
                    if (h + 1) // GRP != kvh:
                        issue_kv((h + 1) // GRP)
                    issue_q(h + 1)

                def win(gn):
                    jlo = 0 if gn > 0 else 1
                    jhi = 2 if gn < NBT - 1 else 1
                    return jlo, jhi

                def front_steps(gn, par, h=h, qi=qi, ki=ki):
                    r = gn % GB
                    jlo, jhi = win(gn)
                    W = (jhi - jlo + 1) * 128
                    k0 = (gn - 1 + jlo) * 128
                    cross = None
                    if gn == NB - 1:
                        cross = (2 - jlo) * 128
                    elif gn == NB:
                        cross = 0
                    st = []
                    def s2():
                        P.op("pe", lambda e: e.matmul(ps_s[r][:, 0:W], qTs[qi][:, gn * 128:(gn + 1) * 128],
                                                      kTs[ki][:, k0:k0 + W], start=True, stop=True),
                             reads=[b_qTs[qi], b_kTs[ki]], writes=[b_ps_s[r]])
                        P.op("dve", lambda e: e.scalar_tensor_tensor(
                            out=s_sb[r][:, 0:W], in0=ps_s[r][:, 0:W], scalar=scale_a,
                            in1=bias[:, h, jlo * 128:jlo * 128 + W], op0=ALU.mult, op1=ALU.add),
                            reads=[b_ps_s[r], b_bias], writes=[b_s[r]])
                        if cross is not None:
                            P.op("dve", lambda e: e.tensor_scalar(
                                out=s_sb[r][:, cross:cross + 128], in0=s_sb[r][:, cross:cross + 128],
                                scalar1=flags[:, 1:2], scalar2=None, op0=ALU.add),
                                reads=[b_s[r], b_flags], writes=[b_s[r]])
                    st.append(s2)
                    st.append(lambda: P.op("dve", lambda e: e.tensor_reduce(out=sm[:, r, 0:1], in_=s_sb[r][:, 0:W],
                                                                            axis=AX.X, op=ALU.max, negate=True),
                                           reads=[b_s[r]], writes=[b_sm[r]], tiny=True))
                    st.append(lambda: P.op("act", lambda e: e.activation(
                        out=p_sb[r][:, 0:W], in_=s_sb[r][:, 0:W], func=AF.Exp, bias=sm[:, r, 0:1], scale=1.0,
                        accum_out=sm[:, r, 1:2]), reads=[b_s[r], b_sm[r]], writes=[b_p[r], b_sm[r]]))
                    st.append(lambda: P.op("act", lambda e: e.activation(
                        out=sm[:, r, 2:3], in_=sm[:, r, 0:1], func=AF.Exp, bias=sinkbc[:, h:h + 1], scale=1.0),
                        reads=[b_sm[r], b_sink], writes=[b_sm[r]], tiny=True))
                    st.append(lambda: P.op("dve", lambda e: e.tensor_tensor(
                        out=sm[:, r, 3:4], in0=sm[:, r, 1:2], in1=sm[:, r, 2:3], op=ALU.add),
                        reads=[b_sm[r]], writes=[b_sm[r]], tiny=True))
                    st.append(lambda: P.op("dve", lambda e: e.reciprocal(out=sm[:, r, 4:5], in_=sm[:, r, 3:4]),
                                           reads=[b_sm[r]], writes=[b_sm[r]], tiny=True))
                    st.append(lambda: P.op("act", lambda e: e.activation(
                        out=pn_sb[par][r][:, 0:W], in_=p_sb[r][:, 0:W], func=AF.Copy, scale=sm[:, r, 4:5]),
                        reads=[b_p[r], b_sm[r]], writes=[b_pn[par][r]]))
                    return st

                def back_steps(gn, par, h=h, qi=qi, ki=ki):
                    r = gn % GB
                    jlo, jhi = win(gn)
                    nj = jhi - jlo + 1
                    W = nj * 128
                    st = []

                    def t1():
                        for j in range(nj):
                            P.op("pe", lambda e, j=j: e.transpose(ps_t[r][:, j * 128:(j + 1) * 128],
                                                                  pn_sb[par][r][:, j * 128:(j + 1) * 128], identB),
                                 reads=[b_pn[par][r], b_identB], writes=[b_ps_t[r]])
                    st.append(t1)
                    st.append(lambda: P.op("act", lambda e: e.activation(out=pT_sb[r][:, 0:W], in_=ps_t[r][:, 0:W],
                                                                         func=AF.Copy),
                                           reads=[b_ps_t[r]], writes=[b_pT[r]]))

                    def t3():
                        for j in range(nj):
                            kb = gn - 1 + jlo + j
                            P.op("pe", lambda e, j=j, kb=kb: e.matmul(ps_o[r], vs[ki][:, kb, :],
                                                                       pT_sb[r][:, j * 128:(j + 1) * 128],
                                                                       start=(j == 0), stop=(j == nj - 1)),
                                 reads=[b_vs[ki], b_pT[r]], writes=[b_ps_o[r]])
                    st.append(t3)

                    def t4():
                        oi = (gn // 4) % 2
                        c = (gn % 4) * 128
                        P.op("dve", lambda e: e.tensor_copy(out=ostg[oi][:, c:c + 128], in_=ps_o[r]),
                             reads=[b_ps_o[r]], writes=[b_ostg[oi]])
                        if gn % 4 == 3:
                            c0 = (gn - 3) * 128
                            P.dma("pool", lambda e: e.dma_start(out=mixT_d[h * 128:(h + 1) * 128, c0:c0 + 512],
                                                                in_=ostg[oi]), "ostg%d" % oi,
                                  reads=[b_ostg[oi]], writes=[B["mixT"]])
                    st.append(t4)
                    return st

                def emit_interleaved(lists):
                    for k in range(len(lists[0])):
                        for l in lists:
                            l[k]()

                def emit_zip(fronts, backs):
                    nf = len(fronts[0]) if fronts else 0
                    nb_ = len(backs[0]) if backs else 0
                    for k in range(max(nf, nb_)):
                        if k < nf:
                            for l in fronts:
                                l[k]()
                        if k < nb_:
                            for l in backs:
                                l[k]()

                NG = NBT // GB
                emit_interleaved([front_steps(gn, 0) for gn in range(0, GB)])
                for g in range(NG):
                    fr = ([front_steps(gn, (g + 1) % 2) for gn in range((g + 1) * GB, (g + 2) * GB)]
                          if g + 1 < NG else [])
                    bk = [back_steps(gn, g % 2) for gn in range(g * GB, (g + 1) * GB)]
                    emit_zip(fr, bk)
                    cast_some(cast_per_b1)
        cast_some(10 ** 9)
        P.scope_bufs.extend(cast_bufs)
        end_phase(MARK0)

    if cfg.stop_after == "B1":
        P.emit()
        es.close()
        return nc

    if True:
        afT1 = [sb([17, TOK], F32) for i in range(2)]
        wab = sb([17, 2, GKW], F32)
        gngbc = sb([128, 256], F32)
        qgTs = sb([128, TOK], BF16)
        kgTs = sb([128, TOK], BF16)
        kgs = sb([128, NBT, 128], BF16)
        vgs = sb([128, NBT, 256], BF16)
        srg = sb([128, NBT, 256], BF16)
        lsb = [sb([128, NBT, 128], F32) for i in range(2)]
        ost = sb([128, NBT, 256], F32)
        etmp = sb([128, 512], F32)
        E1 = [[sb([128, 128], F32) for r in range(2)] for d in range(2)]
        E2 = [[sb([128, 128], F32) for r in range(2)] for d in range(2)]
        E3 = [[sb([128, 128], F32) for r in range(2)] for d in range(2)]
        QtT = [[sb([128, 128], BF16) for r in range(2)] for d in range(2)]
        KtT = [[sb([128, 128], BF16) for r in range(2)] for d in range(2)]
        Kd = [[sb([128, 128], BF16) for r in range(2)] for d in range(2)]
        ATs = [[sb([128, 128], BF16) for r in range(2)] for d in range(2)]
        Sst = [sb([128, 256], F32) for d in range(2)]
        Sbf = [[sb([128, 256], BF16) for r in range(2)] for d in range(2)]
        t1 = [sb([128, 256], F32) for d in range(2)]
        ysb = [sb([128, 256], BF16) for d in range(2)]
        junkg = sb([128, 256], BF16)
        ystg = [[sb([128, 2, 512], BF16) for r in range(2)] for d in range(2)]
        smg = sb([128, 2, 8], F32)
        ps_zy = ps([128, 512])
        ps_z = ps_zy
        ps_y = ps_zy.bitcast(BF16)[:, 0:256]
        ps_bd = [ps([128, 256]) for d in range(2)]
        ps_a2 = [ps([128, 128]) for d in range(2)]
        ps_og = [ps([128, 256]) for d in range(2)]
        ps_u = ps([128, 256])
        b_afT1 = [P.buf("afT1_%d" % i) for i in range(2)]
        b_wab = P.buf("wab"); b_gng = P.buf("gngbc")
        b_qgTs = P.buf("qgTs"); b_kgTs = P.buf("kgTs"); b_kgs = P.buf("kgs"); b_vgs = P.buf("vgs")
        b_srg = P.buf("srg")
        b_l = [P.buf("lsb%d" % i) for i in range(2)]
        b_ost = [P.buf("ost%d" % n) for n in range(NBT)]
        b_etmp = P.buf("etmp")
        mk = lambda nm: [[P.buf("%s%d%d" % (nm, d, r)) for r in range(2)] for d in range(2)]
        b_E1 = mk("E1"); b_E2 = mk("E2"); b_E3 = mk("E3"); b_Qt = mk("Qt"); b_Kt = mk("Kt"); b_Kd = mk("Kd")
        b_AT = mk("AT"); b_Sbf = mk("Sbf"); b_ystg = mk("ystg")
        b_S = [P.buf("S%d" % d) for d in range(2)]
        b_t1 = [P.buf("t1_%d" % d) for d in range(2)]; b_ysb = [P.buf("ysb%d" % d) for d in range(2)]
        b_junkg = P.buf("junkg"); b_smg = [P.buf("smg%d" % d) for d in range(2)]
        b_ps_z = P.buf("ps_zy"); b_ps_bd = [P.buf("ps_bd%d" % d) for d in range(2)]
        b_ps_a2 = [P.buf("ps_a%d" % d) for d in range(2)]; b_ps_og = [P.buf("ps_og%d" % d) for d in range(2)]
        b_ps_u = P.buf("ps_u"); b_ps_y = b_ps_z

        for i, (src, bn) in enumerate(((afT_d, "afT"), (abT_d, "abT"))):
            P.op("pool", lambda e, i=i: e.memset(afT1[i], 1.0), writes=[b_afT1[i]])
            P.dma("sp", lambda e, i=i, src=src: e.dma_start(out=afT1[i][0:16, :], in_=src), "afT1_%d" % i,
                  reads=[B[bn]], writes=[b_afT1[i]])
        P.dma("sp", lambda e: e.dma_start(out=wab[:, 0, :], in_=wabf_d), "wab", writes=[b_wab])
        P.dma("sp", lambda e: e.dma_start(out=wab[:, 1, :], in_=wabb_d), "wab", writes=[b_wab])
        P.dma("sp", lambda e: e.dma_start(out=gngbc, in_=gng_d.partition_broadcast(128)), "gngbc", writes=[b_gng])
        scale_g = 1.0 / math.sqrt(128.0)
        T1 = [cA[:, 2, :], cA[:, 4, :]]
        T2 = [cA[:, 3, :], cA[:, 5, :]]
        MK = [cA[:, 6, :], cA[:, 7, :]]

        for h in range(NHG):
            P.dma("sp", lambda e, h=h: e.dma_start(out=qgTs, in_=qgT_d[h * 128:(h + 1) * 128, :]), "qgTs",
                  reads=[B["qgT"]], writes=[b_qgTs])
            P.dma("sp", lambda e, h=h: e.dma_start(out=kgTs, in_=kgT_d[h * 128:(h + 1) * 128, :]), "kgTs",
                  reads=[B["kgT"]], writes=[b_kgTs])
            P.dma("sp", lambda e, h=h: e.dma_start(
                out=kgs, in_=kg_d[:, h * 128:(h + 1) * 128].rearrange("(n p) d -> p n d", p=128)), "kgs",
                reads=[B["kg"]], writes=[b_kgs])
            P.dma("sp", lambda e, h=h: e.dma_start(
                out=vgs, in_=vg_d[:, h * 256:(h + 1) * 256].rearrange("(n p) d -> p n d", p=128)), "vgs",
                reads=[B["vg"]], writes=[b_vgs])
            P.dma("sp", lambda e, h=h: e.dma_start(
                out=srg, in_=rg_d[:, h * 256:(h + 1) * 256].rearrange("(n p) d -> p n d", p=128)), "srg",
                reads=[B["rg"]], writes=[b_srg])
            for d in range(2):
                for g4 in range(0, NBT, 4):
                    for j in range(4):
                        gn = g4 + j
                        P.op("pe", lambda e, d=d, gn=gn, j=j, h=h: e.matmul(
                            ps_z[:, j * 128:(j + 1) * 128], afT1[d][:, gn * 128:(gn + 1) * 128],
                            wab[:, d, h * 128:(h + 1) * 128], start=True, stop=True),
                            reads=[b_afT1[d], b_wab], writes=[b_ps_z])
                    P.op("act", lambda e: e.activation(out=etmp, in_=ps_z, func=AF.Exp, scale=-1.0),
                         reads=[b_ps_z], writes=[b_etmp])
                    P.op("act", lambda e, d=d, g4=g4: e.activation(
                        out=lsb[d][:, g4:g4 + 4, :], in_=etmp.rearrange("p (a b) -> p a b", a=4), func=AF.Ln,
                        bias=1.0, scale=1.0), reads=[b_etmp], writes=[b_l[d]])
            P.op("act", lambda e: e.activation(out=srg, in_=srg, func=AF.Silu), reads=[b_srg], writes=[b_srg])

            def prep(d, gn, r):
                P.op("pe", lambda e: e.matmul(ps_bd[d][:, 0:128], lsb[d][:, gn, :], T1[d], start=True, stop=True),
                     reads=[b_l[d], b_cA], writes=[b_ps_bd[d]])
                P.op("pe", lambda e: e.matmul(ps_bd[d][:, 128:256], T2[d], lsb[d][:, gn, :], start=True, stop=True),
                     reads=[b_l[d], b_cA], writes=[b_ps_bd[d]])
                P.op("act", lambda e: e.activation(out=E1[d][r], in_=ps_bd[d][:, 0:128], func=AF.Exp),
                     reads=[b_ps_bd[d]], writes=[b_E1[d][r]])
                P.op("act", lambda e: e.activation(out=E2[d][r], in_=ps_bd[d][:, 0:128], func=AF.Exp, scale=-1.0),
                     reads=[b_ps_bd[d]], writes=[b_E2[d][r]])
                P.op("act", lambda e: e.activation(out=E3[d][r], in_=ps_bd[d][:, 128:256], func=AF.Exp),
                     reads=[b_ps_bd[d]], writes=[b_E3[d][r]])
                P.op("dve", lambda e: e.scalar_tensor_tensor(
                    out=QtT[d][r], in0=qgTs[:, gn * 128:(gn + 1) * 128], scalar=scale_g, in1=E1[d][r],
                    op0=ALU.mult, op1=ALU.mult), reads=[b_qgTs, b_E1[d][r]], writes=[b_Qt[d][r]])
                P.op("pool", lambda e: e.tensor_tensor(out=KtT[d][r], in0=kgTs[:, gn * 128:(gn + 1) * 128],
                                                       in1=E2[d][r], op=ALU.mult),
                     reads=[b_kgTs, b_E2[d][r]], writes=[b_Kt[d][r]])
                P.op("pool", lambda e: e.tensor_tensor(out=Kd[d][r], in0=kgs[:, gn, :], in1=E3[d][r], op=ALU.mult),
                     reads=[b_kgs, b_E3[d][r]], writes=[b_Kd[d][r]])

            def post_front(d, gn):
                P.op("act", lambda e: e.activation(out=junkg, in_=ost[:, gn, :], func=AF.Square,
                                                   accum_out=smg[:, d, 0:1]),
                     reads=[b_ost[gn]], writes=[b_junkg, b_smg[d]])
                rstd_ops(smg[:, d, 0:1], b_smg[d], smg[:, d, 1:2], b_smg[d], 256, smg[:, d, 2:3], b_smg[d])
                P.op("dve", lambda e: e.scalar_tensor_tensor(out=t1[d], in0=ost[:, gn, :], scalar=smg[:, d, 1:2],
                                                             in1=gngbc, op0=ALU.mult, op1=ALU.mult),
                     reads=[b_ost[gn], b_smg[d], b_gng], writes=[b_t1[d]])
                P.op("pool", lambda e: e.tensor_tensor(out=ysb[d], in0=t1[d], in1=srg[:, gn, :], op=ALU.mult),
                     reads=[b_t1[d], b_srg], writes=[b_ysb[d]])

            def post_back(d, gn, h=h):
                for c in range(2):
                    P.op("pe", lambda e, c=c: e.transpose(ps_y[:, c * 128:(c + 1) * 128],
                                                          ysb[d][:, c * 128:(c + 1) * 128], identB),
                         reads=[b_ysb[d], b_identB], writes=[b_ps_y])
                yi = (gn // 4) % 2
                col = (gn % 4) * 128
                P.op("act", lambda e: e.activation(out=ystg[d][yi][:, :, col:col + 128],
                                                   in_=ps_y.rearrange("p (c t) -> p c t", c=2), func=AF.Copy),
                     reads=[b_ps_y], writes=[b_ystg[d][yi]])
                done = (gn % 4 == 3) if d == 0 else (gn % 4 == 0)
                if done:
                    c0 = (gn // 4) * 4 * 128
                    r0 = AW + h * 256
                    P.dma("pool", lambda e: e.dma_start(
                        out=mixT_d[r0:r0 + 256, c0:c0 + 512].rearrange("(c p) t -> p c t", p=128),
                        in_=ystg[d][yi]), "ystg%d%d" % (d, yi), reads=[b_ystg[d][yi]], writes=[B["mixT"]])

            def mainA(d, gn, r):
                P.op("pe", lambda e: e.matmul(ps_a2[d], KtT[d][r], QtT[d][r], start=True, stop=True),
                     reads=[b_Kt[d][r], b_Qt[d][r]], writes=[b_ps_a2[d]])
                P.op("dve", lambda e: e.tensor_tensor(out=ATs[d][r], in0=ps_a2[d], in1=MK[d], op=ALU.mult),
                     reads=[b_ps_a2[d], b_cA], writes=[b_AT[d][r]])

            def mainB(d, gn, r, first, step, pend):
                P.op("pe", lambda e: e.matmul(ps_og[d], ATs[d][r], vgs[:, gn, :], start=True, stop=first),
                     reads=[b_AT[d][r], b_vgs], writes=[b_ps_og[d]])
                if not first:
                    pr = (step - 1) % 2
                    P.op("pe", lambda e: e.matmul(ps_og[d], QtT[d][r], Sbf[d][pr], start=False, stop=True),
                         reads=[b_Qt[d][r], b_Sbf[d][pr]], writes=[b_ps_og[d]])
                P.op("pe", lambda e: e.matmul(ps_u, Kd[d][r], vgs[:, gn, :], start=True, stop=True),
                     reads=[b_Kd[d][r], b_vgs], writes=[b_ps_u])
                dec = E1[d][r][:, 127:128] if d == 0 else E1[d][r][:, 0:1]
                if first:
                    P.op("dve", lambda e: e.tensor_copy(out=Sst[d], in_=ps_u), reads=[b_ps_u], writes=[b_S[d]])
                else:
                    P.op("dve", lambda e: e.scalar_tensor_tensor(out=Sst[d], in0=Sst[d], scalar=dec, in1=ps_u,
                                                                 op0=ALU.mult, op1=ALU.add),
                         reads=[b_S[d], b_E1[d][r], b_ps_u], writes=[b_S[d]])
                boundary = (gn == NB - 1) if d == 0 else (gn == NB)
                if boundary:
                    P.op("dve", lambda e: e.tensor_scalar(out=Sst[d], in0=Sst[d], scalar1=flags[:, 0:1], scalar2=None,
                                                          op0=ALU.mult), reads=[b_S[d], b_flags], writes=[b_S[d]])
                sr = step % 2
                P.op("pool", lambda e: e.tensor_copy(out=Sbf[d][sr], in_=Sst[d]), reads=[b_S[d]],
                     writes=[b_Sbf[d][sr]])
                first_visit = (gn < NB) if d == 0 else (gn >= NB)
                if first_visit:
                    P.op("act", lambda e: e.activation(out=ost[:, gn, :], in_=ps_og[d], func=AF.Copy),
                         reads=[b_ps_og[d]], writes=[b_ost[gn]])
                else:
                    P.op("dve", lambda e: e.tensor_tensor(out=ost[:, gn, :], in0=ps_og[d], in1=ost[:, gn, :],
                                                          op=ALU.add), reads=[b_ps_og[d], b_ost[gn]],
                         writes=[b_ost[gn]])
                    pend.append((d, gn))

            blk = lambda d, i: i if d == 0 else NBT - 1 - i
            for d in range(2):
                prep(d, blk(d, 0), 0)
            pend = []
            for i in range(NBT):
                cur = pend
                pend = []
                for (pd, pg) in cur:
                    post_front(pd, pg)
                for d in range(2):
                    if i + 1 < NBT:
                        prep(d, blk(d, i + 1), (i + 1) % 2)
                for d in range(2):
                    mainA(d, blk(d, i), i % 2)
                for d in range(2):
                    mainB(d, blk(d, i), i % 2, i == 0, i, pend)
                for (pd, pg) in cur:
                    post_back(pd, pg)
            for (pd, pg) in pend:
                post_front(pd, pg)
            for (pd, pg) in pend:
                post_back(pd, pg)
        end_phase(MARK0)

    if cfg.stop_after == "B2":
        P.emit()
        es.close()
        return nc

    TC = cfg.TC
    NBC = TC // 128
    HFC = min(16, FC)
    FSPLIT = FC // HFC
    KG = min(8, KC)
    if True:
        x2 = [sb([128, D], F32) for b in range(NBC)]
        h2T = sb([128, KC, TC], BF16)
        RX = max(KC * TC, HFC * TC + 2 * KC * 128, D)
        regX = sb([128, RX], BF16)
        mixTs = regX[:, 0:KC * TC].rearrange("p (k t) -> p k t", k=KC)
        hid = regX[:, 0:HFC * TC].rearrange("p (f t) -> p f t", f=HFC)
        junkC = regX[:, 0:D]
        w1s = [regX[:, HFC * TC + i * KC * 128: HFC * TC + (i + 1) * KC * 128].rearrange("p (k j) -> p k j", k=KC)
               for i in range(2)]
        w1s.append(sb([128, KC, 128], BF16))
        NW2 = 4
        w2s = [sb([128, KG, 512], BF16) for i in range(NW2)]
        gs = [sb([128, 512], F32) for i in range(2)]
        tmpc = [sb([128, 512], F32) for i in range(2)]
        r32 = [sb([128, TC], F32) for i in range(2)]
        fgbc = sb([128, D], F32)
        py = [ps([128, 512]) for b in range(NBC)]
        ph = [ps([128, TC]) for i in range(2)]
        pstc = [ps([128, 512]) for i in range(8 - NBC - 2)]
        b_x2 = [P.buf("x2_%d" % b) for b in range(NBC)]
        b_h2T = P.buf("h2T")
        b_mixTs = P.buf("mixTs")
        b_hid = [P.buf("hid%d" % f) for f in range(HFC)]
        b_w1s = [P.buf("w1s%d" % i) for i in range(3)]
        b_w2s = [P.buf("w2s%d" % i) for i in range(NW2)]
        b_gs = [P.buf("gs%d" % i) for i in range(2)]
        b_tmpc = [P.buf("tmpc%d" % i) for i in range(2)]
        b_r32 = [P.buf("r32_%d" % i) for i in range(2)]
        b_fgbc = P.buf("fgbc")
        b_py = [P.buf("py%d" % b) for b in range(NBC)]
        b_ph = [P.buf("ph%d" % i) for i in range(2)]
        b_pstc = [P.buf("pstc%d" % i) for i in range(len(pstc))]
        b_ss = P.buf("ssC"); b_rs = P.buf("rsC"); b_tmp = P.buf("tmpC")
        ssv, rsv, tmpv = small[:, 1:2], small[:, 2:3], small[:, 3:4]
        P.dma("sp", lambda e: e.dma_start(out=fgbc, in_=fg_d.partition_broadcast(128)), "fgbc", writes=[b_fgbc])
        c_w1 = [0]; c_w2 = [0]; c_gs = [0]; c_tmp = [0]; c_ph = [0]; c_r = [0]
        mixv = mixT_d.rearrange("(k p) t -> p k t", p=128)

        def evac_add(b, j, gidx, half):
            gi = c_gs[0] % 2
            c_gs[0] += 1
            P.dma("pool", lambda e: e.dma_start(
                out=gs[gi], in_=gfull_d[gidx, half:half + 1, j * 512:(j + 1) * 512].partition_broadcast(128)),
                "gs%d" % gi, reads=[B["gfull"]], writes=[b_gs[gi]])
            return gi

        for t in range(TOK // TC):
            tok0 = t * TC
            half = tok0 // L
            for b in range(NBC):
                P.dma("sp", lambda e, b=b, r0=tok0 + b * 128: e.dma_start(out=x2[b], in_=x_d[r0:r0 + 128, :]),
                      "x2_%d" % b, writes=[b_x2[b]])
            P.dma("sp", lambda e, tok0=tok0: e.dma_start(out=mixTs, in_=mixv[:, :, tok0:tok0 + TC]), "mixTs",
                  reads=[B["mixT"]], writes=[b_mixTs] + b_hid + b_w1s[0:2])
            for j in range(D // 512):
                gi = evac_add(None, j, 0, half)
                for kg in range(KC // KG):
                    wi = c_w2[0] % NW2
                    c_w2[0] += 1
                    P.dma("sp", lambda e, wi=wi, kg=kg, j=j: e.dma_start(
                        out=w2s[wi], in_=woutb_d[kg * KG * 128:(kg + 1) * KG * 128, j * 512:(j + 1) * 512].rearrange(
                            "(f p) n -> p f n", p=128)), "w2s%d" % wi, reads=[B["woutb"]], writes=[b_w2s[wi]])
                    for b in range(NBC):
                        for f in range(KG):
                            k = kg * KG + f
                            P.op("pe", lambda e, b=b, f=f, k=k, wi=wi: e.matmul(
                                py[b], mixTs[:, k, b * 128:(b + 1) * 128], w2s[wi][:, f, :],
                                start=(k == 0), stop=(k == KC - 1)),
                                reads=[b_mixTs, b_w2s[wi]], writes=[b_py[b]])
                for b in range(NBC):
                    ti = c_tmp[0] % 2
                    c_tmp[0] += 1
                    P.op("dve", lambda e, b=b, ti=ti, gi=gi: e.tensor_tensor(out=tmpc[ti], in0=py[b], in1=gs[gi],
                                                                             op=ALU.mult),
                         reads=[b_py[b], b_gs[gi]], writes=[b_tmpc[ti]])
                    P.op("pool", lambda e, b=b, ti=ti, j=j: e.tensor_tensor(
                        out=x2[b][:, j * 512:(j + 1) * 512], in0=x2[b][:, j * 512:(j + 1) * 512], in1=tmpc[ti],
                        op=ALU.add), reads=[b_x2[b], b_tmpc[ti]], writes=[b_x2[b]])
            for b in range(NBC):
                norm_transpose_block(x2[b], b_x2[b], half, 1, h2T, b_h2T, b * 128, pstc, b_pstc, junkC,
                                     [b_mixTs], ssv, b_ss, rsv, b_rs, tmpv, b_tmp, undo=True)
            for fh in range(FSPLIT):
                for fi in range(HFC):
                    fc = fh * HFC + fi
                    wi = c_w1[0] % 3
                    c_w1[0] += 1
                    extra = [b_mixTs] if (fh == 0 and fi < 3 and wi < 2) else []
                    P.dma("sp", lambda e, wi=wi, fc=fc: e.dma_start(out=w1s[wi], in_=w1t_d[fc]), "w1s%d" % wi,
                          reads=[B["w1t"]], writes=[b_w1s[wi]] + extra)
                    pi = c_ph[0] % 2
                    c_ph[0] += 1
                    for k in range(KC):
                        P.op("pe", lambda e, wi=wi, k=k, pi=pi: e.matmul(ph[pi], w1s[wi][:, k, :], h2T[:, k, :],
                                                                         start=(k == 0), stop=(k == KC - 1)),
                             reads=[b_w1s[wi], b_h2T], writes=[b_ph[pi]])
                    ri = c_r[0] % 2
                    c_r[0] += 1
                    P.op("act", lambda e, pi=pi, ri=ri: e.activation(out=r32[ri], in_=ph[pi], func=AF.Relu),
                         reads=[b_ph[pi]], writes=[b_r32[ri]])
                    sq_eng = "dve" if fi % 2 == 0 else "pool"
                    P.op(sq_eng, lambda e, fi=fi, ri=ri: e.tensor_tensor(out=hid[:, fi, :], in0=r32[ri], in1=r32[ri],
                                                                         op=ALU.mult),
                         reads=[b_r32[ri]], writes=[b_hid[fi]])
                for j in range(D // 512):
                    gi = evac_add(None, j, 1, half)
                    for g in range(HFC // KG):
                        wi = c_w2[0] % NW2
                        c_w2[0] += 1
                        f0 = fh * HFC + g * KG
                        P.dma("sp", lambda e, wi=wi, f0=f0, j=j: e.dma_start(
                            out=w2s[wi], in_=w2b_d[f0 * 128:(f0 + KG) * 128, j * 512:(j + 1) * 512].rearrange(
                                "(f p) n -> p f n", p=128)), "w2s%d" % wi, reads=[B["w2b"]], writes=[b_w2s[wi]])
                        for b in range(NBC):
                            for f in range(KG):
                                fi = g * KG + f
                                P.op("pe", lambda e, b=b, f=f, fi=fi, wi=wi: e.matmul(
                                    py[b], hid[:, fi, b * 128:(b + 1) * 128], w2s[wi][:, f, :],
                                    start=(fi == 0), stop=(fi == HFC - 1)),
                                    reads=[b_hid[fi], b_w2s[wi]], writes=[b_py[b]])
                    for b in range(NBC):
                        ti = c_tmp[0] % 2
                        c_tmp[0] += 1
                        P.op("dve", lambda e, b=b, ti=ti, gi=gi: e.tensor_tensor(out=tmpc[ti], in0=py[b], in1=gs[gi],
                                                                                 op=ALU.mult),
                             reads=[b_py[b], b_gs[gi]], writes=[b_tmpc[ti]])
                        P.op("pool", lambda e, b=b, ti=ti, j=j: e.tensor_tensor(
                            out=x2[b][:, j * 512:(j + 1) * 512], in0=x2[b][:, j * 512:(j + 1) * 512], in1=tmpc[ti],
                            op=ALU.add), reads=[b_x2[b], b_tmpc[ti]], writes=[b_x2[b]])
            for b in range(NBC):
                P.op("act", lambda e, b=b: e.activation(out=junkC, in_=x2[b], func=AF.Square, accum_out=ssv),
                     reads=[b_x2[b]], writes=[b_mixTs, b_ss])
                rstd_ops(ssv, b_ss, rsv, b_rs, D, tmpv, b_tmp)
                P.op("dve", lambda e, b=b: e.scalar_tensor_tensor(out=x2[b], in0=x2[b], scalar=rsv, in1=fgbc,
                                                                  op0=ALU.mult, op1=ALU.mult),
                     reads=[b_x2[b], b_rs, b_fgbc], writes=[b_x2[b]])
                P.dma("pool", lambda e, b=b, r0=tok0 + b * 128: e.dma_start(out=y_d[r0:r0 + 128, :], in_=x2[b]),
                      "x2_%d" % b, reads=[b_x2[b]], writes=[B["y"]])
        end_phase(MARK0)

    P.emit()
    es.close()
    return nc


def shard_inputs(cfg, inp):
    D, L = cfg.D, cfg.L
    cA, cOH = host_consts()
    f = lambda a: np.ascontiguousarray(np.asarray(a, dtype=np.float32))
    xp, xs = f(inp["x_prompt"]), f(inp["x_sample"])
    cp, cs = f(inp["c_prompt"]), f(inp["c_sample"])
    common = {
        "w_ada": f(inp["w_ada"][0]), "b_ada": f(inp["b_ada"][0]).reshape(1, -1),
        "n1g": f(inp["norm1_g"][0]).reshape(1, -1), "w_in": f(inp["w_in"][0]),
        "wab_f": f(np.concatenate([inp["gla_wa_fwd"][0], inp["gla_ba_fwd"][0][None, :]], axis=0)),
        "wab_b": f(np.concatenate([inp["gla_wa_bwd"][0], inp["gla_ba_bwd"][0][None, :]], axis=0)),
        "gng": f(inp["gla_norm_g"][0]).reshape(1, -1), "sink": f(inp["attn_sink"][0]).reshape(1, -1),
        "relb": f(np.concatenate([inp["rel_bias"], np.ones((1, inp["rel_bias"].shape[1]), np.float32)], axis=0)),
        "w_out": f(inp["w_out"][0]), "n2g": f(inp["norm2_g"][0]).reshape(1, -1),
        "w1": f(inp["w_mlp_in"][0]), "w2": f(inp["w_mlp_out"][0]), "fg": f(inp["final_g"]).reshape(1, -1),
        "cA": cA, "cOH": cOH,
    }
    maps = []
    npc = xp.shape[0] // 2
    for c in range(8):
        m = dict(common)
        fl = np.zeros((128, 2), np.float32)
        if c < npc:
            m["x"] = np.ascontiguousarray(xp[2 * c:2 * c + 2].reshape(2 * L, D))
            m["c2"] = np.ascontiguousarray(cp[2 * c:2 * c + 2])
            fl[:, 0] = 0.0
            fl[:, 1] = NEG
        else:
            s = c - npc
            m["x"] = np.ascontiguousarray(xs[s].reshape(2 * L, D))
            m["c2"] = np.ascontiguousarray(np.stack([cs[s], cs[s]], axis=0))
            fl[:, 0] = 1.0
            fl[:, 1] = 0.0
        m["flags"] = fl
        maps.append(m)
    return maps


_CACHE = {}


def kernel(**inputs):
    cfg = Cfg()
    nc = build_program(cfg)
    maps = shard_inputs(cfg, inputs)
    res = run_bass_kernel_spmd(nc, maps, core_ids=list(range(8)))
    D, L = cfg.D, cfg.L
    npc = inputs["x_prompt"].shape[0] // 2
    yp = np.stack([res.results[c]["y"].reshape(2, L, D) for c in range(npc)], axis=0).reshape(-1, L, D)
    ys = np.stack([res.results[c]["y"].reshape(2 * L, D) for c in range(npc, 8)], axis=0)
    return (np.ascontiguousarray(yp, dtype=np.float32), np.ascontiguousarray(ys, dtype=np.float32))
```
